# Optimizing a Trainium2 kernel written in Bass

```python
import math
import jax, jax.numpy as jnp
from jax import lax
import numpy as np

D_MODEL = 1024
BATCH = 16
SEQ = 2048
DEPTH = 4
DEC_BATCH = 32
DEC_SEQ = 64
PAST_LEN = 1024

CHUNK = 64
N_A_LAYERS = DEPTH // 2
N_B_LAYERS = DEPTH - N_A_LAYERS
SSM_EXPAND = 2
D_INNER = SSM_EXPAND * D_MODEL
SSM_HEAD_DIM = 64
SSM_HEADS = D_INNER // SSM_HEAD_DIM
SSM_GROUPS = 4
HEADS_PER_GROUP = SSM_HEADS // SSM_GROUPS
SSM_STATE = 128
D_CONV = 4
CONV_DIM = D_INNER + 2 * SSM_GROUPS * SSM_STATE
IN_DIM = D_INNER + CONV_DIM + SSM_HEADS
SSD_CHUNK = CHUNK
RMS_EPS = 1e-5
ATT_HEAD_DIM = 64
ATT_HEADS = D_MODEL // ATT_HEAD_DIM
ATT_WIDTH = ATT_HEADS * ATT_HEAD_DIM
ATT_SCALE = ATT_HEAD_DIM ** -0.5
Q_BLOCK = 128
FORGET_BIAS_INIT = 3.0
PEER_HEADS = 8
PEER_NKEYS = 128
PEER_EXPERTS = PEER_NKEYS * PEER_NKEYS
PEER_QDIM = 256
PEER_HALF = PEER_QDIM // 2
PEER_TOPK = 16
PEER_TOK_BLOCK = 256
DN_ALPHA = (2.0 * DEPTH) ** 0.25
DN_BETA = (8.0 * DEPTH) ** -0.25
LN_EPS = 1e-5

kernel_name = "yoco_mamba2_fox_peer_stream_step"


def layer_norm(x, g, b):
    xf = x.astype(jnp.float32)
    mu = jnp.mean(xf, -1, keepdims=True)
    var = jnp.mean(jnp.square(xf - mu), -1, keepdims=True)
    return ((xf - mu) * lax.rsqrt(var + LN_EPS) * g.astype(jnp.float32) + b.astype(jnp.float32)).astype(x.dtype)


def gated_group_rms_norm(y, z, w):
    shp = y.shape
    h = (y * jax.nn.silu(z)).astype(jnp.float32).reshape(shp[:-1] + (SSM_GROUPS, D_INNER // SSM_GROUPS))
    h = h * lax.rsqrt(jnp.mean(h * h, -1, keepdims=True) + RMS_EPS)
    return (h.reshape(shp) * w.astype(jnp.float32)).astype(y.dtype)


def causal_dwconv(u, conv_state, w, b):
    L = u.shape[1]
    ext = jnp.concatenate([conv_state.astype(u.dtype), u], axis=1)
    out = b.astype(u.dtype) + ext[:, 0:L] * w[0]
    for k in range(1, D_CONV):
        out = out + ext[:, k:k + L] * w[k]
    return out, ext[:, L:]


def segsum(a):
    T = a.shape[-1]
    cs = jnp.cumsum(a, -1)
    diff = cs[..., :, None] - cs[..., None, :]
    mask = jnp.tril(jnp.ones((T, T), dtype=bool))
    return jnp.where(mask, diff, -jnp.inf)


def ssd_scan(x, a, Bm, Cm, init_state):
    b, L, G, R, P = x.shape
    N = Bm.shape[-1]
    Q = min(SSD_CHUNK, L)
    nc = L // Q
    xc = x.reshape(b, nc, Q, G, R, P)
    Bc = Bm.reshape(b, nc, Q, G, N)
    Cc = Cm.reshape(b, nc, Q, G, N)
    ac = a.reshape(b, nc, Q, G, R).transpose(0, 3, 4, 1, 2)
    a_cs = jnp.cumsum(ac, -1)
    Lmat = jnp.exp(segsum(ac))
    CB = jnp.einsum("bclgn,bcsgn->bcgls", Cc, Bc)
    y_diag = jnp.einsum("bcgls,bgrcls,bcsgrp->bclgrp", CB, Lmat, xc)
    decay_states = jnp.exp(a_cs[..., -1:] - a_cs)
    chunk_states = jnp.einsum("bclgn,bgrcl,bclgrp->bcgrpn", Bc, decay_states, xc)
    chunk_decay = jnp.exp(a_cs[..., -1])

    def step(h, inp):
        s_c, d_c = inp
        return h * d_c[..., None, None] + s_c, h

    final, states_in = lax.scan(step, init_state,
                                (jnp.moveaxis(chunk_states, 1, 0), jnp.moveaxis(chunk_decay, 3, 0)))
    states_in = jnp.moveaxis(states_in, 0, 1)
    y_off = jnp.einsum("bclgn,bcgrpn,bgrcl->bclgrp", Cc, states_in, jnp.exp(a_cs))
    return (y_diag + y_off).reshape(b, L, G, R, P), final


def mamba2_mixer(x, conv_state, ssm_state, w_in, conv_w, conv_b, dt_bias, A_log, D_skip, norm_w, w_out):
    f32 = jnp.float32
    b, L, _ = x.shape
    z, xBC, dt = jnp.split(x @ w_in, [D_INNER, D_INNER + CONV_DIM], axis=-1)
    xBC, new_conv = causal_dwconv(xBC, conv_state, conv_w, conv_b)
    xBC = jax.nn.silu(xBC)
    xs, Bm, Cm = jnp.split(xBC, [D_INNER, D_INNER + SSM_GROUPS * SSM_STATE], axis=-1)
    xs = xs.reshape(b, L, SSM_GROUPS, HEADS_PER_GROUP, SSM_HEAD_DIM).astype(f32)
    Bm = Bm.reshape(b, L, SSM_GROUPS, SSM_STATE).astype(f32)
    Cm = Cm.reshape(b, L, SSM_GROUPS, SSM_STATE).astype(f32)
    dt = jax.nn.softplus(dt.astype(f32) + dt_bias.astype(f32)).reshape(b, L, SSM_GROUPS, HEADS_PER_GROUP)
    A = -jnp.exp(A_log.astype(f32)).reshape(SSM_GROUPS, HEADS_PER_GROUP)
    h0 = ssm_state.astype(f32).reshape(b, SSM_GROUPS, HEADS_PER_GROUP, SSM_HEAD_DIM, SSM_STATE)
    y, h_final = ssd_scan(xs * dt[..., None], dt * A, Bm, Cm, h0)
    y = y + xs * D_skip.astype(f32).reshape(SSM_GROUPS, HEADS_PER_GROUP)[:, :, None]
    y = gated_group_rms_norm(y.reshape(b, L, D_INNER).astype(x.dtype), z, norm_w)
    return y @ w_out, new_conv, h_final.reshape(b, SSM_HEADS, SSM_HEAD_DIM, SSM_STATE)


def shared_kv(h, kv_w, kv_b_f):
    b, L, _ = h.shape
    k, v, f = jnp.split(h @ kv_w, [ATT_WIDTH, 2 * ATT_WIDTH], axis=-1)
    logf = jax.nn.log_sigmoid(f.astype(jnp.float32) + kv_b_f.astype(jnp.float32))
    return (k.reshape(b, L, ATT_HEADS, ATT_HEAD_DIM), v.reshape(b, L, ATT_HEADS, ATT_HEAD_DIM), logf)


def forgetting_attention(q, cq, qpos, k, v, ck, kpos):
    s = jnp.einsum("bqhd,bkhd->bhqk", q, k).astype(jnp.float32) * ATT_SCALE
    s = s + jnp.transpose(cq, (0, 2, 1))[..., :, None] - jnp.transpose(ck, (0, 2, 1))[..., None, :]
    mask = kpos[None, :] <= qpos[:, None]
    p = jax.nn.softmax(jnp.where(mask, s, -jnp.inf), axis=-1).astype(v.dtype)
    return jnp.einsum("bhqk,bkhd->bqhd", p, v)


def fox_mixer(x, k, v, ck, w_qg, w_o):
    b, L, _ = x.shape
    Lk = k.shape[1]
    q, g = jnp.split(x @ w_qg, 2, axis=-1)
    q = q.reshape(b, L, ATT_HEADS, ATT_HEAD_DIM)
    kpos = jnp.arange(Lk)
    qpos = (Lk - L) + jnp.arange(L)
    cq = ck[:, Lk - L:]
    if L > Q_BLOCK:
        nb = L // Q_BLOCK
        qb = q.reshape(b, nb, Q_BLOCK, ATT_HEADS, ATT_HEAD_DIM).swapaxes(0, 1)
        cqb = cq.reshape(b, nb, Q_BLOCK, ATT_HEADS).swapaxes(0, 1)
        pb = qpos.reshape(nb, Q_BLOCK)
        o = lax.map(lambda a: forgetting_attention(a[0], a[1], a[2], k, v, ck, kpos), (qb, cqb, pb))
        o = o.swapaxes(0, 1).reshape(b, L, ATT_HEADS, ATT_HEAD_DIM)
    else:
        o = forgetting_attention(q, cq, qpos, k, v, ck, kpos)
    o = o.reshape(b, L, ATT_WIDTH) * jax.nn.sigmoid(g)
    return o @ w_o


def peer_ffn(x, w_q, subkeys, u_tab, v_tab):
    shp = x.shape
    xt = x.reshape(-1, D_MODEL)
    T = xt.shape[0]
    q = (xt @ w_q).reshape(T, PEER_HEADS, 2, PEER_HALF)
    s = jnp.einsum("thkd,knd->thkn", q, subkeys).astype(jnp.float32)
    sv, si = lax.top_k(s, PEER_TOPK)
    cand = sv[..., 0, :, None] + sv[..., 1, None, :]
    best, bi = lax.top_k(cand.reshape(T, PEER_HEADS, PEER_TOPK * PEER_TOPK), PEER_TOPK)
    i1 = jnp.take_along_axis(si[..., 0, :], bi // PEER_TOPK, axis=-1)
    i2 = jnp.take_along_axis(si[..., 1, :], bi % PEER_TOPK, axis=-1)
    experts = (i1 * PEER_NKEYS + i2).reshape(T, PEER_HEADS * PEER_TOPK)
    gates = jax.nn.softmax(best, axis=-1).reshape(T, PEER_HEADS * PEER_TOPK)
    nblk = -(-T // PEER_TOK_BLOCK)
    pad = nblk * PEER_TOK_BLOCK - T
    hk = PEER_HEADS * PEER_TOPK
    xp = jnp.pad(xt, ((0, pad), (0, 0))).reshape(nblk, PEER_TOK_BLOCK, D_MODEL)
    ep = jnp.pad(experts, ((0, pad), (0, 0))).reshape(nblk, PEER_TOK_BLOCK, hk)
    gp = jnp.pad(gates, ((0, pad), (0, 0))).reshape(nblk, PEER_TOK_BLOCK, hk)

    def block(args):
        xb, eb, gb = args
        u = jnp.take(u_tab, eb, axis=0)
        act = jax.nn.gelu(jnp.einsum("tkd,td->tk", u, xb).astype(jnp.float32), approximate=False) * gb
        vv = jnp.take(v_tab, eb, axis=0)
        return jnp.einsum("tk,tkd->td", act.astype(xb.dtype), vv)

    out = lax.map(block, (xp, ep, gp)).reshape(nblk * PEER_TOK_BLOCK, D_MODEL)[:T]
    return out.reshape(shp)


def trunk(x, conv_state, ssm_state, past_k, past_v, past_logf,
          a_w_in, a_conv_w, a_conv_b, a_dt_bias, a_A_log, a_D, a_norm_w, a_w_out,
          kv_w, kv_b_f, b_w_qg, b_w_o, peer_w_q, peer_subkeys, peer_u, peer_v, ln_g, ln_b):
    new_conv, new_ssm = [], []
    for i in range(DEPTH):
        if i < N_A_LAYERS:
            y, c_i, s_i = mamba2_mixer(x, conv_state[i], ssm_state[i], a_w_in[i], a_conv_w[i], a_conv_b[i],
                                       a_dt_bias[i], a_A_log[i], a_D[i], a_norm_w[i], a_w_out[i])
            new_conv.append(c_i)
            new_ssm.append(s_i)
        else:
            j = i - N_A_LAYERS
            y = fox_mixer(x, k_all, v_all, ck, b_w_qg[j], b_w_o[j])
        x = layer_norm(DN_ALPHA * x + y, ln_g[i, 0], ln_b[i, 0])
        x = layer_norm(DN_ALPHA * x + peer_ffn(x, peer_w_q[i], peer_subkeys[i], peer_u[i], peer_v[i]),
                       ln_g[i, 1], ln_b[i, 1])
        if i == N_A_LAYERS - 1:
            k_new, v_new, logf_new = shared_kv(x, kv_w, kv_b_f)
            k_all = jnp.concatenate([past_k.astype(k_new.dtype), k_new], axis=1)
            v_all = jnp.concatenate([past_v.astype(v_new.dtype), v_new], axis=1)
            ck = jnp.cumsum(jnp.concatenate([past_logf.astype(jnp.float32), logf_new], axis=1), axis=1)
    return x, jnp.stack(new_conv), jnp.stack(new_ssm), k_new, v_new, logf_new


def setup_inputs(seed: int = 0) -> dict:
    key = jax.random.key(seed)
    ks = jax.random.split(key, 32)
    f32 = jnp.float32

    def nrm(k, shape, scale):
        return jax.random.normal(k, shape, f32) * scale

    dt0 = jnp.exp(jax.random.uniform(ks[10], (N_A_LAYERS, SSM_HEADS), f32, math.log(1e-3), math.log(1e-1)))
    kv_w = nrm(ks[17], (D_MODEL, 2 * ATT_WIDTH + ATT_HEADS), D_MODEL ** -0.5)
    kv_w = kv_w.at[:, ATT_WIDTH:2 * ATT_WIDTH].multiply(DN_BETA)
    return {
        "x_prompt": nrm(ks[0], (BATCH, SEQ, D_MODEL), 1.0),
        "x_sample": nrm(ks[1], (DEC_BATCH, DEC_SEQ, D_MODEL), 1.0),
        "state_ssm": nrm(ks[2], (N_A_LAYERS, DEC_BATCH, SSM_HEADS, SSM_HEAD_DIM, SSM_STATE), 0.1),
        "state_conv": nrm(ks[3], (N_A_LAYERS, DEC_BATCH, D_CONV - 1, CONV_DIM), 1.0),
        "cache_k": nrm(ks[4], (DEC_BATCH, PAST_LEN, ATT_HEADS, ATT_HEAD_DIM), 1.0),
        "cache_v": nrm(ks[5], (DEC_BATCH, PAST_LEN, ATT_HEADS, ATT_HEAD_DIM), DN_BETA),
        "cache_logf": jax.nn.log_sigmoid(FORGET_BIAS_INIT + nrm(ks[6], (DEC_BATCH, PAST_LEN, ATT_HEADS), 1.0)),
        "a_w_in": nrm(ks[7], (N_A_LAYERS, D_MODEL, IN_DIM), D_MODEL ** -0.5),
        "a_conv_w": nrm(ks[8], (N_A_LAYERS, D_CONV, CONV_DIM), D_CONV ** -0.5),
        "a_conv_b": nrm(ks[9], (N_A_LAYERS, CONV_DIM), 0.02),
        "a_dt_bias": dt0 + jnp.log(-jnp.expm1(-dt0)),
        "a_A_log": jnp.log(jax.random.uniform(ks[11], (N_A_LAYERS, SSM_HEADS), f32, 1.0, 16.0)),
        "a_D": 1.0 + nrm(ks[12], (N_A_LAYERS, SSM_HEADS), 0.1),
        "a_norm_w": 1.0 + nrm(ks[13], (N_A_LAYERS, D_INNER), 0.02),
        "a_w_out": nrm(ks[14], (N_A_LAYERS, D_INNER, D_MODEL), DN_BETA * D_INNER ** -0.5),
        "kv_w": kv_w,
        "kv_b_f": FORGET_BIAS_INIT + nrm(ks[15], (ATT_HEADS,), 0.1),
        "b_w_qg": nrm(ks[16], (N_B_LAYERS, D_MODEL, 2 * ATT_WIDTH), D_MODEL ** -0.5),
        "b_w_o": nrm(ks[18], (N_B_LAYERS, ATT_WIDTH, D_MODEL), DN_BETA * ATT_WIDTH ** -0.5),
        "peer_w_q": nrm(ks[19], (DEPTH, D_MODEL, PEER_HEADS * PEER_QDIM), D_MODEL ** -0.5),
        "peer_subkeys": nrm(ks[20], (DEPTH, 2, PEER_NKEYS, PEER_HALF), PEER_HALF ** -0.5),
        "peer_u": nrm(ks[21], (DEPTH, PEER_EXPERTS, D_MODEL), D_MODEL ** -0.5),
        "peer_v": nrm(ks[22], (DEPTH, PEER_EXPERTS, D_MODEL), DN_BETA * PEER_HEADS ** -0.5),
        "ln_g": 1.0 + nrm(ks[23], (DEPTH, 2, D_MODEL), 0.02),
        "ln_b": nrm(ks[24], (DEPTH, 2, D_MODEL), 0.02),
    }


def reference(x_prompt, x_sample, state_ssm, state_conv, cache_k, cache_v, cache_logf,
              a_w_in, a_conv_w, a_conv_b, a_dt_bias, a_A_log, a_D, a_norm_w, a_w_out,
              kv_w, kv_b_f, b_w_qg, b_w_o, peer_w_q, peer_subkeys, peer_u, peer_v, ln_g, ln_b):
    bp = x_prompt.shape[0]
    dt = x_prompt.dtype
    conv0 = jnp.zeros((N_A_LAYERS, bp, D_CONV - 1, CONV_DIM), dt)
    ssm0 = jnp.zeros((N_A_LAYERS, bp, SSM_HEADS, SSM_HEAD_DIM, SSM_STATE), jnp.float32)
    k0 = jnp.zeros((bp, 0, ATT_HEADS, ATT_HEAD_DIM), dt)
    f0 = jnp.zeros((bp, 0, ATT_HEADS), jnp.float32)
    y_prompt, p_conv, p_ssm, p_k, p_v, p_logf = trunk(
        x_prompt, conv0, ssm0, k0, k0, f0,
        a_w_in, a_conv_w, a_conv_b, a_dt_bias, a_A_log, a_D, a_norm_w, a_w_out,
        kv_w, kv_b_f, b_w_qg, b_w_o, peer_w_q, peer_subkeys, peer_u, peer_v, ln_g, ln_b)
    y_sample, s_conv, s_ssm, s_k, s_v, s_logf = trunk(
        x_sample, state_conv, state_ssm, cache_k, cache_v, cache_logf,
        a_w_in, a_conv_w, a_conv_b, a_dt_bias, a_A_log, a_D, a_norm_w, a_w_out,
        kv_w, kv_b_f, b_w_qg, b_w_o, peer_w_q, peer_subkeys, peer_u, peer_v, ln_g, ln_b)
    return (y_prompt, y_sample, p_ssm, p_conv, p_k, p_v, p_logf, s_ssm, s_conv, s_k, s_v, s_logf)
```

```python
import numpy as np
from contextlib import ExitStack
import concourse.bass as bass
import concourse.mybir as mybir
from concourse.bass_utils import run_bass_kernel_spmd

F32 = mybir.dt.float32
BF16 = mybir.dt.bfloat16
U32 = mybir.dt.uint32
AF = mybir.ActivationFunctionType
ALU = mybir.AluOpType
AX = mybir.AxisListType

SEM_CH = 30000
D = 1024
DEPTH = 4
ALPHA = float((2.0 * DEPTH) ** 0.25)
LN_EPS = 1e-5
RMS_EPS = 1e-5
ATT_SCALE = 64 ** -0.5
NCON = 11


class Tile:
    def __init__(self, K, handle, name, dma_sem=False, multi=False):
        self.K = K
        self.h = handle
        self.name = name
        self.w = None
        self.wd = {} if multi else None
        self.r = {}
        self.is_psum = False
        self.dsem = {}
        if dma_sem:
            K.dma_tiles.append(self)
            if K.phase_tiles is not None:
                K.phase_tiles.append(self)

    def __getitem__(self, key):
        return self.h[key]


class Eng:
    def __init__(self, K, name, eng):
        self.K = K
        self.name = name
        self.eng = eng
        self.n = 0
        self.sems = []
        self.seen = {}

    def token(self, n):
        b = (n - 1) // SEM_CH
        while len(self.sems) <= b:
            self.sems.append(self.K.new_sem("e_%s_%d" % (self.name, len(self.sems))))
        return (self.sems[b], (n - 1) % SEM_CH + 1, self)


class Kern:
    def __init__(self, nc, stack):
        self.nc = nc
        self.stack = stack
        self.dma_tiles = []
        self.sem_pool = {"hw": [], "sw": []}
        self.sem_latest = {}
        self.phase_tiles = None
        self.pe = Eng(self, "pe", nc.tensor)
        self.act = Eng(self, "act", nc.scalar)
        self.dve = Eng(self, "dve", nc.vector)
        self.pool = Eng(self, "pool", nc.gpsimd)
        self.sp = Eng(self, "sp", nc.sync)
        self.engs = [self.pe, self.act, self.dve, self.pool, self.sp]
        self.nsem = 0
        self.nwait = 0
        self.uid = 0

    def new_sem(self, name):
        self.nsem += 1
        return self.stack.enter_context(self.nc.semaphore(name))

    def sbuf(self, name, shape, dt, dma=False, stack=None):
        self.uid += 1
        nm = "%s_%d" % (name, self.uid)
        h = (stack or self.stack).enter_context(self.nc.sbuf_tensor(nm, list(shape), dt))
        return Tile(self, h, nm, dma_sem=dma)

    def psum(self, name, shape, dt):
        h = self.stack.enter_context(self.nc.psum_tensor(name, list(shape), dt))
        t = Tile(self, h, name)
        t.is_psum = True
        return t

    def dram(self, name, shape, dt, kind="Internal", multi=False):
        h = self.nc.dram_tensor(name, list(shape), dt, kind=kind).ap()
        return Tile(self, h, name, multi=multi)

    def _wait(self, E, tok):
        sem, val, src = tok
        if src is E and E.name == "pe":
            return
        key = id(sem)
        if src is None:
            val = max(val, self.sem_latest.get(key, 0))
        if E.seen.get(key, 0) >= val:
            return
        E.eng.wait_ge(sem, val)
        self.nwait += 1
        E.seen[key] = val

    def _sync(self, E, reads, writes):
        for t in reads:
            if t.w is not None:
                self._wait(E, t.w)
            if t.wd:
                for tok in t.wd.values():
                    self._wait(E, tok)
            if t.is_psum:
                for k_, tok in t.r.items():
                    if k_ != E.name:
                        self._wait(E, tok)
        for t in writes:
            if t.w is not None:
                self._wait(E, t.w)
            if t.wd:
                for tok in t.wd.values():
                    self._wait(E, tok)
            for tok in t.r.values():
                self._wait(E, tok)

    def op(self, E, fn, reads=(), writes=()):
        self._sync(E, reads, writes)
        inst = fn()
        E.n += 1
        tok = E.token(E.n)
        inst.then_inc(tok[0], 1)
        for t in reads:
            t.r[E.name] = tok
        for t in writes:
            t.w = tok
            t.r = {}
            if t.wd is not None:
                t.wd = {}
        return inst

    def phase(self):
        return _Phase(self)

    def dma(self, Q, out_ap, in_ap, reads, writes, semtile, tag=None):
        import os
        if tag is not None and tag in os.environ.get("KDIS", "").split(","):
            return None
        self._sync(Q, reads, writes)
        inst = Q.eng.dma_start(out=out_ap, in_=in_ap)
        kind = "sw" if Q is self.pool else "hw"
        ent = semtile.dsem.get(kind)
        if ent is None or ent[1] + 16 > SEM_CH:
            if ent is None and self.sem_pool[kind]:
                ent = list(self.sem_pool[kind].pop())
            else:
                ent = [self.new_sem("d%s_%s" % (kind, semtile.name)), 0]
            semtile.dsem[kind] = ent
        ent[1] += 16
        inst.then_inc(ent[0], 16)
        self.sem_latest[id(ent[0])] = ent[1]
        tok = (ent[0], ent[1], None)
        for t in reads:
            t.r["dma%d" % id(ent[0])] = tok
        for t in writes:
            if t.wd is not None:
                t.wd[id(ent[0])] = tok
            else:
                t.w = tok
            t.r = {}
        return inst

    def barrier(self):
        toks = []
        for E in self.engs:
            if E.n > 0:
                toks.append(E.token(E.n))
        for t in self.dma_tiles:
            for ent in t.dsem.values():
                toks.append((ent[0], ent[1], None))
        for E in self.engs:
            for tok in toks:
                if tok[2] is E:
                    continue
                self._wait(E, tok)

    def finish(self):
        for t in self.dma_tiles:
            for ent in t.dsem.values():
                self._wait(self.sp, (ent[0], ent[1], None))


class _Phase:
    def __init__(self, K):
        self.K = K

    def __enter__(self):
        self.stack = ExitStack()
        self.stack.__enter__()
        self.K.phase_tiles = []
        return self.stack

    def __exit__(self, *a):
        K = self.K
        K.barrier()
        for t in K.phase_tiles:
            K.dma_tiles.remove(t)
            for kind, ent in t.dsem.items():
                if ent[1] < 20000:
                    K.sem_pool[kind].append((ent[0], ent[1]))
        K.phase_tiles = None
        return self.stack.__exit__(*a)


def build(cfg):
    NP, SEQ, PAST = cfg
    NT_P = SEQ // 256
    KMAX = max(SEQ, PAST + 64)
    NKT_PAST = PAST // 128
    nc = bass.Bass("TRN2", target_bir_lowering=False)
    st = ExitStack()
    with st:
        K = Kern(nc, st)
        PE, ACT, DVE, POOL, SP = K.pe, K.act, K.dve, K.pool, K.sp

        def din(name, shape):
            return K.dram(name, shape, F32, "ExternalInput")

        def dout(name, shape):
            return K.dram(name, shape, F32, "ExternalOutput")

        x_p = din("x_p", [NP, SEQ, D]); x_s = din("x_s", [4, 64, D])
        st_ssm = din("st_ssm", [2, 4, 2048, 128]); st_conv = din("st_conv", [2, 4, 3, 3072])
        c_k = din("c_k", [4, PAST, D]); c_v = din("c_v", [4, PAST, D]); c_f = din("c_f", [4, PAST, 16])
        w_in = din("w_in", [2, D, 5152]); convp = din("convp", [2, 128, 120])
        dtb = din("dtb", [2, 32]); alog = din("alog", [2, 32]); dsk = din("dsk", [2, 32])
        normw = din("normw", [2, 128, 16]); w_out = din("w_out", [2, 2048, D])
        kv_wkv = din("kv_wkv", [D, 2048]); kv_wf = din("kv_wf", [D, 16]); kvb = din("kvb", [16])
        w_qg = din("w_qg", [2, D, 2048]); w_o = din("w_o", [2, D, D])
        pwq = din("pwq", [4, D, 2048]); skT = din("skT", [4, 2, 128, 128])
        import os
        NEXP = int(os.environ.get("KEXP", "16384"))
        uT = din("uT", [4, D, NEXP]); vt = din("vt", [4, NEXP, D])
        lng = din("lng", [4, 2, D]); lnb = din("lnb", [4, 2, D])
        cons_d = din("consts", [128, NCON * 128])

        y_p = dout("y_p", [NP, SEQ, D]); y_s = dout("y_s", [4, 64, D])
        o_pssm = dout("o_pssm", [2, NP, 2048, 128]); o_pconv = dout("o_pconv", [2, NP, 3, 3072])
        o_pk = dout("o_pk", [NP, SEQ, D]); o_pv = dout("o_pv", [NP, SEQ, D]); o_pf = dout("o_pf", [NP, SEQ, 16])
        o_sssm = dout("o_sssm", [2, 4, 2048, 128]); o_sconv = dout("o_sconv", [2, 4, 3, 3072])
        o_sk = dout("o_sk", [4, 64, D]); o_sv = dout("o_sv", [4, 64, D]); o_sf = dout("o_sf", [4, 64, 16])

        NSEQ = NP + 4
        KTscr = [K.dram("ktscr%d" % i, [8, 128, KMAX], BF16, multi=True) for i in range(NSEQ)]
        Vscr = [K.dram("vscr%d" % i, [KMAX + 128, 1056], BF16, multi=True) for i in range(NSEQ)]

        ps = [K.psum("ps%d" % i, [128, 512], F32) for i in range(8)]
        con = K.sbuf("con", [128, NCON, 128], F32, dma=True)
        conb = K.sbuf("conb", [128, NCON, 128], BF16)
        C_ID, C_TRI, C_TRI2, C_ONE, C_ONE2, C_IOTA, C_TSU, C_MS0, C_MS1, C_CM0, C_CM1 = range(NCON)
        x = K.sbuf("x", [128, 2, D], F32, dma=True)
        xb = K.sbuf("xb", [128, 2, D], BF16)
        xT = K.sbuf("xT", [128, 8, 256], BF16)
        junk = K.sbuf("junk", [128, D], BF16)
        gb = K.sbuf("gb", [128, 2, D], F32, dma=True)
        st4 = K.sbuf("st4", [128, 16], F32)
        NSLOT = 4
        wsl = [K.sbuf("wsl%d" % i, [128, 4096], BF16, dma=True) for i in range(NSLOT)]
        wsl_i = [0]
        cvp = K.sbuf("cvp", [128, 2, 120], F32, dma=True)
        nwp = K.sbuf("nwp", [128, 2, 16], F32, dma=True)
        dtb_s = K.sbuf("dtb_s", [128, 2, 32], F32, dma=True)
        aneg = K.sbuf("aneg", [128, 2, 32], F32, dma=True)
        dsk_s = K.sbuf("dsk_s", [128, 2, 32], F32, dma=True)
        kvb_s = K.sbuf("kvb_s", [128, 16], F32, dma=True)
        hTp = [K.sbuf("hTp%d" % l, [128, 2048], F32) for l in range(2)]
        cst = [K.sbuf("cst%d" % l, [128, 24, 4, 3], F32) for l in range(2)]
        SUF = K.sbuf("SUF", [128, 4, 16, 16], F32)
        AT0 = K.sbuf("AT0", [128, 1, 16, 16], F32)
        AT1 = K.sbuf("AT1", [128, 1, 16, 16], F32)
        NCS = K.sbuf("NCS", [128, 4, 16], F32)

        def wload(src_ap, view):
            s = wsl[wsl_i[0] % NSLOT]
            wsl_i[0] += 1
            v = view(s)
            K.dma(POOL, v, src_ap, [], [s], s)
            return s, v

        def v3(n1, n2):
            return lambda s: s[:, 0:n1 * n2].rearrange("p (a b) -> p a b", b=n2)

        K.dma(SP, con[:], cons_d[:, :].rearrange("p (a b) -> p a b", b=128), [], [con], con)
        K.op(DVE, lambda: nc.vector.tensor_copy(out=conb[:], in_=con[:]), [con], [conb])
        K.dma(SP, cvp[:], convp[:, :, :].rearrange("l p f -> p l f"), [], [cvp], cvp)
        K.dma(SP, nwp[:], normw[:, :, :].rearrange("l p f -> p l f"), [], [nwp], nwp)
        for l in range(2):
            K.dma(SP, dtb_s[:, l, :], dtb[l, :].partition_broadcast(128), [], [dtb_s], dtb_s)
            K.dma(SP, aneg[:, l, :], alog[l, :].partition_broadcast(128), [], [aneg], aneg)
            K.dma(SP, dsk_s[:, l, :], dsk[l, :].partition_broadcast(128), [], [dsk_s], dsk_s)
        K.dma(SP, kvb_s[:], kvb[:].partition_broadcast(128), [], [kvb_s], kvb_s)
        K.op(ACT, lambda: nc.scalar.activation(out=aneg[:], in_=aneg[:], func=AF.Exp), [aneg], [aneg])
        K.op(DVE, lambda: nc.vector.tensor_scalar(out=aneg[:], in0=aneg[:], scalar1=-1.0, scalar2=None, op0=ALU.mult), [aneg], [aneg])

        def cF(i):
            return con[:, i, :]

        def cB(i):
            return conb[:, i, :]

        def mm(out, lhsT, rhs, start, stop, reads, writes):
            K.op(PE, lambda: nc.tensor.matmul(out, lhsT, rhs, start=start, stop=stop), reads, writes)

        def tr(out, in_, ident, reads, writes):
            K.op(PE, lambda: nc.tensor.transpose(out, in_, ident), reads, writes)

        def act(out, in_, func, reads, writes, **kw):
            K.op(ACT, lambda: nc.scalar.activation(out=out, in_=in_, func=func, **kw), reads, writes)

        def tt(out, in0, in1, op, reads, writes, E=None):
            E = E or DVE
            K.op(E, lambda: E.eng.tensor_tensor(out=out, in0=in0, in1=in1, op=op), reads, writes)

        def ts(out, in0, s1, s2, op0, op1, reads, writes):
            if s2 is None:
                K.op(DVE, lambda: nc.vector.tensor_scalar(out=out, in0=in0, scalar1=s1, scalar2=None, op0=op0), reads, writes)
            else:
                K.op(DVE, lambda: nc.vector.tensor_scalar(out=out, in0=in0, scalar1=s1, scalar2=s2, op0=op0, op1=op1), reads, writes)

        def stt(out, in0, scalar, in1, op0, op1, reads, writes):
            K.op(DVE, lambda: nc.vector.scalar_tensor_tensor(out=out, in0=in0, scalar=scalar, in1=in1, op0=op0, op1=op1), reads, writes)

        def cpy(E, out, in_, reads, writes):
            if E is ACT:
                K.op(ACT, lambda: nc.scalar.copy(out=out, in_=in_), reads, writes)
            else:
                K.op(E, lambda: E.eng.tensor_copy(out=out, in_=in_), reads, writes)

        def make_xT():
            for s in range(2):
                cpy(ACT, xb[:, s, :], x[:, s, :], [x], [xb])
                pb = ps[7 - s][:].bitcast(BF16)
                for dk in range(8):
                    tr(pb[:, dk * 128:(dk + 1) * 128], xb[:, s, dk * 128:(dk + 1) * 128], cB(C_ID), [xb, conb], [ps[7 - s]])
                cpy(DVE, xT[:, :, s * 128:(s + 1) * 128], pb[:, 0:1024].rearrange("p (a b) -> p a b", b=128), [ps[7 - s]], [xT])

        def deepnorm(s, banks, li, which):
            for hh in range(2):
                stt(x[:, s, hh * 512:(hh + 1) * 512], x[:, s, hh * 512:(hh + 1) * 512], ALPHA, banks[hh][:, :],
                    ALU.mult, ALU.add, [x, banks[hh]], [x])
            act(junk[:], x[:, s, :], AF.Identity, [x], [junk, st4], accum_out=st4[:, 0:1])
            act(junk[:], x[:, s, :], AF.Square, [x], [junk, st4], accum_out=st4[:, 1:2])
            ts(st4[:, 2:3], st4[:, 0:1], 1.0 / D, None, ALU.mult, None, [st4], [st4])
            tt(st4[:, 3:4], st4[:, 2:3], st4[:, 2:3], ALU.mult, [st4], [st4])
            stt(st4[:, 4:5], st4[:, 1:2], 1.0 / D, st4[:, 3:4], ALU.mult, ALU.subtract, [st4], [st4])
            act(st4[:, 7:8], st4[:, 4:5], AF.Sqrt, [st4], [st4], bias=LN_EPS)
            K.op(DVE, lambda: nc.vector.reciprocal(out=st4[:, 5:6], in_=st4[:, 7:8]), [st4], [st4])
            stt(st4[:, 6:7], st4[:, 2:3], -1.0, st4[:, 5:6], ALU.mult, ALU.mult, [st4], [st4])
            act(x[:, s, :], x[:, s, :], AF.Identity, [x, st4], [x], scale=st4[:, 5:6], bias=st4[:, 6:7])
            tt(x[:, s, :], x[:, s, :], gb[:, 0, :], ALU.mult, [x, gb], [x])
            tt(x[:, s, :], x[:, s, :], gb[:, 1, :], ALU.add, [x, gb], [x])

        def load_gb(li, which):
            K.dma(SP, gb[:, 0, :], lng[li, which, :].partition_broadcast(128), [], [gb], gb)
            K.dma(SP, gb[:, 1, :], lnb[li, which, :].partition_broadcast(128), [], [gb], gb)

        def mamba_phase(li, tl):
            segs = tl["segs"]
            nseg = len(segs)
            L = 256 // nseg
            nsub = 2 if nseg == 4 else 1
            TM = C_TRI if nsub == 1 else C_TRI2
            with K.phase() as ph:
                zs = K.sbuf("zs", [128, 2, 2048], BF16, stack=ph)
                xsT = K.sbuf("xsT", [128, 16, 256], BF16, stack=ph)
                BT = K.sbuf("BT", [128, 4, 256], BF16, stack=ph)
                CTt = K.sbuf("CT", [128, 4, 256], BF16, stack=ph)
                CTm = K.sbuf("CTm", [128, 2, 4, 128], BF16, stack=ph)
                ext = [K.sbuf("ext%d" % i, [128, 4, 67], F32, stack=ph) for i in range(2)]
                cvt = [K.sbuf("cvt%d" % i, [128, 4, 64], F32, stack=ph) for i in range(2)]
                lastT = K.sbuf("lastT", [128, 8, 12], BF16, stack=ph)
                cvout = [K.sbuf("cvout%d" % i, [12, 512], F32, dma=True, stack=ph) for i in range(2)]
                hb16 = [K.sbuf("hb16_%d" % i, [128, 512], BF16, stack=ph) for i in range(2)]
                dt = K.sbuf("dt", [128, 2, 32], F32, stack=ph)
                aa = K.sbuf("aa", [128, 2, 32], F32, stack=ph)
                acs = K.sbuf("acs", [128, 32], F32, stack=ph)
                eacs = K.sbuf("eacs", [128, 32], F32, stack=ph)
                xs_tm = K.sbuf("xs_tm", [128, 2048], BF16, stack=ph)
                xdt = K.sbuf("xdt", [128, 2048], BF16, stack=ph)
                xdtw = K.sbuf("xdtw", [128, 512], BF16, stack=ph)
                Btm = K.sbuf("Btm", [128, 4, 128], BF16, stack=ph)
                abc = K.sbuf("abc", [128, 8, 128], F32, stack=ph)
                t1 = K.sbuf("t1", [128, 8, 128], F32, stack=ph)
                Dm = K.sbuf("Dm", [128, 8, 128], BF16, stack=ph)
                MT = K.sbuf("MT", [128, 8, 128], BF16, stack=ph)
                CBm = K.sbuf("CBm", [128, 128], BF16, stack=ph)
                yt = K.sbuf("yt", [128, 2048], F32, stack=ph)
                ytmp = K.sbuf("ytmp", [128, 512], F32, stack=ph)
                sm = K.sbuf("sm", [128, 64], F32, stack=ph)
                yn = K.sbuf("yn", [128, 2048], BF16, stack=ph)
                ynT = K.sbuf("ynT", [128, 2, 16, 128], BF16, stack=ph)
                stg = K.sbuf("stg", [128, 2048], F32, dma=True, stack=ph)
                if nseg == 1:
                    hT = [hTp[li]]
                else:
                    hT = [K.sbuf("hTs%d" % i, [128, 2048], F32, stack=ph) for i in range(2)]
                    cl = K.sbuf("cl", [12, 1536], F32, dma=True, stack=ph)
                    for hf in range(2):
                        K.dma(SP, cl[:], st_conv[li, :, :, hf * 1536:(hf + 1) * 1536].rearrange("b r c -> (b r) c"), [], [cl], cl)
                        for c in range(12):
                            tr(ps[hf][:, c * 12:c * 12 + 12], cl[:, c * 128:(c + 1) * 128], con[0:12, C_ID, 0:12], [cl, con], [ps[hf]])
                        cpy(DVE, cst[li][:, hf * 12:(hf + 1) * 12, :, :], ps[hf][:, 0:144].rearrange("p (c s r) -> p c s r", s=4, r=3), [ps[hf]], [cst[li]])

                def load_state(h_, src):
                    K.dma(SP, stg[:].rearrange("p (c n) -> p c n", n=128), src.rearrange("(c p) n -> p c n", p=128), [], [stg], stg)
                    for q4 in range(4):
                        for q in range(4):
                            cq = q4 * 4 + q
                            tr(ps[4 + q4][:, q * 128:(q + 1) * 128], stg[:, cq * 128:(cq + 1) * 128], cF(C_ID), [stg, con], [ps[4 + q4]])
                        cpy(DVE if q4 % 2 else ACT, h_[:, q4 * 512:(q4 + 1) * 512], ps[4 + q4][:, :], [ps[4 + q4]], [h_])

                def store_state(h_, dst):
                    for q4 in range(4):
                        for q in range(4):
                            cq = q4 * 4 + q
                            tr(ps[q4][:, q * 128:(q + 1) * 128], h_[:, cq * 128:(cq + 1) * 128], cF(C_ID), [h_, con], [ps[q4]])
                        cpy(ACT if q4 % 2 else DVE, stg[:, q4 * 512:(q4 + 1) * 512], ps[q4][:, :], [ps[q4]], [stg])
                    K.dma(SP, dst.rearrange("(c p) n -> p c n", p=128), stg[:].rearrange("p (c n) -> p c n", n=128), [stg], [], stg)

                is_last = tl["last"]
                if is_last:
                    cpy(DVE, lastT[:, :, 0:3 * nseg].rearrange("p a (s r) -> p a s r", r=3),
                        xT[:, :, :].rearrange("p a (s l) -> p a s l", l=L)[:, :, :, L - 3:L], [xT], [lastT])
                for cg in range(4):
                    wt, wv = wload(w_in[li, :, cg * 512:(cg + 1) * 512].rearrange("(a p) c -> p a c", p=128), v3(8, 512))
                    for s in range(2):
                        bank = ps[s]
                        for dk in range(8):
                            mm(bank[:, :], xT[:, dk, s * 128:(s + 1) * 128], wv[:, dk, :], dk == 0, dk == 7, [xT, wt], [bank])
                        act(zs[:, s, cg * 512:(cg + 1) * 512], bank[:, :], AF.Silu, [bank], [zs])
                for g in range(6):
                    wt, wv = wload(w_in[li, :, 2048 + g * 512:2048 + (g + 1) * 512].rearrange("(a p) c -> p a c", p=128), v3(8, 512))
                    for cc in range(4):
                        c = g * 4 + cc
                        bank = ps[2 + (c % 2)]
                        for dk in range(8):
                            mm(bank[:, 0:256], wv[:, dk, cc * 128:(cc + 1) * 128], xT[:, dk, :], dk == 0, dk == 7, [xT, wt], [bank])
                        e = ext[c % 2]
                        cv = cvt[c % 2]
                        ev = e[:, 0:nseg, 0:3 + L] if nseg == 4 else e[:, :, :].rearrange("p a b -> p (a b)")[:, 0:259].unsqueeze(1)
                        cvv = cv[:, 0:nseg, 0:L] if nseg == 4 else cv[:, :, :].rearrange("p a b -> p (a b)").unsqueeze(1)
                        cpy(ACT, ev[:, :, 3:3 + L], bank[:, 0:256].rearrange("p (s l) -> p s l", l=L), [bank], [e])
                        cpy(DVE, ev[:, :, 0:3], cst[li][:, c, 0:nseg, :], [cst[li]], [e])
                        wcol = lambda k: cvp[:, li, c * 5 + k:c * 5 + k + 1]
                        ts(cvv, ev[:, :, 0:L], wcol(0), wcol(4), ALU.mult, ALU.add, [e, cvp], [cv])
                        for k in range(1, 4):
                            stt(cvv, ev[:, :, k:k + L], wcol(k), cvv, ALU.mult, ALU.add, [e, cvp, cv], [cv])
                        cpy(DVE, cst[li][:, c, 0:nseg, :], ev[:, :, L:L + 3], [e], [cst[li]])
                        if c < 16:
                            dst, dt_ = xsT[:, c, :], xsT
                        elif c < 20:
                            dst, dt_ = BT[:, c - 16, :], BT
                        else:
                            dst, dt_ = CTt[:, c - 20, :], CTt
                        act(dst.rearrange("p (s l) -> p s l", l=L), cvv, AF.Silu, [cv], [dt_])
                    if is_last:
                        bank = ps[4]
                        for dk in range(8):
                            mm(bank[0:3 * nseg, :], lastT[:, dk, 0:3 * nseg], wv[:, dk, :], dk == 0, dk == 7, [lastT, wt], [bank])
                        co = cvout[g % 2]
                        cpy(ACT, co[0:3 * nseg, :], bank[0:3 * nseg, :], [bank], [co])
                        if tl["kind"] == "p":
                            K.dma(SP, o_pconv[li, tl["seq"], :, g * 512:(g + 1) * 512], co[0:3, :], [co], [], co)
                        else:
                            K.dma(SP, o_sconv[li, :, :, g * 512:(g + 1) * 512].rearrange("b r c -> (b r) c"), co[0:12, :], [co], [], co)
                wt, wv = wload(w_in[li, :, 5120:5152].rearrange("(a p) c -> p a c", p=128), v3(8, 32))
                for s in range(2):
                    bank = ps[4 + s]
                    for dk in range(8):
                        mm(bank[:, 0:32], xT[:, dk, s * 128:(s + 1) * 128], wv[:, dk, :], dk == 0, dk == 7, [xT, wt], [bank])
                    tt(dt[:, s, :], bank[:, 0:32], dtb_s[:, li, :], ALU.add, [bank, dtb_s], [dt])
                act(dt[:], dt[:], AF.Exp, [dt], [dt])
                act(dt[:], dt[:], AF.Ln, [dt], [dt], bias=1.0)
                tt(aa[:], dt[:], aneg[:, li, :].unsqueeze(1).to_broadcast([128, 2, 32]), ALU.mult, [dt, aneg], [aa])
                if nsub == 2:
                    cpy(DVE, CTm[:].rearrange("p a b c -> p (a b c)"), conb[:, C_CM0, 64:65].to_broadcast([128, 1024]), [conb], [CTm])

                for s in range(2):
                    cols = slice(s * 128, (s + 1) * 128)
                    sub_segs = [0] if nsub == 1 else [0, 1]
                    if nsub == 2:
                        for si in range(2):
                            load_state(hT[si], st_ssm[li, 2 * s + si, :, :])
                    prs = [slice(0, 128)] if nsub == 1 else [slice(0, 64), slice(64, 128)]
                    lastc = [127] if nsub == 1 else [63, 127]
                    for half in range(2):
                        pb = ps[half][:].bitcast(BF16)
                        for q in range(8):
                            tr(pb[:, q * 128:(q + 1) * 128], xsT[:, half * 8 + q, cols], cB(C_ID), [xsT, conb], [ps[half]])
                        cpy(ACT, xs_tm[:, half * 1024:(half + 1) * 1024], pb[:, 0:1024], [ps[half]], [xs_tm])
                    pb = ps[2][:].bitcast(BF16)
                    for g in range(4):
                        tr(pb[:, g * 128:(g + 1) * 128], BT[:, g, cols], cB(C_ID), [BT, conb], [ps[2]])
                    cpy(ACT, Btm[:].rearrange("p a b -> p (a b)"), pb[:, 0:512], [ps[2]], [Btm])
                    tt(xdt[:].rearrange("p (r q) -> p r q", q=64), xs_tm[:].rearrange("p (r q) -> p r q", q=64),
                       dt[:, s, :].unsqueeze(2).to_broadcast([128, 32, 64]), ALU.mult, [xs_tm, dt], [xdt])
                    mm(ps[3][:, 0:32], cF(TM), aa[:, s, :], True, True, [con, aa], [ps[3]])
                    cpy(DVE, acs[:], ps[3][:, 0:32], [ps[3]], [acs])
                    act(eacs[:], ps[3][:, 0:32], AF.Exp, [ps[3]], [eacs])
                    if nsub == 2:
                        for sg in range(2):
                            cpy(DVE, CTm[:, sg, :, sg * 64:(sg + 1) * 64], CTt[:, :, s * 128 + sg * 64:s * 128 + (sg + 1) * 64], [CTt], [CTm])
                    for g in range(4):
                        hs = slice(g * 8, (g + 1) * 8)
                        cpy(DVE, abc[:], aa[:, s, hs].unsqueeze(2).to_broadcast([128, 8, 128]), [aa], [abc])
                        for rr in range(8):
                            bank = ps[4 + rr // 4]
                            mm(bank[:, (rr % 4) * 128:(rr % 4 + 1) * 128], abc[:, rr, :], cF(TM), True, True, [abc, con], [bank])
                        for hb in range(2):
                            tt(t1[:, hb * 4:(hb + 1) * 4, :], ps[4 + hb][:, :].rearrange("p (r l) -> p r l", l=128),
                               acs[:, g * 8 + hb * 4:g * 8 + hb * 4 + 4].unsqueeze(2).to_broadcast([128, 4, 128]), ALU.subtract,
                               [ps[4 + hb], acs], [t1])
                        ts(t1[:], t1[:], 0.0, None, ALU.min, None, [t1], [t1])
                        act(Dm[:], t1[:], AF.Exp, [t1], [Dm])
                        mm(ps[6][:, 0:128], BT[:, g, cols], CTt[:, g, cols], True, True, [BT, CTt], [ps[6]])
                        tt(CBm[:], ps[6][:, 0:128], cF(TM), ALU.mult, [ps[6], con], [CBm])
                        tt(MT[:], Dm[:], CBm[:].unsqueeze(1).to_broadcast([128, 8, 128]), ALU.mult, [Dm, CBm], [MT])
                        for si, sgslot in enumerate(sub_segs):
                            pr = prs[si]
                            lc = lastc[si]
                            for hb in range(2):
                                rowv = ps[4 + hb][:, :].rearrange("p (r l) -> p r l", l=128)
                                tt(sm[pr, hb * 4:(hb + 1) * 4], rowv[pr, :, lc], acs[pr, g * 8 + hb * 4:g * 8 + hb * 4 + 4], ALU.subtract,
                                   [ps[4 + hb], acs], [sm])
                                act(sm[:, 16 + si * 8 + hb * 4:16 + si * 8 + (hb + 1) * 4], rowv[:, :, lc], AF.Exp, [ps[4 + hb]], [sm])
                        act(sm[:, 8:16], sm[:, 0:8], AF.Exp, [sm], [sm])
                        for rr in range(8):
                            r = g * 8 + rr
                            mm(ps[7][:, rr * 64:(rr + 1) * 64], MT[:, rr, :], xdt[:, r * 64:(r + 1) * 64], True, True, [MT, xdt], [ps[7]])
                        for si, sgslot in enumerate(sub_segs):
                            lhs = CTt[:, g, cols] if nsub == 1 else CTm[:, si, g, :]
                            hb_ = hb16[si]
                            cpy(ACT, hb_[:], hT[sgslot][:, g * 512:(g + 1) * 512], [hT[sgslot]], [hb_])
                            mm(ps[6][:, :], lhs, hb_[:], si == 0, si == len(sub_segs) - 1,
                               [CTt, CTm, hb_], [ps[6]])
                        tt(ytmp[:].rearrange("p (r q) -> p r q", q=64), ps[6][:, :].rearrange("p (r q) -> p r q", q=64),
                           eacs[:, hs].unsqueeze(2).to_broadcast([128, 8, 64]), ALU.mult, [ps[6], eacs], [ytmp])
                        tt(yt[:, g * 512:(g + 1) * 512], ytmp[:], ps[7][:, :], ALU.add, [ytmp, ps[7]], [yt])
                        tt(ytmp[:].rearrange("p (r q) -> p r q", q=64), xs_tm[:, g * 512:(g + 1) * 512].rearrange("p (r q) -> p r q", q=64),
                           dsk_s[:, li, hs].unsqueeze(2).to_broadcast([128, 8, 64]), ALU.mult, [xs_tm, dsk_s], [ytmp])
                        tt(yt[:, g * 512:(g + 1) * 512], yt[:, g * 512:(g + 1) * 512], ytmp[:], ALU.add, [yt, ytmp], [yt])
                        tt(xdtw[:].rearrange("p (r q) -> p r q", q=64), xdt[:, g * 512:(g + 1) * 512].rearrange("p (r q) -> p r q", q=64),
                           sm[:, 8:16].unsqueeze(2).to_broadcast([128, 8, 64]), ALU.mult, [xdt, sm], [xdtw])
                        for si, sgslot in enumerate(sub_segs):
                            pr = prs[si]
                            h_ = hT[sgslot]
                            mm(ps[6][:, :], Btm[pr, g, :], xdtw[pr, :], True, True, [Btm, xdtw], [ps[6]])
                            tt(ytmp[:].rearrange("p (r q) -> p r q", q=64), h_[:, g * 512:(g + 1) * 512].rearrange("p (r q) -> p r q", q=64),
                               sm[:, 16 + si * 8:24 + si * 8].unsqueeze(2).to_broadcast([128, 8, 64]), ALU.mult, [h_, sm], [ytmp])
                            tt(h_[:, g * 512:(g + 1) * 512], ytmp[:], ps[6][:, :], ALU.add, [ytmp, ps[6]], [h_])
                    tt(yt[:], yt[:], zs[:, s, :], ALU.mult, [yt, zs], [yt])
                    for g in range(4):
                        act(junk[:, 0:512], yt[:, g * 512:(g + 1) * 512], AF.Square, [yt], [junk, sm], accum_out=sm[:, 32 + g:33 + g])
                    act(sm[:, 40:44], sm[:, 32:36], AF.Sqrt, [sm], [sm], scale=1.0 / 512, bias=RMS_EPS)
                    K.op(DVE, lambda: nc.vector.reciprocal(out=sm[:, 36:40], in_=sm[:, 40:44]), [sm], [sm])
                    for g in range(4):
                        act(yn[:, g * 512:(g + 1) * 512], yt[:, g * 512:(g + 1) * 512], AF.Copy, [yt, sm], [yn], scale=sm[:, 36 + g:37 + g])
                    for half in range(2):
                        pb = ps[half][:].bitcast(BF16)
                        for q in range(8):
                            tr(pb[:, q * 128:(q + 1) * 128], yn[:, (half * 8 + q) * 128:(half * 8 + q + 1) * 128], cB(C_ID), [yn, conb], [ps[half]])
                        tt(ynT[:, s, half * 8:(half + 1) * 8, :], pb[:, 0:1024].rearrange("p (a b) -> p a b", b=128),
                           nwp[:, li, half * 8:(half + 1) * 8].unsqueeze(2).to_broadcast([128, 8, 128]), ALU.mult, [ps[half], nwp], [ynT])
                    if nsub == 2:
                        for si in range(2):
                            store_state(hT[si], o_sssm[li, 2 * s + si, :, :])
                if is_last and nsub == 1:
                    store_state(hT[0], o_pssm[li, tl["seq"], :, :])
                load_gb(li, 0)
                banks = [ps[0], ps[1], ps[2], ps[3]]
                for j in range(4):
                    wt, wv = wload(w_out[li, j * 512:(j + 1) * 512, :].rearrange("(a p) c -> p a c", p=128), v3(4, 1024))
                    for cc in range(4):
                        cq = j * 4 + cc
                        for s in range(2):
                            for hh in range(2):
                                mm(banks[s * 2 + hh][:, :], ynT[:, s, cq, :], wv[:, cc, hh * 512:(hh + 1) * 512], cq == 0, cq == 15,
                                   [ynT, wt], [banks[s * 2 + hh]])
                for s in range(2):
                    deepnorm(s, banks[s * 2:s * 2 + 2], li, 0)
            make_xT()

        import os
        pstop = int(os.environ.get("PSTOP", "99"))

        def peer_phase(li):
            with K.phase() as ph:
              for _once in (0,):
                    GT = K.sbuf("GT", [128, 128, 256], BF16, stack=ph)
                    qT = K.sbuf("qT", [128, 16, 256], BF16, stack=ph)
                    skb = K.sbuf("skb", [128, 2, 128], BF16, dma=True, stack=ph)
                    ssb = K.sbuf("ssb", [128, 16, 128], F32, stack=ph)
                    tmp = K.sbuf("tmp", [128, 256], F32, stack=ph)
                    sv = K.sbuf("sv", [128, 16, 16], F32, stack=ph)
                    siu = K.sbuf("siu", [128, 8, 16], U32, stack=ph)
                    sif = K.sbuf("sif", [128, 8, 16], F32, stack=ph)
                    cand = K.sbuf("cand", [128, 8, 256], F32, stack=ph)
                    best = K.sbuf("best", [128, 8, 16], F32, stack=ph)
                    eb = K.sbuf("eb", [128, 8, 16], F32, stack=ph)
                    zz = K.sbuf("zz", [128, 32], F32, stack=ph)
                    bia = K.sbuf("bia", [128, 8, 16], F32, stack=ph)
                    thr = K.sbuf("thr", [128, 8, 16], F32, stack=ph)
                    biaT = K.sbuf("biaT", [128, 256], F32, stack=ph)
                    thrT = K.sbuf("thrT", [128, 256], F32, stack=ph)
                    idxT = K.sbuf("idxT", [128, 256], F32, stack=ph)
                    Qrep = K.sbuf("Qrep", [128, 16, 128], BF16, stack=ph)
                    OH = K.sbuf("OH", [128, 16, 128], BF16, stack=ph)
                    et = [K.sbuf("et%d" % i, [128, 128], F32, stack=ph) for i in range(4)]
                    Rt = [K.sbuf("Rt%d" % i, [128, 128], BF16, stack=ph) for i in range(4)]
                    gh = [K.sbuf("gh%d" % i, [128, 256], BF16, stack=ph) for i in range(3)]
                    ATt = [K.sbuf("ATt%d" % i, [128, 256], BF16, stack=ph) for i in range(3)]

                    K.dma(POOL, skb[:], skT[li, :, :, :].rearrange("k d n -> d k n"), [], [skb], skb)
                    for cg in range(4):
                        wt, wv = wload(pwq[li, :, cg * 512:(cg + 1) * 512].rearrange("(a p) c -> p a c", p=128), v3(8, 512))
                        for cc in range(4):
                            oc = cg * 4 + cc
                            bank = ps[oc % 2]
                            for dk in range(8):
                                mm(bank[:, 0:256], wv[:, dk, cc * 128:(cc + 1) * 128], xT[:, dk, :], dk == 0, dk == 7, [xT, wt], [bank])
                            cpy(ACT if oc % 2 else DVE, qT[:, oc, :], bank[:, 0:256], [bank], [qT])
                    if pstop == 1:
                        break
                    for s in range(2):
                        cols = slice(s * 128, (s + 1) * 128)
                        for oc in range(16):
                            bank = ps[2 + oc // 4]
                            mm(bank[:, (oc % 4) * 128:(oc % 4 + 1) * 128], qT[:, oc, cols], skb[:, oc % 2, :], True, True, [qT, skb], [bank])
                        for q4 in range(4):
                            cpy(ACT, ssb[:, q4 * 4:(q4 + 1) * 4, :], ps[2 + q4][:, :].rearrange("p (a b) -> p a b", b=128), [ps[2 + q4]], [ssb])
                        for oc in range(16):
                            K.op(DVE, lambda: nc.vector.max(out=sv[:, oc, 0:8], in_=ssb[:, oc, :]), [ssb], [sv])
                            if oc % 2 == 0:
                                K.op(DVE, lambda: nc.vector.max_index(out=siu[:, oc // 2, 0:8], in_max=sv[:, oc, 0:8], in_values=ssb[:, oc, :]), [ssb, sv], [siu])
                            K.op(DVE, lambda: nc.vector.match_replace(out=tmp[:, 0:128], in_to_replace=sv[:, oc, 0:8], in_values=ssb[:, oc, :], imm_value=-1e30), [ssb, sv], [tmp])
                            K.op(DVE, lambda: nc.vector.max(out=sv[:, oc, 8:16], in_=tmp[:, 0:128]), [tmp], [sv])
                            if oc % 2 == 0:
                                K.op(DVE, lambda: nc.vector.max_index(out=siu[:, oc // 2, 8:16], in_max=sv[:, oc, 8:16], in_values=tmp[:, 0:128]), [tmp, sv], [siu])
                        cpy(DVE, sif[:], siu[:], [siu], [sif])
                        sv4 = sv[:].rearrange("p (h k) a -> p h k a", k=2)
                        tt(cand[:].rearrange("p h (a b) -> p h a b", b=16), sv4[:, :, 0, :].unsqueeze(3).to_broadcast([128, 8, 16, 16]),
                           sv4[:, :, 1, :].unsqueeze(2).to_broadcast([128, 8, 16, 16]), ALU.add, [sv], [cand])
                        for h in range(8):
                            K.op(DVE, lambda: nc.vector.max(out=best[:, h, 0:8], in_=cand[:, h, :]), [cand], [best])
                            K.op(DVE, lambda: nc.vector.match_replace(out=tmp[:], in_to_replace=best[:, h, 0:8], in_values=cand[:, h, :], imm_value=-1e30), [cand, best], [tmp])
                            K.op(DVE, lambda: nc.vector.max(out=best[:, h, 8:16], in_=tmp[:]), [tmp], [best])
                        tt(eb[:], best[:], best[:, :, 0:1].to_broadcast([128, 8, 16]), ALU.subtract, [best], [eb])
                        act(eb[:], eb[:], AF.Exp, [eb], [eb])
                        K.op(DVE, lambda: nc.vector.tensor_reduce(out=zz[:, 0:8], in_=eb[:], axis=AX.X, op=ALU.add), [eb], [zz])
                        act(zz[:, 8:16], zz[:, 0:8], AF.Ln, [zz], [zz])
                        tt(zz[:, 16:24], best[:, :, 0], zz[:, 8:16], ALU.add, [best, zz], [zz])
                        tt(bia[:], sv4[:, :, 0, :], zz[:, 16:24].unsqueeze(2).to_broadcast([128, 8, 16]), ALU.subtract, [sv, zz], [bia])
                        stt(thr[:], sv4[:, :, 0, :], -1.0, best[:, :, 15:16].to_broadcast([128, 8, 16]), ALU.mult, ALU.add, [sv, best], [thr])
                        ts(thr[:], thr[:], -2e-5, None, ALU.add, None, [thr], [thr])
                        for k_, (src, dstT) in enumerate(((bia, biaT), (thr, thrT), (sif, idxT))):
                            bank = ps[2 + k_]
                            tr(bank[:, 0:128], src[:].rearrange("p h a -> p (h a)"), cF(C_ID), [src, con], [bank])
                            cpy(ACT, dstT[:, cols], bank[:, 0:128], [bank], [dstT])
                    if pstop == 2:
                        break
                    for t0 in range(0, 256, 16):
                        for h in range(8):
                            cpy(DVE, Qrep[:, :, h * 16:(h + 1) * 16], qT[:, 2 * h + 1, t0:t0 + 16].unsqueeze(2).to_broadcast([128, 16, 16]), [qT], [Qrep])
                        tt(OH[:], con[:, C_IOTA, :].unsqueeze(1).to_broadcast([128, 16, 128]),
                           idxT[:, t0:t0 + 16].unsqueeze(2).to_broadcast([128, 16, 128]), ALU.is_equal, [con, idxT], [OH])
                        for tq in range(0, 16, 4):
                            gbank = ps[4 + (tq // 4) % 2]
                            for ti in range(4):
                                t = t0 + tq + ti
                                tl_ = tq + ti
                                abank = ps[t % 4]
                                mm(abank[:, 0:128], Qrep[:, tl_, :], skb[:, 1, :], True, True, [Qrep, skb], [abank])
                                e_ = et[t % 4]
                                r_ = Rt[t % 4]
                                act(e_[:], abank[:, 0:128], AF.Exp, [abank, biaT], [e_], bias=biaT[:, t:t + 1])
                                stt(r_[:], abank[:, 0:128], thrT[:, t:t + 1], e_[:], ALU.is_ge, ALU.mult, [abank, thrT, e_], [r_])
                                mm(gbank[:, ti * 128:(ti + 1) * 128], r_[:], OH[:, tl_, :], True, True, [r_, OH], [gbank])
                            cpy(ACT, GT[:, :, t0 + tq:t0 + tq + 4], gbank[:, :].rearrange("p (t i) -> p i t", i=128), [gbank], [GT])
                    if pstop == 3:
                        break
                    load_gb(li, 1)
                    accb = [ps[0], ps[1], ps[2], ps[3]]
                    NCG = int(os.environ.get('NCG', '32'))
                    for cg in range(NCG):
                        ut, uv = wload(uT[li, :, cg * 512:(cg + 1) * 512].rearrange("(a p) e -> p a e", p=128), v3(8, 512))
                        vt_, vv = wload(vt[li, cg * 512:(cg + 1) * 512, :].rearrange("(a p) d -> p a d", p=128), v3(4, 1024))
                        for cc in range(4):
                            c = cg * 4 + cc
                            hb = ps[4 + c % 3]
                            for dk in range(8):
                                mm(hb[:, 0:256], uv[:, dk, cc * 128:(cc + 1) * 128], xT[:, dk, :], dk == 0, dk == 7, [xT, ut], [hb])
                            g_ = gh[c % 3]
                            a_ = ATt[c % 3]
                            act(g_[:], hb[:, 0:256], AF.Gelu, [hb], [g_])
                            tt(a_[:], g_[:], GT[:, c, :], ALU.mult, [g_, GT], [a_])
                            for s in range(2):
                                for hh in range(2):
                                    mm(accb[s * 2 + hh][:, :], a_[:, s * 128:(s + 1) * 128], vv[:, cc, hh * 512:(hh + 1) * 512], c == 0, c == NCG * 4 - 1,
                                       [a_, vt_], [accb[s * 2 + hh]])
                    for s in range(2):
                        deepnorm(s, accb[s * 2:s * 2 + 2], li, 1)
            if pstop < 99:
                raise _Stop()

        kvstop = int(os.environ.get("KVSTOP", "99"))

        def kv_phase(tl):
            segs = tl["segs"]
            nseg = len(segs)
            kind = tl["kind"]
            with K.phase() as ph:
              for _once in (0,):
                    kst = [K.sbuf("kst%d" % i, [128, D], F32, dma=True, stack=ph) for i in range(2)]
                    vaug = [K.sbuf("vaug%d" % i, [128, 16, 66], BF16, dma=True, stack=ph) for i in range(2)]
                    ktile = [K.sbuf("ktile%d" % i, [128, 256], BF16, dma=True, stack=ph) for i in range(2)]
                    lf = K.sbuf("lf", [128, 2, 16], F32, dma=True, stack=ph)
                    tb = K.sbuf("tb", [128, 16], F32, stack=ph)
                    pk = K.sbuf("pk", [128, D], BF16, dma=True, stack=ph)
                    pf = K.sbuf("pf", [128, 16], F32, dma=True, stack=ph)
                    rsum = K.sbuf("rsum", [128, 16], F32, stack=ph)
                    KVX = os.environ.get("KVX", "")
                    for i in range(2):
                      if "a" not in KVX:
                        cpy(DVE, vaug[i][:, :, 64:66], conb[:, C_ONE, 0:32].rearrange("p (h q) -> p h q", q=2), [conb], [vaug[i]])

                    def tok_rows(dst, s):
                        if kind == "p":
                            b, T = tl["seq"], tl["T"]
                            return dst[b, T * 256 + s * 128:T * 256 + (s + 1) * 128, :]
                        return dst[2 * s:2 * s + 2, :, :].rearrange("b l f -> (b l) f")

                    ok, ov, of = (o_pk, o_pv, o_pf) if kind == "p" else (o_sk, o_sv, o_sf)
                    cnt = 0
                    for part in range(2):
                        for cgp in range(2):
                            cg = part * 2 + cgp
                            wt, wv = wload(kv_wkv[:, cg * 512:(cg + 1) * 512].rearrange("(a p) c -> p a c", p=128), v3(8, 512))
                            for s in range(2):
                                bank = ps[s]
                                for dk in range(8):
                                    mm(bank[:, :], xT[:, dk, s * 128:(s + 1) * 128], wv[:, dk, :], dk == 0, dk == 7, [xT, wt], [bank])
                                kk = kst[s]
                                if "b" not in KVX:
                                    cpy(ACT, kk[:, cgp * 512:(cgp + 1) * 512], bank[:, :], [bank], [kk])
                                if part == 1 and "c" not in KVX:
                                    cpy(DVE, vaug[s][:, cgp * 8:(cgp + 1) * 8, 0:64], bank[:, :].rearrange("p (h q) -> p h q", q=64), [bank], [vaug[s]])
                                if cgp == 1:
                                    K.dma(SP, tok_rows(ok if part == 0 else ov, s), kk[:], [kk], [], kk, tag="ok")
                            if part == 0 and "d" not in KVX:
                                for cc in range(4):
                                    c = cgp * 4 + cc
                                    bank = ps[2 + c % 2]
                                    for dk in range(8):
                                        mm(bank[:, 0:256], wv[:, dk, cc * 128:(cc + 1) * 128], xT[:, dk, :], dk == 0, dk == 7, [xT, wt], [bank])
                                    kt_ = ktile[c % 2]
                                    cpy(ACT if c % 2 else DVE, kt_[:], bank[:, 0:256], [bank], [kt_])
                                    if kind == "p":
                                        sl = tl["slot"]
                                        K.dma(SP, KTscr[sl][c, :, tl["T"] * 256:(tl["T"] + 1) * 256], kt_[:], [kt_], [KTscr[sl]], kt_, tag="kt")
                                    else:
                                        for i in range(4):
                                            K.dma(SP, KTscr[NP + i][c, :, PAST:PAST + 64], kt_[:, i * 64:(i + 1) * 64], [kt_], [KTscr[NP + i]], kt_)
                    if kvstop == 1:
                        break
                    for s in range(2):
                        if kind == "p":
                            sl = tl["slot"]
                            r0 = tl["T"] * 256 + s * 128
                            K.dma(SP, Vscr[sl][r0:r0 + 128, :], vaug[s][:].rearrange("p h q -> p (h q)"), [vaug[s]], [Vscr[sl]], vaug[s], tag="vs")
                        else:
                            for sg in range(2):
                                K.dma(SP, Vscr[NP + 2 * s + sg][PAST:PAST + 64, :], vaug[s][sg * 64:(sg + 1) * 64, :, :].rearrange("p h q -> p (h q)"),
                                      [vaug[s]], [Vscr[NP + 2 * s + sg]], vaug[s])
                    if kvstop == 2:
                        break
                    wf32 = K.sbuf("wf32", [128, 8, 16], F32, dma=True, stack=ph)
                    wt = K.sbuf("wfb", [128, 8, 16], BF16, stack=ph)
                    K.dma(SP, wf32[:], kv_wf[:, :].rearrange("(a p) c -> p a c", p=128), [], [wf32], wf32)
                    cpy(DVE, wt[:], wf32[:], [wf32], [wt])
                    wv = wt[:]
                    for s in range(2):
                        bank = ps[4 + s]
                        for dk in range(8):
                            mm(bank[:, 0:16], xT[:, dk, s * 128:(s + 1) * 128], wv[:, dk, :], dk == 0, dk == 7, [xT, wt], [bank])
                        tt(lf[:, s, :], bank[:, 0:16], kvb_s[:], ALU.add, [bank, kvb_s], [lf])
                    act(lf[:], lf[:], AF.Exp, [lf], [lf], scale=-1.0)
                    act(lf[:], lf[:], AF.Ln, [lf], [lf], bias=1.0)
                    ts(lf[:], lf[:], -1.0, None, ALU.mult, None, [lf], [lf])
                    for s in range(2):
                        K.dma(SP, tok_rows(of, s), lf[:, s, :], [lf], [], lf, tag="lf")
                    if kvstop == 3:
                        break
                    if kind == "p":
                        sl = 0
                        T = tl["T"]
                        if T == 0:
                            K.op(DVE, lambda: nc.vector.memset(SUF[:, 0, :, :], 0.0), [], [SUF])
                        for s in range(2):
                            j = T * 2 + s
                            mm(ps[6][:, 0:16], cF(C_TSU), lf[:, s, :], True, True, [con, lf], [ps[6]])
                            mm(ps[6][:, 16:32], cF(C_ONE), lf[:, s, :], True, True, [con, lf], [ps[6]])
                            mm(ps[6][:, 32:48], cF(C_TRI), lf[:, s, :], True, True, [con, lf], [ps[6]])
                            AT = AT0 if s == 0 else AT1
                            if j > 0:
                                cpy(DVE, AT[:, 0, 0:j, :], SUF[:, 0, 0:j, :], [SUF], [AT])
                            ts(AT[:, 0, j, :], ps[6][:, 32:48], -1.0, None, ALU.mult, None, [ps[6]], [AT])
                            if j > 0:
                                tt(SUF[:, 0, 0:j, :], SUF[:, 0, 0:j, :], ps[6][:, 16:32].unsqueeze(1).to_broadcast([128, j, 16]), ALU.add, [SUF, ps[6]], [SUF])
                            cpy(DVE, SUF[:, 0, j, :], ps[6][:, 0:16], [ps[6]], [SUF])
                    else:
                        for i in range(4):
                            s, sg = i // 2, i % 2
                            pr = slice(sg * 64, (sg + 1) * 64)
                            mm(ps[6][0:64, i * 16:(i + 1) * 16], con[pr, C_TRI, sg * 64:(sg + 1) * 64], lf[pr, s, :], True, True, [con, lf], [ps[6]])
                        ts(NCS[0:64, :, :].rearrange("p a b -> p (a b)"), ps[6][0:64, 0:64], -1.0, None, ALU.mult, None, [ps[6]], [NCS])
                        for i in range(4):
                            K.op(DVE, lambda: nc.vector.memset(rsum[:], 0.0), [], [rsum])
                            for j in range(NKT_PAST - 1, -1, -1):
                                K.dma(SP, pf[:], c_f[i, j * 128:(j + 1) * 128, :], [], [pf], pf)
                                mm(ps[6][:, 0:16], cF(C_TSU), pf[:], True, False, [con, pf], [ps[6]])
                                mm(ps[6][:, 0:16], cF(C_ONE), rsum[:], False, True, [con, rsum], [ps[6]])
                                cpy(DVE, SUF[:, i, j, :], ps[6][:, 0:16], [ps[6]], [SUF])
                                tt(rsum[:], rsum[:], pf[:], ALU.add, [rsum, pf], [rsum])
                            for j in range(NKT_PAST):
                                K.dma(POOL, pk[:], c_k[i, j * 128:(j + 1) * 128, :], [], [pk], pk)
                                for half in range(1):
                                    pb = ps[j % 2][:].bitcast(BF16)
                                    for q in range(8):
                                        tr(pb[:, q * 128:(q + 1) * 128], pk[:, q * 128:(q + 1) * 128], cB(C_ID), [pk, conb], [ps[j % 2]])
                                    kt_ = kst[j % 2]
                                    ktb = kt_[:].bitcast(BF16)
                                    cpy(ACT if j % 2 else DVE, ktb[:, 0:1024], pb[:, 0:1024], [ps[j % 2]], [kt_])
                                    K.dma(SP, KTscr[NP + i][:, :, j * 128:(j + 1) * 128].rearrange("c p k -> p c k"),
                                          ktb[:, 0:1024].rearrange("p (c k) -> p c k", k=128), [kt_], [KTscr[NP + i]], kt_)
                                va = vaug[j % 2]
                                K.dma(POOL, va[:, :, 0:64], c_v[i, j * 128:(j + 1) * 128, :].rearrange("p (h q) -> p h q", q=64), [], [va], va)
                                K.dma(SP, Vscr[NP + i][j * 128:(j + 1) * 128, :], va[:].rearrange("p h q -> p (h q)"), [va], [Vscr[NP + i]], va)

        def attn_phase(lj, li, tl):
            segs = tl["segs"]
            kind = tl["kind"]
            with K.phase() as ph:
                KT = K.sbuf("KT", [128, 8, KMAX], BF16, dma=True, stack=ph)
                VA = K.sbuf("VA", [128, KMAX // 128 + 1, 1056], BF16, dma=True, stack=ph)
                qTt = K.sbuf("qTt", [128, 8, 256], BF16, stack=ph)
                gs = K.sbuf("gs", [128, 2, D], BF16, stack=ph)
                PT = [K.sbuf("PT%d" % i, [128, 128], BF16, stack=ph) for i in range(4)]
                rc = K.sbuf("rc", [128, 4], F32, stack=ph)
                og = K.sbuf("og", [128, D], F32, stack=ph)
                ogb = K.sbuf("ogb", [128, D], BF16, stack=ph)
                ogT = K.sbuf("ogT", [128, 2, 8, 128], BF16, stack=ph)
                for cg in range(2):
                    wt, wv = wload(w_qg[lj, :, cg * 512:(cg + 1) * 512].rearrange("(a p) c -> p a c", p=128), v3(8, 512))
                    for cc in range(4):
                        c = cg * 4 + cc
                        bank = ps[c % 2]
                        for dk in range(8):
                            mm(bank[:, 0:256], wv[:, dk, cc * 128:(cc + 1) * 128], xT[:, dk, :], dk == 0, dk == 7, [xT, wt], [bank])
                        cpy(ACT if c % 2 else DVE, qTt[:, c, :], bank[:, 0:256], [bank], [qTt])
                for cg in range(2):
                    wt, wv = wload(w_qg[lj, :, 1024 + cg * 512:1024 + (cg + 1) * 512].rearrange("(a p) c -> p a c", p=128), v3(8, 512))
                    for s in range(2):
                        bank = ps[2 + s]
                        for dk in range(8):
                            mm(bank[:, :], xT[:, dk, s * 128:(s + 1) * 128], wv[:, dk, :], dk == 0, dk == 7, [xT, wt], [bank])
                        act(gs[:, s, cg * 512:(cg + 1) * 512], bank[:, :], AF.Sigmoid, [bank], [gs])
                cnt = [0]

                def attend(s, keysets):
                    nks = len(keysets)
                    for ki, ks in enumerate(keysets):
                        ks["load"]()
                        nk = ks["nfull"] + (1 if ks["part"] else 0)
                        for h in range(16):
                            accb = ps[4 + h // 4]
                            hh = h % 4
                            c, pbs = h // 2, (h % 2) * 64
                            for j in range(nk):
                                part = (j == ks["nfull"])
                                kp = 64 if part else 128
                                sb = ps[cnt[0] % 4]
                                p_ = PT[cnt[0] % 4]
                                cnt[0] += 1
                                mm(sb[0:kp, 0:128], KT[pbs:pbs + 64, c, j * 128:j * 128 + kp], qTt[pbs:pbs + 64, c, s * 128:(s + 1) * 128], True, True,
                                   [KT, qTt], [sb])
                                bias_ap, btile = ks["bias"](j, h)
                                act(p_[0:kp, :], sb[0:kp, 0:128], AF.Exp, [sb, btile], [p_], scale=ATT_SCALE, bias=bias_ap)
                                msk = ks["mask"](j)
                                if msk is not None:
                                    tt(p_[0:kp, :], p_[0:kp, :], msk, ALU.mult, [p_, conb], [p_])
                                mm(accb[:, hh * 65:(hh + 1) * 65], p_[0:kp, :], VA[0:kp, j, h * 66:h * 66 + 65],
                                   j == 0, j == nk - 1, [p_, VA], [accb])
                        pr = ks["rows"]
                        npr = pr.stop - pr.start
                        for hg in range(4):
                            accb = ps[4 + hg]
                            av = accb[pr, 0:260].rearrange("p (h q) -> p h q", q=65)
                            K.op(DVE, lambda: nc.vector.reciprocal(out=rc[pr, :], in_=av[:, :, 64]), [accb], [rc])
                            tt(og[pr, hg * 256:(hg + 1) * 256].rearrange("p (h q) -> p h q", q=64), av[:, :, 0:64],
                               rc[pr, :].unsqueeze(2).to_broadcast([npr, 4, 64]), ALU.mult, [accb, rc], [og])

                for s in range(2):
                    if kind == "p":
                        sl = tl["slot"]
                        T = tl["T"]
                        nkeys = (T + 1) * 256
                        nfull = T * 2 + s + 1

                        def load_p(sl=sl, nkeys=nkeys, s=s):
                            if s == 0:
                                K.dma(SP, KT[:, :, 0:nkeys], KTscr[sl][:, :, 0:nkeys].rearrange("c p k -> p c k"), [KTscr[sl]], [KT], KT)
                                K.dma(SP, VA[:, 0:nkeys // 128, :], Vscr[sl][0:nkeys, :].rearrange("(j p) f -> p j f", p=128), [Vscr[sl]], [VA], VA)
                        AT = AT0 if s == 0 else AT1
                        attend(s, [dict(load=load_p, nfull=nfull, part=False, rows=slice(0, 128),
                                        bias=lambda j, h, AT=AT: (AT[:, 0, j, h:h + 1], AT),
                                        mask=lambda j, nfull=nfull: (cB(C_TRI) if j == nfull - 1 else None))])
                    else:
                        kss = []
                        for sg in range(2):
                            i = 2 * s + sg

                            def load_s(i=i):
                                K.dma(SP, KT[:, :, 0:PAST + 64], KTscr[NP + i][:, :, 0:PAST + 64].rearrange("c p k -> p c k"), [KTscr[NP + i]], [KT], KT)
                                K.dma(SP, VA[:, 0:NKT_PAST, :], Vscr[NP + i][0:PAST, :].rearrange("(j p) f -> p j f", p=128), [Vscr[NP + i]], [VA], VA)
                                K.dma(SP, VA[0:64, NKT_PAST, :], Vscr[NP + i][PAST:PAST + 64, :], [Vscr[NP + i]], [VA], VA)

                            def bias_s(j, h, i=i):
                                if j == NKT_PAST:
                                    return NCS[0:64, i, h:h + 1], NCS
                                return SUF[:, i, j, h:h + 1], SUF

                            def mask_s(j, sg=sg):
                                if j == NKT_PAST:
                                    return conb[0:64, C_MS0 + sg, :]
                                return cB(C_CM0 + sg)
                            kss.append(dict(load=load_s, nfull=NKT_PAST, part=True, bias=bias_s, mask=mask_s, rows=slice(sg * 64, (sg + 1) * 64)))
                        attend(s, kss)
                    tt(ogb[:], og[:], gs[:, s, :], ALU.mult, [og, gs], [ogb])
                    pb = ps[s][:].bitcast(BF16)
                    for q in range(8):
                        tr(pb[:, q * 128:(q + 1) * 128], ogb[:, q * 128:(q + 1) * 128], cB(C_ID), [ogb, conb], [ps[s]])
                    cpy(ACT, ogT[:, s, :, :], pb[:, 0:1024].rearrange("p (a b) -> p a b", b=128), [ps[s]], [ogT])
                load_gb(li, 0)
                banks = [ps[0], ps[1], ps[2], ps[3]]
                for j in range(2):
                    wt, wv = wload(w_o[lj, j * 512:(j + 1) * 512, :].rearrange("(a p) c -> p a c", p=128), v3(4, 1024))
                    for cc in range(4):
                        cq = j * 4 + cc
                        for s in range(2):
                            for hh in range(2):
                                mm(banks[s * 2 + hh][:, :], ogT[:, s, cq, :], wv[:, cc, hh * 512:(hh + 1) * 512], cq == 0, cq == 7,
                                   [ogT, wt], [banks[s * 2 + hh]])
                for s in range(2):
                    deepnorm(s, banks[s * 2:s * 2 + 2], li, 0)
            make_xT()

        tiles = []
        for b in range(NP):
            for T in range(NT_P):
                tiles.append(dict(kind="p", seq=b, T=T, slot=b, segs=[0], last=(T == NT_P - 1), outseq=[("p", b)]))
        tiles.append(dict(kind="s", segs=[0, 1, 2, 3], last=True, outseq=[("s", i) for i in range(4)], T=0))

        stop_at = int(os.environ.get("KSTOP", "100000"))
        stage = [0]

        class _Stop(Exception):
            pass

        def chk():
            stage[0] += 1
            if stage[0] >= stop_at:
                raise _Stop()

        try:
          for tl in tiles:
              kind = tl["kind"]
              if kind == "p":
                  K.dma(SP, x[:], x_p[tl["seq"], tl["T"] * 256:(tl["T"] + 1) * 256, :].rearrange("(s p) d -> p s d", p=128), [], [x], x)
                  if tl["T"] == 0:
                      for l in range(2):
                          K.op(DVE, lambda: nc.vector.memset(hTp[l][:], 0.0), [], [hTp[l]])
                          K.op(DVE, lambda: nc.vector.memset(cst[l][:], 0.0), [], [cst[l]])
              else:
                  K.dma(SP, x[:], x_s[:, :, :].rearrange("b l d -> (b l) d").rearrange("(s p) d -> p s d", p=128), [], [x], x)
              make_xT()
              chk()
              konly = os.environ.get("KONLY", "")
              if konly == "kv":
                  kv_phase(tl)
                  raise _Stop()
              for li in range(DEPTH):
                  if li < 2:
                      mamba_phase(li, tl)
                  else:
                      attn_phase(li - 2, li, tl)
                  chk()
                  peer_phase(li)
                  chk()
                  if li < DEPTH - 1:
                      make_xT()
                  if li == 1:
                      kv_phase(tl)
                      chk()
              if kind == "p":
                  K.dma(SP, y_p[tl["seq"], tl["T"] * 256:(tl["T"] + 1) * 256, :].rearrange("(s p) d -> p s d", p=128), x[:], [x], [], x)
              else:
                  K.dma(SP, y_s[:, :, :].rearrange("b l d -> (b l) d").rearrange("(s p) d -> p s d", p=128), x[:], [x], [], x)
        except _Stop:
            K.dma(SP, y_p[0, 0:256, :].rearrange("(s p) d -> p s d", p=128), x[:], [x], [], x)
        K.finish()
        build.stats = dict(nsem=K.nsem, nwait=K.nwait, n={E.name: E.n for E in K.engs})
    return nc


def _consts():
    t = np.zeros((NCON, 128, 128), np.float32)
    idx = np.arange(128)
    tri = (idx[:, None] <= idx[None, :]).astype(np.float32)
    blk = ((idx[:, None] // 64) == (idx[None, :] // 64)).astype(np.float32)
    t[0] = np.eye(128, dtype=np.float32)
    t[1] = tri
    t[2] = tri * blk
    t[3] = 1.0
    t[4] = blk
    t[5] = np.broadcast_to(idx[None, :].astype(np.float32), (128, 128))
    t[6] = (idx[:, None] > idx[None, :]).astype(np.float32)
    t[7][0:64, 0:64] = tri[0:64, 0:64]
    t[8][0:64, 64:128] = tri[0:64, 0:64]
    t[9][:, 0:64] = 1.0
    t[10][:, 64:128] = 1.0
    return np.ascontiguousarray(t.transpose(1, 0, 2).reshape(128, NCON * 128))


_NC_CACHE = {}


def _NEXP():
    import os
    return int(os.environ.get("KEXP", "16384"))


def run(inputs, n_cores):
    f = lambda a: np.ascontiguousarray(np.asarray(a, dtype=np.float32))
    xp = f(inputs["x_prompt"]); xs = f(inputs["x_sample"])
    B, SEQ, _ = xp.shape
    DB = xs.shape[0]
    assert DB == 4 * n_cores and B % n_cores == 0 and xs.shape[1] == 64
    NP = B // n_cores
    PAST = inputs["cache_k"].shape[1]
    cfg = (NP, SEQ, PAST)
    if cfg not in _NC_CACHE:
        _NC_CACHE[cfg] = build(cfg)
    nc = _NC_CACHE[cfg]
    ssm = f(inputs["state_ssm"]).reshape(2, DB, 2048, 128)
    cvs = f(inputs["state_conv"])
    ck = f(inputs["cache_k"]).reshape(DB, PAST, D); cv = f(inputs["cache_v"]).reshape(DB, PAST, D); cf = f(inputs["cache_logf"])
    cw = f(inputs["a_conv_w"]); cb = f(inputs["a_conv_b"])
    convp = np.zeros((2, 128, 24, 5), np.float32)
    convp[:, :, :, 0:4] = cw.reshape(2, 4, 24, 128).transpose(0, 3, 2, 1)
    convp[:, :, :, 4] = cb.reshape(2, 24, 128).transpose(0, 2, 1)
    shared = {
        "w_in": f(inputs["a_w_in"]), "convp": np.ascontiguousarray(convp.reshape(2, 128, 120)),
        "dtb": f(inputs["a_dt_bias"]), "alog": f(inputs["a_A_log"]), "dsk": f(inputs["a_D"]),
        "normw": np.ascontiguousarray(f(inputs["a_norm_w"]).reshape(2, 16, 128).transpose(0, 2, 1)),
        "w_out": f(inputs["a_w_out"]), "kv_wkv": np.ascontiguousarray(f(inputs["kv_w"])[:, 0:2048]),
        "kv_wf": np.ascontiguousarray(f(inputs["kv_w"])[:, 2048:2064]), "kvb": f(inputs["kv_b_f"]),
        "w_qg": f(inputs["b_w_qg"]), "w_o": f(inputs["b_w_o"]), "pwq": f(inputs["peer_w_q"]),
        "skT": np.ascontiguousarray(f(inputs["peer_subkeys"]).transpose(0, 1, 3, 2)),
        "uT": np.ascontiguousarray(f(inputs["peer_u"])[:, 0:_NEXP()].transpose(0, 2, 1)), "vt": np.ascontiguousarray(f(inputs["peer_v"])[:, 0:_NEXP()]),
        "lng": f(inputs["ln_g"]), "lnb": f(inputs["ln_b"]), "consts": _consts(),
    }
    in_maps = []
    for i in range(n_cores):
        m = dict(shared)
        m["x_p"] = np.ascontiguousarray(xp[i * NP:(i + 1) * NP]); m["x_s"] = np.ascontiguousarray(xs[4 * i:4 * i + 4])
        m["st_ssm"] = np.ascontiguousarray(ssm[:, 4 * i:4 * i + 4]); m["st_conv"] = np.ascontiguousarray(cvs[:, 4 * i:4 * i + 4])
        m["c_k"] = np.ascontiguousarray(ck[4 * i:4 * i + 4]); m["c_v"] = np.ascontiguousarray(cv[4 * i:4 * i + 4])
        m["c_f"] = np.ascontiguousarray(cf[4 * i:4 * i + 4])
        in_maps.append(m)
    res = run_bass_kernel_spmd(nc, in_maps, core_ids=list(range(n_cores)))
    R = res.results
    cat = lambda k, ax: np.concatenate([np.asarray(r[k]) for r in R], axis=ax)
    y_p = cat("y_p", 0); y_s = cat("y_s", 0)
    p_ssm = cat("o_pssm", 1).reshape(2, B, 32, 64, 128); p_conv = cat("o_pconv", 1)
    p_k = cat("o_pk", 0).reshape(B, SEQ, 16, 64); p_v = cat("o_pv", 0).reshape(B, SEQ, 16, 64); p_f = cat("o_pf", 0)
    s_ssm = cat("o_sssm", 1).reshape(2, DB, 32, 64, 128); s_conv = cat("o_sconv", 1)
    s_k = cat("o_sk", 0).reshape(DB, 64, 16, 64); s_v = cat("o_sv", 0).reshape(DB, 64, 16, 64); s_f = cat("o_sf", 0)
    return (y_p, y_s, p_ssm, p_conv, p_k, p_v, p_f, s_ssm, s_conv, s_k, s_v, s_f)


def kernel(**inputs):
    return run(inputs, 8)
```

```python
import numpy as np
from contextlib import ExitStack
import concourse.bass as bass
import concourse.mybir as mybir
from concourse.bass_utils import run_bass_kernel_spmd

F32 = mybir.dt.float32
BF16 = mybir.dt.bfloat16
U32 = mybir.dt.uint32
AF = mybir.ActivationFunctionType
ALU = mybir.AluOpType
AX = mybir.AxisListType

SEM_CH = 30000
D = 1024
DEPTH = 4
ALPHA = float((2.0 * DEPTH) ** 0.25)
LN_EPS = 1e-5
RMS_EPS = 1e-5
ATT_SCALE = 64 ** -0.5
NCON = 11


class Tile:
    def __init__(self, K, handle, name, dma_sem=False, multi=False):
        self.K = K
        self.h = handle
        self.name = name
        self.w = None
        self.wd = {} if multi else None
        self.r = {}
        self.is_psum = False
        self.dsem = {}
        if dma_sem:
            K.dma_tiles.append(self)
            if K.phase_tiles is not None:
                K.phase_tiles.append(self)

    def __getitem__(self, key):
        return self.h[key]


class Eng:
    def __init__(self, K, name, eng):
        self.K = K
        self.name = name
        self.eng = eng
        self.n = 0
        self.sems = []
        self.seen = {}

    def token(self, n):
        b = (n - 1) // SEM_CH
        while len(self.sems) <= b:
            self.sems.append(self.K.new_sem("e_%s_%d" % (self.name, len(self.sems))))
        return (self.sems[b], (n - 1) % SEM_CH + 1, self)


class Kern:
    def __init__(self, nc, stack):
        self.nc = nc
        self.stack = stack
        self.dma_tiles = []
        self.sem_pool = {"hw": [], "sw": []}
        self.sem_latest = {}
        self.phase_tiles = None
        self.pe = Eng(self, "pe", nc.tensor)
        self.act = Eng(self, "act", nc.scalar)
        self.dve = Eng(self, "dve", nc.vector)
        self.pool = Eng(self, "pool", nc.gpsimd)
        self.sp = Eng(self, "sp", nc.sync)
        self.engs = [self.pe, self.act, self.dve, self.pool, self.sp]
        self.nsem = 0
        self.nwait = 0
        self.uid = 0

    def new_sem(self, name):
        self.nsem += 1
        return self.stack.enter_context(self.nc.semaphore(name))

    def sbuf(self, name, shape, dt, dma=False, stack=None):
        self.uid += 1
        nm = "%s_%d" % (name, self.uid)
        h = (stack or self.stack).enter_context(self.nc.sbuf_tensor(nm, list(shape), dt))
        return Tile(self, h, nm, dma_sem=dma)

    def psum(self, name, shape, dt):
        h = self.stack.enter_context(self.nc.psum_tensor(name, list(shape), dt))
        t = Tile(self, h, name)
        t.is_psum = True
        return t

    def dram(self, name, shape, dt, kind="Internal", multi=False):
        h = self.nc.dram_tensor(name, list(shape), dt, kind=kind).ap()
        return Tile(self, h, name, multi=multi)

    def _wait(self, E, tok):
        sem, val, src = tok
        if src is E and E.name == "pe":
            return
        key = id(sem)
        if src is None:
            val = max(val, self.sem_latest.get(key, 0))
        if E.seen.get(key, 0) >= val:
            return
        E.eng.wait_ge(sem, val)
        self.nwait += 1
        E.seen[key] = val

    def _sync(self, E, reads, writes):
        for t in reads:
            if t.w is not None:
                self._wait(E, t.w)
            if t.wd:
                for tok in t.wd.values():
                    self._wait(E, tok)
            if t.is_psum:
                for k_, tok in t.r.items():
                    if k_ != E.name:
                        self._wait(E, tok)
        for t in writes:
            if t.w is not None:
                self._wait(E, t.w)
            if t.wd:
                for tok in t.wd.values():
                    self._wait(E, tok)
            for tok in t.r.values():
                self._wait(E, tok)

    def op(self, E, fn, reads=(), writes=()):
        self._sync(E, reads, writes)
        inst = fn()
        E.n += 1
        tok = E.token(E.n)
        inst.then_inc(tok[0], 1)
        for t in reads:
            t.r[E.name] = tok
        for t in writes:
            t.w = tok
            t.r = {}
            if t.wd is not None:
                t.wd = {}
        return inst

    def phase(self):
        return _Phase(self)

    def dma(self, Q, out_ap, in_ap, reads, writes, semtile, tag=None):
        import os
        if tag is not None and tag in os.environ.get("KDIS", "").split(","):
            return None
        self._sync(Q, reads, writes)
        inst = Q.eng.dma_start(out=out_ap, in_=in_ap)
        kind = "sw" if Q is self.pool else "hw"
        ent = semtile.dsem.get(kind)
        if ent is None or ent[1] + 16 > SEM_CH:
            if ent is None and self.sem_pool[kind]:
                ent = list(self.sem_pool[kind].pop())
            else:
                ent = [self.new_sem("d%s_%s" % (kind, semtile.name)), 0]
            semtile.dsem[kind] = ent
        ent[1] += 16
        inst.then_inc(ent[0], 16)
        self.sem_latest[id(ent[0])] = ent[1]
        tok = (ent[0], ent[1], None)
        for t in reads:
            t.r["dma%d" % id(ent[0])] = tok
        for t in writes:
            if t.wd is not None:
                t.wd[id(ent[0])] = tok
            else:
                t.w = tok
            t.r = {}
        return inst

    def barrier(self):
        toks = []
        for E in self.engs:
            if E.n > 0:
                toks.append(E.token(E.n))
        for t in self.dma_tiles:
            for ent in t.dsem.values():
                toks.append((ent[0], ent[1], None))
        for E in self.engs:
            for tok in toks:
                if tok[2] is E:
                    continue
                self._wait(E, tok)

    def finish(self):
        for t in self.dma_tiles:
            for ent in t.dsem.values():
                self._wait(self.sp, (ent[0], ent[1], None))


class _Phase:
    def __init__(self, K):
        self.K = K

    def __enter__(self):
        self.stack = ExitStack()
        self.stack.__enter__()
        self.K.phase_tiles = []
        return self.stack

    def __exit__(self, *a):
        K = self.K
        K.barrier()
        for t in K.phase_tiles:
            K.dma_tiles.remove(t)
            for kind, ent in t.dsem.items():
                if ent[1] < 20000:
                    K.sem_pool[kind].append((ent[0], ent[1]))
        K.phase_tiles = None
        return self.stack.__exit__(*a)


def build(cfg):
    NP, SEQ, PAST = cfg
    NT_P = SEQ // 256
    KMAX = max(SEQ, PAST + 64)
    NKT_PAST = PAST // 128
    nc = bass.Bass("TRN2", target_bir_lowering=False)
    st = ExitStack()
    with st:
        K = Kern(nc, st)
        PE, ACT, DVE, POOL, SP = K.pe, K.act, K.dve, K.pool, K.sp

        def din(name, shape):
            return K.dram(name, shape, F32, "ExternalInput")

        def dout(name, shape):
            return K.dram(name, shape, F32, "ExternalOutput")

        x_p = din("x_p", [NP, SEQ, D]); x_s = din("x_s", [4, 64, D])
        st_ssm = din("st_ssm", [2, 4, 2048, 128]); st_conv = din("st_conv", [2, 4, 3, 3072])
        c_k = din("c_k", [4, PAST, D]); c_v = din("c_v", [4, PAST, D]); c_f = din("c_f", [4, PAST, 16])
        w_in = din("w_in", [2, D, 5152]); convp = din("convp", [2, 128, 120])
        dtb = din("dtb", [2, 32]); alog = din("alog", [2, 32]); dsk = din("dsk", [2, 32])
        normw = din("normw", [2, 128, 16]); w_out = din("w_out", [2, 2048, D])
        kv_wkv = din("kv_wkv", [D, 2048]); kv_wf = din("kv_wf", [D, 16]); kvb = din("kvb", [16])
        w_qg = din("w_qg", [2, D, 2048]); w_o = din("w_o", [2, D, D])
        pwq = din("pwq", [4, D, 2048]); skT = din("skT", [4, 2, 128, 128])
        import os
        NEXP = int(os.environ.get("KEXP", "16384"))
        uT = din("uT", [4, D, NEXP]); vt = din("vt", [4, NEXP, D])
        lng = din("lng", [4, 2, D]); lnb = din("lnb", [4, 2, D])
        cons_d = din("consts", [128, NCON * 128])

        y_p = dout("y_p", [NP, SEQ, D]); y_s = dout("y_s", [4, 64, D])
        o_pssm = dout("o_pssm", [2, NP, 2048, 128]); o_pconv = dout("o_pconv", [2, NP, 3, 3072])
        o_pk = dout("o_pk", [NP, SEQ, D]); o_pv = dout("o_pv", [NP, SEQ, D]); o_pf = dout("o_pf", [NP, SEQ, 16])
        o_sssm = dout("o_sssm", [2, 4, 2048, 128]); o_sconv = dout("o_sconv", [2, 4, 3, 3072])
        o_sk = dout("o_sk", [4, 64, D]); o_sv = dout("o_sv", [4, 64, D]); o_sf = dout("o_sf", [4, 64, 16])

        NSEQ = NP + 4
        KTscr = [K.dram("ktscr%d" % i, [8, 128, KMAX], BF16, multi=True) for i in range(NSEQ)]
        Vscr = [K.dram("vscr%d" % i, [KMAX + 128, 1056], BF16, multi=True) for i in range(NSEQ)]

        ps = [K.psum("ps%d" % i, [128, 512], F32) for i in range(8)]
        con = K.sbuf("con", [128, NCON, 128], F32, dma=True)
        conb = K.sbuf("conb", [128, NCON, 128], BF16)
        C_ID, C_TRI, C_TRI2, C_ONE, C_ONE2, C_IOTA, C_TSU, C_MS0, C_MS1, C_CM0, C_CM1 = range(NCON)
        x = K.sbuf("x", [128, 2, D], F32, dma=True)
        xb = K.sbuf("xb", [128, 2, D], BF16)
        xT = K.sbuf("xT", [128, 8, 256], BF16)
        junk = K.sbuf("junk", [128, D], BF16)
        gb = K.sbuf("gb", [128, 2, D], F32, dma=True)
        st4 = K.sbuf("st4", [128, 16], F32)
        NSLOT = 4
        wsl = [K.sbuf("wsl%d" % i, [128, 4096], BF16, dma=True) for i in range(NSLOT)]
        wsl_i = [0]
        cvp = K.sbuf("cvp", [128, 2, 120], F32, dma=True)
        nwp = K.sbuf("nwp", [128, 2, 16], F32, dma=True)
        dtb_s = K.sbuf("dtb_s", [128, 2, 32], F32, dma=True)
        aneg = K.sbuf("aneg", [128, 2, 32], F32, dma=True)
        dsk_s = K.sbuf("dsk_s", [128, 2, 32], F32, dma=True)
        kvb_s = K.sbuf("kvb_s", [128, 16], F32, dma=True)
        hTp = [K.sbuf("hTp%d" % l, [128, 2048], F32) for l in range(2)]
        cst = [K.sbuf("cst%d" % l, [128, 24, 4, 3], F32) for l in range(2)]
        SUF = K.sbuf("SUF", [128, 4, 16, 16], F32)
        AT0 = K.sbuf("AT0", [128, 1, 16, 16], F32)
        AT1 = K.sbuf("AT1", [128, 1, 16, 16], F32)
        NCS = K.sbuf("NCS", [128, 4, 16], F32)

        def wload(src_ap, view):
            s = wsl[wsl_i[0] % NSLOT]
            wsl_i[0] += 1
            v = view(s)
            K.dma(POOL, v, src_ap, [], [s], s)
            return s, v

        def v3(n1, n2):
            return lambda s: s[:, 0:n1 * n2].rearrange("p (a b) -> p a b", b=n2)

        K.dma(SP, con[:], cons_d[:, :].rearrange("p (a b) -> p a b", b=128), [], [con], con)
        K.op(DVE, lambda: nc.vector.tensor_copy(out=conb[:], in_=con[:]), [con], [conb])
        K.dma(SP, cvp[:], convp[:, :, :].rearrange("l p f -> p l f"), [], [cvp], cvp)
        K.dma(SP, nwp[:], normw[:, :, :].rearrange("l p f -> p l f"), [], [nwp], nwp)
        for l in range(2):
            K.dma(SP, dtb_s[:, l, :], dtb[l, :].partition_broadcast(128), [], [dtb_s], dtb_s)
            K.dma(SP, aneg[:, l, :], alog[l, :].partition_broadcast(128), [], [aneg], aneg)
            K.dma(SP, dsk_s[:, l, :], dsk[l, :].partition_broadcast(128), [], [dsk_s], dsk_s)
        K.dma(SP, kvb_s[:], kvb[:].partition_broadcast(128), [], [kvb_s], kvb_s)
        K.op(ACT, lambda: nc.scalar.activation(out=aneg[:], in_=aneg[:], func=AF.Exp), [aneg], [aneg])
        K.op(DVE, lambda: nc.vector.tensor_scalar(out=aneg[:], in0=aneg[:], scalar1=-1.0, scalar2=None, op0=ALU.mult), [aneg], [aneg])

        def cF(i):
            return con[:, i, :]

        def cB(i):
            return conb[:, i, :]

        def mm(out, lhsT, rhs, start, stop, reads, writes):
            K.op(PE, lambda: nc.tensor.matmul(out, lhsT, rhs, start=start, stop=stop), reads, writes)

        def tr(out, in_, ident, reads, writes):
            K.op(PE, lambda: nc.tensor.transpose(out, in_, ident), reads, writes)

        def act(out, in_, func, reads, writes, **kw):
            K.op(ACT, lambda: nc.scalar.activation(out=out, in_=in_, func=func, **kw), reads, writes)

        def tt(out, in0, in1, op, reads, writes, E=None):
            E = E or DVE
            K.op(E, lambda: E.eng.tensor_tensor(out=out, in0=in0, in1=in1, op=op), reads, writes)

        def ts(out, in0, s1, s2, op0, op1, reads, writes):
            if s2 is None:
                K.op(DVE, lambda: nc.vector.tensor_scalar(out=out, in0=in0, scalar1=s1, scalar2=None, op0=op0), reads, writes)
            else:
                K.op(DVE, lambda: nc.vector.tensor_scalar(out=out, in0=in0, scalar1=s1, scalar2=s2, op0=op0, op1=op1), reads, writes)

        def stt(out, in0, scalar, in1, op0, op1, reads, writes):
            K.op(DVE, lambda: nc.vector.scalar_tensor_tensor(out=out, in0=in0, scalar=scalar, in1=in1, op0=op0, op1=op1), reads, writes)

        def cpy(E, out, in_, reads, writes):
            if E is ACT:
                K.op(ACT, lambda: nc.scalar.copy(out=out, in_=in_), reads, writes)
            else:
                K.op(E, lambda: E.eng.tensor_copy(out=out, in_=in_), reads, writes)

        def make_xT():
            for s in range(2):
                cpy(ACT, xb[:, s, :], x[:, s, :], [x], [xb])
                pb = ps[7 - s][:].bitcast(BF16)
                for dk in range(8):
                    tr(pb[:, dk * 128:(dk + 1) * 128], xb[:, s, dk * 128:(dk + 1) * 128], cB(C_ID), [xb, conb], [ps[7 - s]])
                cpy(DVE, xT[:, :, s * 128:(s + 1) * 128], pb[:, 0:1024].rearrange("p (a b) -> p a b", b=128), [ps[7 - s]], [xT])

        def deepnorm(s, banks, li, which):
            for hh in range(2):
                stt(x[:, s, hh * 512:(hh + 1) * 512], x[:, s, hh * 512:(hh + 1) * 512], ALPHA, banks[hh][:, :],
                    ALU.mult, ALU.add, [x, banks[hh]], [x])
            act(junk[:], x[:, s, :], AF.Identity, [x], [junk, st4], accum_out=st4[:, 0:1])
            act(junk[:], x[:, s, :], AF.Square, [x], [junk, st4], accum_out=st4[:, 1:2])
            ts(st4[:, 2:3], st4[:, 0:1], 1.0 / D, None, ALU.mult, None, [st4], [st4])
            tt(st4[:, 3:4], st4[:, 2:3], st4[:, 2:3], ALU.mult, [st4], [st4])
            stt(st4[:, 4:5], st4[:, 1:2], 1.0 / D, st4[:, 3:4], ALU.mult, ALU.subtract, [st4], [st4])
            act(st4[:, 7:8], st4[:, 4:5], AF.Sqrt, [st4], [st4], bias=LN_EPS)
            K.op(DVE, lambda: nc.vector.reciprocal(out=st4[:, 5:6], in_=st4[:, 7:8]), [st4], [st4])
            stt(st4[:, 6:7], st4[:, 2:3], -1.0, st4[:, 5:6], ALU.mult, ALU.mult, [st4], [st4])
            act(x[:, s, :], x[:, s, :], AF.Identity, [x, st4], [x], scale=st4[:, 5:6], bias=st4[:, 6:7])
            tt(x[:, s, :], x[:, s, :], gb[:, 0, :], ALU.mult, [x, gb], [x])
            tt(x[:, s, :], x[:, s, :], gb[:, 1, :], ALU.add, [x, gb], [x])

        def load_gb(li, which):
            K.dma(SP, gb[:, 0, :], lng[li, which, :].partition_broadcast(128), [], [gb], gb)
            K.dma(SP, gb[:, 1, :], lnb[li, which, :].partition_broadcast(128), [], [gb], gb)

        def mamba_phase(li, tl):
            segs = tl["segs"]
            nseg = len(segs)
            L = 256 // nseg
            nsub = 2 if nseg == 4 else 1
            TM = C_TRI if nsub == 1 else C_TRI2
            with K.phase() as ph:
                zs = K.sbuf("zs", [128, 2, 2048], BF16, stack=ph)
                xsT = K.sbuf("xsT", [128, 16, 256], BF16, stack=ph)
                BT = K.sbuf("BT", [128, 4, 256], BF16, stack=ph)
                CTt = K.sbuf("CT", [128, 4, 256], BF16, stack=ph)
                CTm = K.sbuf("CTm", [128, 2, 4, 128], BF16, stack=ph)
                ext = [K.sbuf("ext%d" % i, [128, 4, 67], F32, stack=ph) for i in range(2)]
                cvt = [K.sbuf("cvt%d" % i, [128, 4, 64], F32, stack=ph) for i in range(2)]
                lastT = K.sbuf("lastT", [128, 8, 12], BF16, stack=ph)
                cvout = [K.sbuf("cvout%d" % i, [12, 512], F32, dma=True, stack=ph) for i in range(2)]
                hb16 = [K.sbuf("hb16_%d" % i, [128, 512], BF16, stack=ph) for i in range(2)]
                dt = K.sbuf("dt", [128, 2, 32], F32, stack=ph)
                aa = K.sbuf("aa", [128, 2, 32], F32, stack=ph)
                acs = K.sbuf("acs", [128, 32], F32, stack=ph)
                eacs = K.sbuf("eacs", [128, 32], F32, stack=ph)
                xs_tm = K.sbuf("xs_tm", [128, 2048], BF16, stack=ph)
                xdt = K.sbuf("xdt", [128, 2048], BF16, stack=ph)
                xdtw = K.sbuf("xdtw", [128, 512], BF16, stack=ph)
                Btm = K.sbuf("Btm", [128, 4, 128], BF16, stack=ph)
                abc = K.sbuf("abc", [128, 8, 128], F32, stack=ph)
                t1 = K.sbuf("t1", [128, 8, 128], F32, stack=ph)
                Dm = K.sbuf("Dm", [128, 8, 128], BF16, stack=ph)
                MT = K.sbuf("MT", [128, 8, 128], BF16, stack=ph)
                CBm = K.sbuf("CBm", [128, 128], BF16, stack=ph)
                yt = K.sbuf("yt", [128, 2048], F32, stack=ph)
                ytmp = K.sbuf("ytmp", [128, 512], F32, stack=ph)
                sm = K.sbuf("sm", [128, 64], F32, stack=ph)
                yn = K.sbuf("yn", [128, 2048], BF16, stack=ph)
                ynT = K.sbuf("ynT", [128, 2, 16, 128], BF16, stack=ph)
                stg = K.sbuf("stg", [128, 2048], F32, dma=True, stack=ph)
                if nseg == 1:
                    hT = [hTp[li]]
                else:
                    hT = [K.sbuf("hTs%d" % i, [128, 2048], F32, stack=ph) for i in range(2)]
                    cl = K.sbuf("cl", [12, 1536], F32, dma=True, stack=ph)
                    for hf in range(2):
                        K.dma(SP, cl[:], st_conv[li, :, :, hf * 1536:(hf + 1) * 1536].rearrange("b r c -> (b r) c"), [], [cl], cl)
                        for c in range(12):
                            tr(ps[hf][:, c * 12:c * 12 + 12], cl[:, c * 128:(c + 1) * 128], con[0:12, C_ID, 0:12], [cl, con], [ps[hf]])
                        cpy(DVE, cst[li][:, hf * 12:(hf + 1) * 12, :, :], ps[hf][:, 0:144].rearrange("p (c s r) -> p c s r", s=4, r=3), [ps[hf]], [cst[li]])

                def load_state(h_, src):
                    K.dma(SP, stg[:].rearrange("p (c n) -> p c n", n=128), src.rearrange("(c p) n -> p c n", p=128), [], [stg], stg)
                    for q4 in range(4):
                        for q in range(4):
                            cq = q4 * 4 + q
                            tr(ps[4 + q4][:, q * 128:(q + 1) * 128], stg[:, cq * 128:(cq + 1) * 128], cF(C_ID), [stg, con], [ps[4 + q4]])
                        cpy(DVE if q4 % 2 else ACT, h_[:, q4 * 512:(q4 + 1) * 512], ps[4 + q4][:, :], [ps[4 + q4]], [h_])

                def store_state(h_, dst):
                    for q4 in range(4):
                        for q in range(4):
                            cq = q4 * 4 + q
                            tr(ps[q4][:, q * 128:(q + 1) * 128], h_[:, cq * 128:(cq + 1) * 128], cF(C_ID), [h_, con], [ps[q4]])
                        cpy(ACT if q4 % 2 else DVE, stg[:, q4 * 512:(q4 + 1) * 512], ps[q4][:, :], [ps[q4]], [stg])
                    K.dma(SP, dst.rearrange("(c p) n -> p c n", p=128), stg[:].rearrange("p (c n) -> p c n", n=128), [stg], [], stg)

                is_last = tl["last"]
                if is_last:
                    cpy(DVE, lastT[:, :, 0:3 * nseg].rearrange("p a (s r) -> p a s r", r=3),
                        xT[:, :, :].rearrange("p a (s l) -> p a s l", l=L)[:, :, :, L - 3:L], [xT], [lastT])
                for cg in range(4):
                    wt, wv = wload(w_in[li, :, cg * 512:(cg + 1) * 512].rearrange("(a p) c -> p a c", p=128), v3(8, 512))
                    for s in range(2):
                        bank = ps[s]
                        for dk in range(8):
                            mm(bank[:, :], xT[:, dk, s * 128:(s + 1) * 128], wv[:, dk, :], dk == 0, dk == 7, [xT, wt], [bank])
                        act(zs[:, s, cg * 512:(cg + 1) * 512], bank[:, :], AF.Silu, [bank], [zs])
                for g in range(6):
                    wt, wv = wload(w_in[li, :, 2048 + g * 512:2048 + (g + 1) * 512].rearrange("(a p) c -> p a c", p=128), v3(8, 512))
                    for cc in range(4):
                        c = g * 4 + cc
                        bank = ps[2 + (c % 2)]
                        for dk in range(8):
                            mm(bank[:, 0:256], wv[:, dk, cc * 128:(cc + 1) * 128], xT[:, dk, :], dk == 0, dk == 7, [xT, wt], [bank])
                        e = ext[c % 2]
                        cv = cvt[c % 2]
                        ev = e[:, 0:nseg, 0:3 + L] if nseg == 4 else e[:, :, :].rearrange("p a b -> p (a b)")[:, 0:259].unsqueeze(1)
                        cvv = cv[:, 0:nseg, 0:L] if nseg == 4 else cv[:, :, :].rearrange("p a b -> p (a b)").unsqueeze(1)
                        cpy(ACT, ev[:, :, 3:3 + L], bank[:, 0:256].rearrange("p (s l) -> p s l", l=L), [bank], [e])
                        cpy(DVE, ev[:, :, 0:3], cst[li][:, c, 0:nseg, :], [cst[li]], [e])
                        wcol = lambda k: cvp[:, li, c * 5 + k:c * 5 + k + 1]
                        ts(cvv, ev[:, :, 0:L], wcol(0), wcol(4), ALU.mult, ALU.add, [e, cvp], [cv])
                        for k in range(1, 4):
                            stt(cvv, ev[:, :, k:k + L], wcol(k), cvv, ALU.mult, ALU.add, [e, cvp, cv], [cv])
                        cpy(DVE, cst[li][:, c, 0:nseg, :], ev[:, :, L:L + 3], [e], [cst[li]])
                        if c < 16:
                            dst, dt_ = xsT[:, c, :], xsT
                        elif c < 20:
                            dst, dt_ = BT[:, c - 16, :], BT
                        else:
                            dst, dt_ = CTt[:, c - 20, :], CTt
                        act(dst.rearrange("p (s l) -> p s l", l=L), cvv, AF.Silu, [cv], [dt_])
                    if is_last:
                        bank = ps[4]
                        for dk in range(8):
                            mm(bank[0:3 * nseg, :], lastT[:, dk, 0:3 * nseg], wv[:, dk, :], dk == 0, dk == 7, [lastT, wt], [bank])
                        co = cvout[g % 2]
                        cpy(ACT, co[0:3 * nseg, :], bank[0:3 * nseg, :], [bank], [co])
                        if tl["kind"] == "p":
                            K.dma(SP, o_pconv[li, tl["seq"], :, g * 512:(g + 1) * 512], co[0:3, :], [co], [], co)
                        else:
                            K.dma(SP, o_sconv[li, :, :, g * 512:(g + 1) * 512].rearrange("b r c -> (b r) c"), co[0:12, :], [co], [], co)
                wt, wv = wload(w_in[li, :, 5120:5152].rearrange("(a p) c -> p a c", p=128), v3(8, 32))
                for s in range(2):
                    bank = ps[4 + s]
                    for dk in range(8):
                        mm(bank[:, 0:32], xT[:, dk, s * 128:(s + 1) * 128], wv[:, dk, :], dk == 0, dk == 7, [xT, wt], [bank])
                    tt(dt[:, s, :], bank[:, 0:32], dtb_s[:, li, :], ALU.add, [bank, dtb_s], [dt])
                act(dt[:], dt[:], AF.Exp, [dt], [dt])
                act(dt[:], dt[:], AF.Ln, [dt], [dt], bias=1.0)
                tt(aa[:], dt[:], aneg[:, li, :].unsqueeze(1).to_broadcast([128, 2, 32]), ALU.mult, [dt, aneg], [aa])
                if nsub == 2:
                    cpy(DVE, CTm[:].rearrange("p a b c -> p (a b c)"), conb[:, C_CM0, 64:65].to_broadcast([128, 1024]), [conb], [CTm])

                for s in range(2):
                    cols = slice(s * 128, (s + 1) * 128)
                    sub_segs = [0] if nsub == 1 else [0, 1]
                    if nsub == 2:
                        for si in range(2):
                            load_state(hT[si], st_ssm[li, 2 * s + si, :, :])
                    prs = [slice(0, 128)] if nsub == 1 else [slice(0, 64), slice(64, 128)]
                    lastc = [127] if nsub == 1 else [63, 127]
                    for half in range(2):
                        pb = ps[half][:].bitcast(BF16)
                        for q in range(8):
                            tr(pb[:, q * 128:(q + 1) * 128], xsT[:, half * 8 + q, cols], cB(C_ID), [xsT, conb], [ps[half]])
                        cpy(ACT, xs_tm[:, half * 1024:(half + 1) * 1024], pb[:, 0:1024], [ps[half]], [xs_tm])
                    pb = ps[2][:].bitcast(BF16)
                    for g in range(4):
                        tr(pb[:, g * 128:(g + 1) * 128], BT[:, g, cols], cB(C_ID), [BT, conb], [ps[2]])
                    cpy(ACT, Btm[:].rearrange("p a b -> p (a b)"), pb[:, 0:512], [ps[2]], [Btm])
                    tt(xdt[:].rearrange("p (r q) -> p r q", q=64), xs_tm[:].rearrange("p (r q) -> p r q", q=64),
                       dt[:, s, :].unsqueeze(2).to_broadcast([128, 32, 64]), ALU.mult, [xs_tm, dt], [xdt])
                    mm(ps[3][:, 0:32], cF(TM), aa[:, s, :], True, True, [con, aa], [ps[3]])
                    cpy(DVE, acs[:], ps[3][:, 0:32], [ps[3]], [acs])
                    act(eacs[:], ps[3][:, 0:32], AF.Exp, [ps[3]], [eacs])
                    if nsub == 2:
                        for sg in range(2):
                            cpy(DVE, CTm[:, sg, :, sg * 64:(sg + 1) * 64], CTt[:, :, s * 128 + sg * 64:s * 128 + (sg + 1) * 64], [CTt], [CTm])
                    for g in range(4):
                        hs = slice(g * 8, (g + 1) * 8)
                        cpy(DVE, abc[:], aa[:, s, hs].unsqueeze(2).to_broadcast([128, 8, 128]), [aa], [abc])
                        for rr in range(8):
                            bank = ps[4 + rr // 4]
                            mm(bank[:, (rr % 4) * 128:(rr % 4 + 1) * 128], abc[:, rr, :], cF(TM), True, True, [abc, con], [bank])
                        for hb in range(2):
                            tt(t1[:, hb * 4:(hb + 1) * 4, :], ps[4 + hb][:, :].rearrange("p (r l) -> p r l", l=128),
                               acs[:, g * 8 + hb * 4:g * 8 + hb * 4 + 4].unsqueeze(2).to_broadcast([128, 4, 128]), ALU.subtract,
                               [ps[4 + hb], acs], [t1])
                        ts(t1[:], t1[:], 0.0, None, ALU.min, None, [t1], [t1])
                        act(Dm[:], t1[:], AF.Exp, [t1], [Dm])
                        mm(ps[6][:, 0:128], BT[:, g, cols], CTt[:, g, cols], True, True, [BT, CTt], [ps[6]])
                        tt(CBm[:], ps[6][:, 0:128], cF(TM), ALU.mult, [ps[6], con], [CBm])
                        tt(MT[:], Dm[:], CBm[:].unsqueeze(1).to_broadcast([128, 8, 128]), ALU.mult, [Dm, CBm], [MT])
                        for si, sgslot in enumerate(sub_segs):
                            pr = prs[si]
                            lc = lastc[si]
                            for hb in range(2):
                                rowv = ps[4 + hb][:, :].rearrange("p (r l) -> p r l", l=128)
                                tt(sm[pr, hb * 4:(hb + 1) * 4], rowv[pr, :, lc], acs[pr, g * 8 + hb * 4:g * 8 + hb * 4 + 4], ALU.subtract,
                                   [ps[4 + hb], acs], [sm])
                                act(sm[:, 16 + si * 8 + hb * 4:16 + si * 8 + (hb + 1) * 4], rowv[:, :, lc], AF.Exp, [ps[4 + hb]], [sm])
                        act(sm[:, 8:16], sm[:, 0:8], AF.Exp, [sm], [sm])
                        for rr in range(8):
                            r = g * 8 + rr
                            mm(ps[7][:, rr * 64:(rr + 1) * 64], MT[:, rr, :], xdt[:, r * 64:(r + 1) * 64], True, True, [MT, xdt], [ps[7]])
                        for si, sgslot in enumerate(sub_segs):
                            lhs = CTt[:, g, cols] if nsub == 1 else CTm[:, si, g, :]
                            hb_ = hb16[si]
                            cpy(ACT, hb_[:], hT[sgslot][:, g * 512:(g + 1) * 512], [hT[sgslot]], [hb_])
                            mm(ps[6][:, :], lhs, hb_[:], si == 0, si == len(sub_segs) - 1,
                               [CTt, CTm, hb_], [ps[6]])
                        tt(ytmp[:].rearrange("p (r q) -> p r q", q=64), ps[6][:, :].rearrange("p (r q) -> p r q", q=64),
                           eacs[:, hs].unsqueeze(2).to_broadcast([128, 8, 64]), ALU.mult, [ps[6], eacs], [ytmp])
                        tt(yt[:, g * 512:(g + 1) * 512], ytmp[:], ps[7][:, :], ALU.add, [ytmp, ps[7]], [yt])
                        tt(ytmp[:].rearrange("p (r q) -> p r q", q=64), xs_tm[:, g * 512:(g + 1) * 512].rearrange("p (r q) -> p r q", q=64),
                           dsk_s[:, li, hs].unsqueeze(2).to_broadcast([128, 8, 64]), ALU.mult, [xs_tm, dsk_s], [ytmp])
                        tt(yt[:, g * 512:(g + 1) * 512], yt[:, g * 512:(g + 1) * 512], ytmp[:], ALU.add, [yt, ytmp], [yt])
                        tt(xdtw[:].rearrange("p (r q) -> p r q", q=64), xdt[:, g * 512:(g + 1) * 512].rearrange("p (r q) -> p r q", q=64),
                           sm[:, 8:16].unsqueeze(2).to_broadcast([128, 8, 64]), ALU.mult, [xdt, sm], [xdtw])
                        for si, sgslot in enumerate(sub_segs):
                            pr = prs[si]
                            h_ = hT[sgslot]
                            mm(ps[6][:, :], Btm[pr, g, :], xdtw[pr, :], True, True, [Btm, xdtw], [ps[6]])
                            tt(ytmp[:].rearrange("p (r q) -> p r q", q=64), h_[:, g * 512:(g + 1) * 512].rearrange("p (r q) -> p r q", q=64),
                               sm[:, 16 + si * 8:24 + si * 8].unsqueeze(2).to_broadcast([128, 8, 64]), ALU.mult, [h_, sm], [ytmp])
                            tt(h_[:, g * 512:(g + 1) * 512], ytmp[:], ps[6][:, :], ALU.add, [ytmp, ps[6]], [h_])
                    tt(yt[:], yt[:], zs[:, s, :], ALU.mult, [yt, zs], [yt])
                    for g in range(4):
                        act(junk[:, 0:512], yt[:, g * 512:(g + 1) * 512], AF.Square, [yt], [junk, sm], accum_out=sm[:, 32 + g:33 + g])
                    act(sm[:, 40:44], sm[:, 32:36], AF.Sqrt, [sm], [sm], scale=1.0 / 512, bias=RMS_EPS)
                    K.op(DVE, lambda: nc.vector.reciprocal(out=sm[:, 36:40], in_=sm[:, 40:44]), [sm], [sm])
                    for g in range(4):
                        act(yn[:, g * 512:(g + 1) * 512], yt[:, g * 512:(g + 1) * 512], AF.Copy, [yt, sm], [yn], scale=sm[:, 36 + g:37 + g])
                    for half in range(2):
                        pb = ps[half][:].bitcast(BF16)
                        for q in range(8):
                            tr(pb[:, q * 128:(q + 1) * 128], yn[:, (half * 8 + q) * 128:(half * 8 + q + 1) * 128], cB(C_ID), [yn, conb], [ps[half]])
                        tt(ynT[:, s, half * 8:(half + 1) * 8, :], pb[:, 0:1024].rearrange("p (a b) -> p a b", b=128),
                           nwp[:, li, half * 8:(half + 1) * 8].unsqueeze(2).to_broadcast([128, 8, 128]), ALU.mult, [ps[half], nwp], [ynT])
                    if nsub == 2:
                        for si in range(2):
                            store_state(hT[si], o_sssm[li, 2 * s + si, :, :])
                if is_last and nsub == 1:
                    store_state(hT[0], o_pssm[li, tl["seq"], :, :])
                load_gb(li, 0)
                banks = [ps[0], ps[1], ps[2], ps[3]]
                for j in range(4):
                    wt, wv = wload(w_out[li, j * 512:(j + 1) * 512, :].rearrange("(a p) c -> p a c", p=128), v3(4, 1024))
                    for cc in range(4):
                        cq = j * 4 + cc
                        for s in range(2):
                            for hh in range(2):
                                mm(banks[s * 2 + hh][:, :], ynT[:, s, cq, :], wv[:, cc, hh * 512:(hh + 1) * 512], cq == 0, cq == 15,
                                   [ynT, wt], [banks[s * 2 + hh]])
                for s in range(2):
                    deepnorm(s, banks[s * 2:s * 2 + 2], li, 0)
            make_xT()

        import os
        pstop = int(os.environ.get("PSTOP", "99"))

        def peer_phase(li):
            with K.phase() as ph:
              for _once in (0,):
                    GT = K.sbuf("GT", [128, 128, 256], BF16, stack=ph)
                    qT = K.sbuf("qT", [128, 16, 256], BF16, stack=ph)
                    skb = K.sbuf("skb", [128, 2, 128], BF16, dma=True, stack=ph)
                    ssb = K.sbuf("ssb", [128, 16, 128], F32, stack=ph)
                    tmp = K.sbuf("tmp", [128, 256], F32, stack=ph)
                    sv = K.sbuf("sv", [128, 16, 16], F32, stack=ph)
                    siu = K.sbuf("siu", [128, 8, 16], U32, stack=ph)
                    sif = K.sbuf("sif", [128, 8, 16], F32, stack=ph)
                    cand = K.sbuf("cand", [128, 8, 256], F32, stack=ph)
                    best = K.sbuf("best", [128, 8, 16], F32, stack=ph)
                    eb = K.sbuf("eb", [128, 8, 16], F32, stack=ph)
                    zz = K.sbuf("zz", [128, 32], F32, stack=ph)
                    bia = K.sbuf("bia", [128, 8, 16], F32, stack=ph)
                    thr = K.sbuf("thr", [128, 8, 16], F32, stack=ph)
                    biaT = K.sbuf("biaT", [128, 256], F32, stack=ph)
                    thrT = K.sbuf("thrT", [128, 256], F32, stack=ph)
                    idxT = K.sbuf("idxT", [128, 256], F32, stack=ph)
                    Qrep = [K.sbuf("Qrep%d" % i, [128, 8, 128], BF16, stack=ph) for i in range(2)]
                    OH = [K.sbuf("OH%d" % i, [128, 8, 128], BF16, stack=ph) for i in range(2)]
                    et = [K.sbuf("et%d" % i, [128, 128], F32, stack=ph) for i in range(4)]
                    Rt = [K.sbuf("Rt%d" % i, [128, 128], BF16, stack=ph) for i in range(4)]
                    gh = [K.sbuf("gh%d" % i, [128, 256], BF16, stack=ph) for i in range(3)]
                    ATt = [K.sbuf("ATt%d" % i, [128, 256], BF16, stack=ph) for i in range(3)]

                    K.dma(POOL, skb[:], skT[li, :, :, :].rearrange("k d n -> d k n"), [], [skb], skb)
                    for cg in range(4):
                        wt, wv = wload(pwq[li, :, cg * 512:(cg + 1) * 512].rearrange("(a p) c -> p a c", p=128), v3(8, 512))
                        for cc in range(4):
                            oc = cg * 4 + cc
                            bank = ps[oc % 2]
                            for dk in range(8):
                                mm(bank[:, 0:256], wv[:, dk, cc * 128:(cc + 1) * 128], xT[:, dk, :], dk == 0, dk == 7, [xT, wt], [bank])
                            cpy(ACT if oc % 2 else DVE, qT[:, oc, :], bank[:, 0:256], [bank], [qT])
                    if pstop == 1:
                        break
                    for s in range(2):
                        cols = slice(s * 128, (s + 1) * 128)
                        for oc in range(16):
                            bank = ps[2 + oc // 4]
                            mm(bank[:, (oc % 4) * 128:(oc % 4 + 1) * 128], qT[:, oc, cols], skb[:, oc % 2, :], True, True, [qT, skb], [bank])
                        for q4 in range(4):
                            cpy(ACT, ssb[:, q4 * 4:(q4 + 1) * 4, :], ps[2 + q4][:, :].rearrange("p (a b) -> p a b", b=128), [ps[2 + q4]], [ssb])
                        for oc in range(16):
                            K.op(DVE, lambda: nc.vector.max(out=sv[:, oc, 0:8], in_=ssb[:, oc, :]), [ssb], [sv])
                            if oc % 2 == 0:
                                K.op(DVE, lambda: nc.vector.max_index(out=siu[:, oc // 2, 0:8], in_max=sv[:, oc, 0:8], in_values=ssb[:, oc, :]), [ssb, sv], [siu])
                            K.op(DVE, lambda: nc.vector.match_replace(out=tmp[:, 0:128], in_to_replace=sv[:, oc, 0:8], in_values=ssb[:, oc, :], imm_value=-1e30), [ssb, sv], [tmp])
                            K.op(DVE, lambda: nc.vector.max(out=sv[:, oc, 8:16], in_=tmp[:, 0:128]), [tmp], [sv])
                            if oc % 2 == 0:
                                K.op(DVE, lambda: nc.vector.max_index(out=siu[:, oc // 2, 8:16], in_max=sv[:, oc, 8:16], in_values=tmp[:, 0:128]), [tmp, sv], [siu])
                        cpy(DVE, sif[:], siu[:], [siu], [sif])
                        sv4 = sv[:].rearrange("p (h k) a -> p h k a", k=2)
                        tt(cand[:].rearrange("p h (a b) -> p h a b", b=16), sv4[:, :, 0, :].unsqueeze(3).to_broadcast([128, 8, 16, 16]),
                           sv4[:, :, 1, :].unsqueeze(2).to_broadcast([128, 8, 16, 16]), ALU.add, [sv], [cand])
                        for h in range(8):
                            K.op(DVE, lambda: nc.vector.max(out=best[:, h, 0:8], in_=cand[:, h, :]), [cand], [best])
                            K.op(DVE, lambda: nc.vector.match_replace(out=tmp[:], in_to_replace=best[:, h, 0:8], in_values=cand[:, h, :], imm_value=-1e30), [cand, best], [tmp])
                            K.op(DVE, lambda: nc.vector.max(out=best[:, h, 8:16], in_=tmp[:]), [tmp], [best])
                        tt(eb[:], best[:], best[:, :, 0:1].to_broadcast([128, 8, 16]), ALU.subtract, [best], [eb])
                        act(eb[:], eb[:], AF.Exp, [eb], [eb])
                        K.op(DVE, lambda: nc.vector.tensor_reduce(out=zz[:, 0:8], in_=eb[:], axis=AX.X, op=ALU.add), [eb], [zz])
                        act(zz[:, 8:16], zz[:, 0:8], AF.Ln, [zz], [zz])
                        tt(zz[:, 16:24], best[:, :, 0], zz[:, 8:16], ALU.add, [best, zz], [zz])
                        tt(bia[:], sv4[:, :, 0, :], zz[:, 16:24].unsqueeze(2).to_broadcast([128, 8, 16]), ALU.subtract, [sv, zz], [bia])
                        stt(thr[:], sv4[:, :, 0, :], -1.0, best[:, :, 15:16].to_broadcast([128, 8, 16]), ALU.mult, ALU.add, [sv, best], [thr])
                        ts(thr[:], thr[:], -2e-5, None, ALU.add, None, [thr], [thr])
                        for k_, (src, dstT) in enumerate(((bia, biaT), (thr, thrT), (sif, idxT))):
                            bank = ps[2 + k_]
                            tr(bank[:, 0:128], src[:].rearrange("p h a -> p (h a)"), cF(C_ID), [src, con], [bank])
                            cpy(ACT, dstT[:, cols], bank[:, 0:128], [bank], [dstT])
                    if pstop == 2:
                        break
                    LA = 3

                    def fill_batch(b):
                        t0 = b * 8
                        q_, o_ = Qrep[b % 2], OH[b % 2]
                        for h in range(8):
                            cpy(DVE, q_[:, :, h * 16:(h + 1) * 16], qT[:, 2 * h + 1, t0:t0 + 8].unsqueeze(2).to_broadcast([128, 8, 16]), [qT], [q_])
                        tt(o_[:], con[:, C_IOTA, :].unsqueeze(1).to_broadcast([128, 8, 128]),
                           idxT[:, t0:t0 + 8].unsqueeze(2).to_broadcast([128, 8, 128]), ALU.is_equal, [con, idxT], [o_])

                    def stage_a(t):
                        if t % 8 == 0:
                            fill_batch(t // 8)
                        q_ = Qrep[(t // 8) % 2]
                        abank = ps[t % 4]
                        mm(abank[:, 0:128], q_[:, t % 8, :], skb[:, 1, :], True, True, [q_, skb], [abank])

                    def stage_b(t):
                        abank = ps[t % 4]
                        o_ = OH[(t // 8) % 2]
                        gbank = ps[4 + (t // 4) % 2]
                        e_ = et[t % 4]
                        r_ = Rt[t % 4]
                        act(e_[:], abank[:, 0:128], AF.Exp, [abank, biaT], [e_], bias=biaT[:, t:t + 1])
                        stt(r_[:], abank[:, 0:128], thrT[:, t:t + 1], e_[:], ALU.is_ge, ALU.mult, [abank, thrT, e_], [r_])
                        mm(gbank[:, (t % 4) * 128:(t % 4 + 1) * 128], r_[:], o_[:, t % 8, :], True, True, [r_, o_], [gbank])
                        if t % 4 == 3:
                            cpy(ACT, GT[:, :, t - 3:t + 1], gbank[:, :].rearrange("p (t i) -> p i t", i=128), [gbank], [GT])

                    for t in range(LA):
                        stage_a(t)
                    for t in range(256):
                        if t + LA < 256:
                            stage_a(t + LA)
                        stage_b(t)
                    if pstop == 3:
                        break
                    load_gb(li, 1)
                    accb = [ps[0], ps[1], ps[2], ps[3]]
                    NCG = int(os.environ.get('NCG', '32'))
                    wts = {}

                    def stage_1(c):
                        cg, cc = c // 4, c % 4
                        if cc == 0:
                            ut, uv = wload(uT[li, :, cg * 512:(cg + 1) * 512].rearrange("(a p) e -> p a e", p=128), v3(8, 512))
                            vt_, vv = wload(vt[li, cg * 512:(cg + 1) * 512, :].rearrange("(a p) d -> p a d", p=128), v3(4, 1024))
                            wts[cg] = (ut, uv, vt_, vv)
                        ut, uv, vt_, vv = wts[cg]
                        hb = ps[4 + c % 3]
                        for dk in range(8):
                            mm(hb[:, 0:256], uv[:, dk, cc * 128:(cc + 1) * 128], xT[:, dk, :], dk == 0, dk == 7, [xT, ut], [hb])
                        g_ = gh[c % 3]
                        a_ = ATt[c % 3]
                        act(g_[:], hb[:, 0:256], AF.Gelu, [hb], [g_])
                        tt(a_[:], g_[:], GT[:, c, :], ALU.mult, [g_, GT], [a_])

                    def stage_2(c):
                        cg, cc = c // 4, c % 4
                        ut, uv, vt_, vv = wts[cg]
                        a_ = ATt[c % 3]
                        for s in range(2):
                            for hh in range(2):
                                mm(accb[s * 2 + hh][:, :], a_[:, s * 128:(s + 1) * 128], vv[:, cc, hh * 512:(hh + 1) * 512], c == 0, c == NCG * 4 - 1,
                                   [a_, vt_], [accb[s * 2 + hh]])

                    NCH = NCG * 4
                    stage_1(0)
                    stage_1(1)
                    for c in range(NCH):
                        if c + 2 < NCH:
                            stage_1(c + 2)
                        stage_2(c)
                    for s in range(2):
                        deepnorm(s, accb[s * 2:s * 2 + 2], li, 1)
            if pstop < 99:
                raise _Stop()

        kvstop = int(os.environ.get("KVSTOP", "99"))

        def kv_phase(tl):
            segs = tl["segs"]
            nseg = len(segs)
            kind = tl["kind"]
            with K.phase() as ph:
              for _once in (0,):
                    kst = [K.sbuf("kst%d" % i, [128, D], F32, dma=True, stack=ph) for i in range(2)]
                    vaug = [K.sbuf("vaug%d" % i, [128, 16, 66], BF16, dma=True, stack=ph) for i in range(2)]
                    ktile = [K.sbuf("ktile%d" % i, [128, 256], BF16, dma=True, stack=ph) for i in range(2)]
                    lf = K.sbuf("lf", [128, 2, 16], F32, dma=True, stack=ph)
                    tb = K.sbuf("tb", [128, 16], F32, stack=ph)
                    pk = K.sbuf("pk", [128, D], BF16, dma=True, stack=ph)
                    pf = K.sbuf("pf", [128, 16], F32, dma=True, stack=ph)
                    rsum = K.sbuf("rsum", [128, 16], F32, stack=ph)
                    KVX = os.environ.get("KVX", "")
                    for i in range(2):
                      if "a" not in KVX:
                        cpy(DVE, vaug[i][:, :, 64:66], conb[:, C_ONE, 0:32].rearrange("p (h q) -> p h q", q=2), [conb], [vaug[i]])

                    def tok_rows(dst, s):
                        if kind == "p":
                            b, T = tl["seq"], tl["T"]
                            return dst[b, T * 256 + s * 128:T * 256 + (s + 1) * 128, :]
                        return dst[2 * s:2 * s + 2, :, :].rearrange("b l f -> (b l) f")

                    ok, ov, of = (o_pk, o_pv, o_pf) if kind == "p" else (o_sk, o_sv, o_sf)
                    cnt = 0
                    for part in range(2):
                        for cgp in range(2):
                            cg = part * 2 + cgp
                            wt, wv = wload(kv_wkv[:, cg * 512:(cg + 1) * 512].rearrange("(a p) c -> p a c", p=128), v3(8, 512))
                            for s in range(2):
                                bank = ps[s]
                                for dk in range(8):
                                    mm(bank[:, :], xT[:, dk, s * 128:(s + 1) * 128], wv[:, dk, :], dk == 0, dk == 7, [xT, wt], [bank])
                                kk = kst[s]
                                if "b" not in KVX:
                                    cpy(ACT, kk[:, cgp * 512:(cgp + 1) * 512], bank[:, :], [bank], [kk])
                                if part == 1 and "c" not in KVX:
                                    cpy(DVE, vaug[s][:, cgp * 8:(cgp + 1) * 8, 0:64], bank[:, :].rearrange("p (h q) -> p h q", q=64), [bank], [vaug[s]])
                                if cgp == 1:
                                    K.dma(SP, tok_rows(ok if part == 0 else ov, s), kk[:], [kk], [], kk, tag="ok")
                            if part == 0 and "d" not in KVX:
                                for cc in range(4):
                                    c = cgp * 4 + cc
                                    bank = ps[2 + c % 2]
                                    for dk in range(8):
                                        mm(bank[:, 0:256], wv[:, dk, cc * 128:(cc + 1) * 128], xT[:, dk, :], dk == 0, dk == 7, [xT, wt], [bank])
                                    kt_ = ktile[c % 2]
                                    cpy(ACT if c % 2 else DVE, kt_[:], bank[:, 0:256], [bank], [kt_])
                                    if kind == "p":
                                        sl = tl["slot"]
                                        K.dma(SP, KTscr[sl][c, :, tl["T"] * 256:(tl["T"] + 1) * 256], kt_[:], [kt_], [KTscr[sl]], kt_, tag="kt")
                                    else:
                                        for i in range(4):
                                            K.dma(SP, KTscr[NP + i][c, :, PAST:PAST + 64], kt_[:, i * 64:(i + 1) * 64], [kt_], [KTscr[NP + i]], kt_)
                    if kvstop == 1:
                        break
                    for s in range(2):
                        if kind == "p":
                            sl = tl["slot"]
                            r0 = tl["T"] * 256 + s * 128
                            K.dma(SP, Vscr[sl][r0:r0 + 128, :], vaug[s][:].rearrange("p h q -> p (h q)"), [vaug[s]], [Vscr[sl]], vaug[s], tag="vs")
                        else:
                            for sg in range(2):
                                K.dma(SP, Vscr[NP + 2 * s + sg][PAST:PAST + 64, :], vaug[s][sg * 64:(sg + 1) * 64, :, :].rearrange("p h q -> p (h q)"),
                                      [vaug[s]], [Vscr[NP + 2 * s + sg]], vaug[s])
                    if kvstop == 2:
                        break
                    wf32 = K.sbuf("wf32", [128, 8, 16], F32, dma=True, stack=ph)
                    wt = K.sbuf("wfb", [128, 8, 16], BF16, stack=ph)
                    K.dma(SP, wf32[:], kv_wf[:, :].rearrange("(a p) c -> p a c", p=128), [], [wf32], wf32)
                    cpy(DVE, wt[:], wf32[:], [wf32], [wt])
                    wv = wt[:]
                    for s in range(2):
                        bank = ps[4 + s]
                        for dk in range(8):
                            mm(bank[:, 0:16], xT[:, dk, s * 128:(s + 1) * 128], wv[:, dk, :], dk == 0, dk == 7, [xT, wt], [bank])
                        tt(lf[:, s, :], bank[:, 0:16], kvb_s[:], ALU.add, [bank, kvb_s], [lf])
                    act(lf[:], lf[:], AF.Exp, [lf], [lf], scale=-1.0)
                    act(lf[:], lf[:], AF.Ln, [lf], [lf], bias=1.0)
                    ts(lf[:], lf[:], -1.0, None, ALU.mult, None, [lf], [lf])
                    for s in range(2):
                        K.dma(SP, tok_rows(of, s), lf[:, s, :], [lf], [], lf, tag="lf")
                    if kvstop == 3:
                        break
                    if kind == "p":
                        sl = 0
                        T = tl["T"]
                        if T == 0:
                            K.op(DVE, lambda: nc.vector.memset(SUF[:, 0, :, :], 0.0), [], [SUF])
                        for s in range(2):
                            j = T * 2 + s
                            mm(ps[6][:, 0:16], cF(C_TSU), lf[:, s, :], True, True, [con, lf], [ps[6]])
                            mm(ps[6][:, 16:32], cF(C_ONE), lf[:, s, :], True, True, [con, lf], [ps[6]])
                            mm(ps[6][:, 32:48], cF(C_TRI), lf[:, s, :], True, True, [con, lf], [ps[6]])
                            AT = AT0 if s == 0 else AT1
                            if j > 0:
                                cpy(DVE, AT[:, 0, 0:j, :], SUF[:, 0, 0:j, :], [SUF], [AT])
                            ts(AT[:, 0, j, :], ps[6][:, 32:48], -1.0, None, ALU.mult, None, [ps[6]], [AT])
                            if j > 0:
                                tt(SUF[:, 0, 0:j, :], SUF[:, 0, 0:j, :], ps[6][:, 16:32].unsqueeze(1).to_broadcast([128, j, 16]), ALU.add, [SUF, ps[6]], [SUF])
                            cpy(DVE, SUF[:, 0, j, :], ps[6][:, 0:16], [ps[6]], [SUF])
                    else:
                        for i in range(4):
                            s, sg = i // 2, i % 2
                            pr = slice(sg * 64, (sg + 1) * 64)
                            mm(ps[6][0:64, i * 16:(i + 1) * 16], con[pr, C_TRI, sg * 64:(sg + 1) * 64], lf[pr, s, :], True, True, [con, lf], [ps[6]])
                        ts(NCS[0:64, :, :].rearrange("p a b -> p (a b)"), ps[6][0:64, 0:64], -1.0, None, ALU.mult, None, [ps[6]], [NCS])
                        for i in range(4):
                            K.op(DVE, lambda: nc.vector.memset(rsum[:], 0.0), [], [rsum])
                            for j in range(NKT_PAST - 1, -1, -1):
                                K.dma(SP, pf[:], c_f[i, j * 128:(j + 1) * 128, :], [], [pf], pf)
                                mm(ps[6][:, 0:16], cF(C_TSU), pf[:], True, False, [con, pf], [ps[6]])
                                mm(ps[6][:, 0:16], cF(C_ONE), rsum[:], False, True, [con, rsum], [ps[6]])
                                cpy(DVE, SUF[:, i, j, :], ps[6][:, 0:16], [ps[6]], [SUF])
                                tt(rsum[:], rsum[:], pf[:], ALU.add, [rsum, pf], [rsum])
                            for j in range(NKT_PAST):
                                K.dma(POOL, pk[:], c_k[i, j * 128:(j + 1) * 128, :], [], [pk], pk)
                                for half in range(1):
                                    pb = ps[j % 2][:].bitcast(BF16)
                                    for q in range(8):
                                        tr(pb[:, q * 128:(q + 1) * 128], pk[:, q * 128:(q + 1) * 128], cB(C_ID), [pk, conb], [ps[j % 2]])
                                    kt_ = kst[j % 2]
                                    ktb = kt_[:].bitcast(BF16)
                                    cpy(ACT if j % 2 else DVE, ktb[:, 0:1024], pb[:, 0:1024], [ps[j % 2]], [kt_])
                                    K.dma(SP, KTscr[NP + i][:, :, j * 128:(j + 1) * 128].rearrange("c p k -> p c k"),
                                          ktb[:, 0:1024].rearrange("p (c k) -> p c k", k=128), [kt_], [KTscr[NP + i]], kt_)
                                va = vaug[j % 2]
                                K.dma(POOL, va[:, :, 0:64], c_v[i, j * 128:(j + 1) * 128, :].rearrange("p (h q) -> p h q", q=64), [], [va], va)
                                K.dma(SP, Vscr[NP + i][j * 128:(j + 1) * 128, :], va[:].rearrange("p h q -> p (h q)"), [va], [Vscr[NP + i]], va)

        def attn_phase(lj, li, tl):
            segs = tl["segs"]
            kind = tl["kind"]
            with K.phase() as ph:
                KT = K.sbuf("KT", [128, 8, KMAX], BF16, dma=True, stack=ph)
                VA = K.sbuf("VA", [128, KMAX // 128 + 1, 1056], BF16, dma=True, stack=ph)
                qTt = K.sbuf("qTt", [128, 8, 256], BF16, stack=ph)
                gs = K.sbuf("gs", [128, 2, D], BF16, stack=ph)
                PT = [K.sbuf("PT%d" % i, [128, 128], BF16, stack=ph) for i in range(4)]
                rc = K.sbuf("rc", [128, 4], F32, stack=ph)
                og = K.sbuf("og", [128, D], F32, stack=ph)
                ogb = K.sbuf("ogb", [128, D], BF16, stack=ph)
                ogT = K.sbuf("ogT", [128, 2, 8, 128], BF16, stack=ph)
                for cg in range(2):
                    wt, wv = wload(w_qg[lj, :, cg * 512:(cg + 1) * 512].rearrange("(a p) c -> p a c", p=128), v3(8, 512))
                    for cc in range(4):
                        c = cg * 4 + cc
                        bank = ps[c % 2]
                        for dk in range(8):
                            mm(bank[:, 0:256], wv[:, dk, cc * 128:(cc + 1) * 128], xT[:, dk, :], dk == 0, dk == 7, [xT, wt], [bank])
                        cpy(ACT if c % 2 else DVE, qTt[:, c, :], bank[:, 0:256], [bank], [qTt])
                for cg in range(2):
                    wt, wv = wload(w_qg[lj, :, 1024 + cg * 512:1024 + (cg + 1) * 512].rearrange("(a p) c -> p a c", p=128), v3(8, 512))
                    for s in range(2):
                        bank = ps[2 + s]
                        for dk in range(8):
                            mm(bank[:, :], xT[:, dk, s * 128:(s + 1) * 128], wv[:, dk, :], dk == 0, dk == 7, [xT, wt], [bank])
                        act(gs[:, s, cg * 512:(cg + 1) * 512], bank[:, :], AF.Sigmoid, [bank], [gs])
                cnt = [0]

                def attend(s, keysets):
                    nks = len(keysets)
                    for ki, ks in enumerate(keysets):
                        ks["load"]()
                        nk = ks["nfull"] + (1 if ks["part"] else 0)
                        for h in range(16):
                            accb = ps[4 + h // 4]
                            hh = h % 4
                            c, pbs = h // 2, (h % 2) * 64
                            for j in range(nk):
                                part = (j == ks["nfull"])
                                kp = 64 if part else 128
                                sb = ps[cnt[0] % 4]
                                p_ = PT[cnt[0] % 4]
                                cnt[0] += 1
                                mm(sb[0:kp, 0:128], KT[pbs:pbs + 64, c, j * 128:j * 128 + kp], qTt[pbs:pbs + 64, c, s * 128:(s + 1) * 128], True, True,
                                   [KT, qTt], [sb])
                                bias_ap, btile = ks["bias"](j, h)
                                act(p_[0:kp, :], sb[0:kp, 0:128], AF.Exp, [sb, btile], [p_], scale=ATT_SCALE, bias=bias_ap)
                                msk = ks["mask"](j)
                                if msk is not None:
                                    tt(p_[0:kp, :], p_[0:kp, :], msk, ALU.mult, [p_, conb], [p_])
                                mm(accb[:, hh * 65:(hh + 1) * 65], p_[0:kp, :], VA[0:kp, j, h * 66:h * 66 + 65],
                                   j == 0, j == nk - 1, [p_, VA], [accb])
                        pr = ks["rows"]
                        npr = pr.stop - pr.start
                        for hg in range(4):
                            accb = ps[4 + hg]
                            av = accb[pr, 0:260].rearrange("p (h q) -> p h q", q=65)
                            K.op(DVE, lambda: nc.vector.reciprocal(out=rc[pr, :], in_=av[:, :, 64]), [accb], [rc])
                            tt(og[pr, hg * 256:(hg + 1) * 256].rearrange("p (h q) -> p h q", q=64), av[:, :, 0:64],
                               rc[pr, :].unsqueeze(2).to_broadcast([npr, 4, 64]), ALU.mult, [accb, rc], [og])

                for s in range(2):
                    if kind == "p":
                        sl = tl["slot"]
                        T = tl["T"]
                        nkeys = (T + 1) * 256
                        nfull = T * 2 + s + 1

                        def load_p(sl=sl, nkeys=nkeys, s=s):
                            if s == 0:
                                K.dma(SP, KT[:, :, 0:nkeys], KTscr[sl][:, :, 0:nkeys].rearrange("c p k -> p c k"), [KTscr[sl]], [KT], KT)
                                K.dma(SP, VA[:, 0:nkeys // 128, :], Vscr[sl][0:nkeys, :].rearrange("(j p) f -> p j f", p=128), [Vscr[sl]], [VA], VA)
                        AT = AT0 if s == 0 else AT1
                        attend(s, [dict(load=load_p, nfull=nfull, part=False, rows=slice(0, 128),
                                        bias=lambda j, h, AT=AT: (AT[:, 0, j, h:h + 1], AT),
                                        mask=lambda j, nfull=nfull: (cB(C_TRI) if j == nfull - 1 else None))])
                    else:
                        kss = []
                        for sg in range(2):
                            i = 2 * s + sg

                            def load_s(i=i):
                                K.dma(SP, KT[:, :, 0:PAST + 64], KTscr[NP + i][:, :, 0:PAST + 64].rearrange("c p k -> p c k"), [KTscr[NP + i]], [KT], KT)
                                K.dma(SP, VA[:, 0:NKT_PAST, :], Vscr[NP + i][0:PAST, :].rearrange("(j p) f -> p j f", p=128), [Vscr[NP + i]], [VA], VA)
                                K.dma(SP, VA[0:64, NKT_PAST, :], Vscr[NP + i][PAST:PAST + 64, :], [Vscr[NP + i]], [VA], VA)

                            def bias_s(j, h, i=i):
                                if j == NKT_PAST:
                                    return NCS[0:64, i, h:h + 1], NCS
                                return SUF[:, i, j, h:h + 1], SUF

                            def mask_s(j, sg=sg):
                                if j == NKT_PAST:
                                    return conb[0:64, C_MS0 + sg, :]
                                return cB(C_CM0 + sg)
                            kss.append(dict(load=load_s, nfull=NKT_PAST, part=True, bias=bias_s, mask=mask_s, rows=slice(sg * 64, (sg + 1) * 64)))
                        attend(s, kss)
                    tt(ogb[:], og[:], gs[:, s, :], ALU.mult, [og, gs], [ogb])
                    pb = ps[s][:].bitcast(BF16)
                    for q in range(8):
                        tr(pb[:, q * 128:(q + 1) * 128], ogb[:, q * 128:(q + 1) * 128], cB(C_ID), [ogb, conb], [ps[s]])
                    cpy(ACT, ogT[:, s, :, :], pb[:, 0:1024].rearrange("p (a b) -> p a b", b=128), [ps[s]], [ogT])
                load_gb(li, 0)
                banks = [ps[0], ps[1], ps[2], ps[3]]
                for j in range(2):
                    wt, wv = wload(w_o[lj, j * 512:(j + 1) * 512, :].rearrange("(a p) c -> p a c", p=128), v3(4, 1024))
                    for cc in range(4):
                        cq = j * 4 + cc
                        for s in range(2):
                            for hh in range(2):
                                mm(banks[s * 2 + hh][:, :], ogT[:, s, cq, :], wv[:, cc, hh * 512:(hh + 1) * 512], cq == 0, cq == 7,
                                   [ogT, wt], [banks[s * 2 + hh]])
                for s in range(2):
                    deepnorm(s, banks[s * 2:s * 2 + 2], li, 0)
            make_xT()

        tiles = []
        for b in range(NP):
            for T in range(NT_P):
                tiles.append(dict(kind="p", seq=b, T=T, slot=b, segs=[0], last=(T == NT_P - 1), outseq=[("p", b)]))
        tiles.append(dict(kind="s", segs=[0, 1, 2, 3], last=True, outseq=[("s", i) for i in range(4)], T=0))

        stop_at = int(os.environ.get("KSTOP", "100000"))
        stage = [0]

        class _Stop(Exception):
            pass

        def chk():
            stage[0] += 1
            if stage[0] >= stop_at:
                raise _Stop()

        try:
          for tl in tiles:
              kind = tl["kind"]
              if kind == "p":
                  K.dma(SP, x[:], x_p[tl["seq"], tl["T"] * 256:(tl["T"] + 1) * 256, :].rearrange("(s p) d -> p s d", p=128), [], [x], x)
                  if tl["T"] == 0:
                      for l in range(2):
                          K.op(DVE, lambda: nc.vector.memset(hTp[l][:], 0.0), [], [hTp[l]])
                          K.op(DVE, lambda: nc.vector.memset(cst[l][:], 0.0), [], [cst[l]])
              else:
                  K.dma(SP, x[:], x_s[:, :, :].rearrange("b l d -> (b l) d").rearrange("(s p) d -> p s d", p=128), [], [x], x)
              make_xT()
              chk()
              konly = os.environ.get("KONLY", "")
              if konly == "kv":
                  kv_phase(tl)
                  raise _Stop()
              for li in range(DEPTH):
                  if li < 2:
                      mamba_phase(li, tl)
                  else:
                      attn_phase(li - 2, li, tl)
                  chk()
                  peer_phase(li)
                  chk()
                  if li < DEPTH - 1:
                      make_xT()
                  if li == 1:
                      kv_phase(tl)
                      chk()
              if kind == "p":
                  K.dma(SP, y_p[tl["seq"], tl["T"] * 256:(tl["T"] + 1) * 256, :].rearrange("(s p) d -> p s d", p=128), x[:], [x], [], x)
              else:
                  K.dma(SP, y_s[:, :, :].rearrange("b l d -> (b l) d").rearrange("(s p) d -> p s d", p=128), x[:], [x], [], x)
        except _Stop:
            K.dma(SP, y_p[0, 0:256, :].rearrange("(s p) d -> p s d", p=128), x[:], [x], [], x)
        K.finish()
        build.stats = dict(nsem=K.nsem, nwait=K.nwait, n={E.name: E.n for E in K.engs})
    return nc


def _consts():
    t = np.zeros((NCON, 128, 128), np.float32)
    idx = np.arange(128)
    tri = (idx[:, None] <= idx[None, :]).astype(np.float32)
    blk = ((idx[:, None] // 64) == (idx[None, :] // 64)).astype(np.float32)
    t[0] = np.eye(128, dtype=np.float32)
    t[1] = tri
    t[2] = tri * blk
    t[3] = 1.0
    t[4] = blk
    t[5] = np.broadcast_to(idx[None, :].astype(np.float32), (128, 128))
    t[6] = (idx[:, None] > idx[None, :]).astype(np.float32)
    t[7][0:64, 0:64] = tri[0:64, 0:64]
    t[8][0:64, 64:128] = tri[0:64, 0:64]
    t[9][:, 0:64] = 1.0
    t[10][:, 64:128] = 1.0
    return np.ascontiguousarray(t.transpose(1, 0, 2).reshape(128, NCON * 128))


_NC_CACHE = {}


def _NEXP():
    import os
    return int(os.environ.get("KEXP", "16384"))


def run(inputs, n_cores):
    f = lambda a: np.ascontiguousarray(np.asarray(a, dtype=np.float32))
    xp = f(inputs["x_prompt"]); xs = f(inputs["x_sample"])
    B, SEQ, _ = xp.shape
    DB = xs.shape[0]
    assert DB == 4 * n_cores and B % n_cores == 0 and xs.shape[1] == 64
    NP = B // n_cores
    PAST = inputs["cache_k"].shape[1]
    cfg = (NP, SEQ, PAST)
    if cfg not in _NC_CACHE:
        _NC_CACHE[cfg] = build(cfg)
    nc = _NC_CACHE[cfg]
    ssm = f(inputs["state_ssm"]).reshape(2, DB, 2048, 128)
    cvs = f(inputs["state_conv"])
    ck = f(inputs["cache_k"]).reshape(DB, PAST, D); cv = f(inputs["cache_v"]).reshape(DB, PAST, D); cf = f(inputs["cache_logf"])
    cw = f(inputs["a_conv_w"]); cb = f(inputs["a_conv_b"])
    convp = np.zeros((2, 128, 24, 5), np.float32)
    convp[:, :, :, 0:4] = cw.reshape(2, 4, 24, 128).transpose(0, 3, 2, 1)
    convp[:, :, :, 4] = cb.reshape(2, 24, 128).transpose(0, 2, 1)
    shared = {
        "w_in": f(inputs["a_w_in"]), "convp": np.ascontiguousarray(convp.reshape(2, 128, 120)),
        "dtb": f(inputs["a_dt_bias"]), "alog": f(inputs["a_A_log"]), "dsk": f(inputs["a_D"]),
        "normw": np.ascontiguousarray(f(inputs["a_norm_w"]).reshape(2, 16, 128).transpose(0, 2, 1)),
        "w_out": f(inputs["a_w_out"]), "kv_wkv": np.ascontiguousarray(f(inputs["kv_w"])[:, 0:2048]),
        "kv_wf": np.ascontiguousarray(f(inputs["kv_w"])[:, 2048:2064]), "kvb": f(inputs["kv_b_f"]),
        "w_qg": f(inputs["b_w_qg"]), "w_o": f(inputs["b_w_o"]), "pwq": f(inputs["peer_w_q"]),
        "skT": np.ascontiguousarray(f(inputs["peer_subkeys"]).transpose(0, 1, 3, 2)),
        "uT": np.ascontiguousarray(f(inputs["peer_u"])[:, 0:_NEXP()].transpose(0, 2, 1)), "vt": np.ascontiguousarray(f(inputs["peer_v"])[:, 0:_NEXP()]),
        "lng": f(inputs["ln_g"]), "lnb": f(inputs["ln_b"]), "consts": _consts(),
    }
    in_maps = []
    for i in range(n_cores):
        m = dict(shared)
        m["x_p"] = np.ascontiguousarray(xp[i * NP:(i + 1) * NP]); m["x_s"] = np.ascontiguousarray(xs[4 * i:4 * i + 4])
        m["st_ssm"] = np.ascontiguousarray(ssm[:, 4 * i:4 * i + 4]); m["st_conv"] = np.ascontiguousarray(cvs[:, 4 * i:4 * i + 4])
        m["c_k"] = np.ascontiguousarray(ck[4 * i:4 * i + 4]); m["c_v"] = np.ascontiguousarray(cv[4 * i:4 * i + 4])
        m["c_f"] = np.ascontiguousarray(cf[4 * i:4 * i + 4])
        in_maps.append(m)
    res = run_bass_kernel_spmd(nc, in_maps, core_ids=list(range(n_cores)))
    R = res.results
    cat = lambda k, ax: np.concatenate([np.asarray(r[k]) for r in R], axis=ax)
    y_p = cat("y_p", 0); y_s = cat("y_s", 0)
    p_ssm = cat("o_pssm", 1).reshape(2, B, 32, 64, 128); p_conv = cat("o_pconv", 1)
    p_k = cat("o_pk", 0).reshape(B, SEQ, 16, 64); p_v = cat("o_pv", 0).reshape(B, SEQ, 16, 64); p_f = cat("o_pf", 0)
    s_ssm = cat("o_sssm", 1).reshape(2, DB, 32, 64, 128); s_conv = cat("o_sconv", 1)
    s_k = cat("o_sk", 0).reshape(DB, 64, 16, 64); s_v = cat("o_sv", 0).reshape(DB, 64, 16, 64); s_f = cat("o_sf", 0)
    return (y_p, y_s, p_ssm, p_conv, p_k, p_v, p_f, s_ssm, s_conv, s_k, s_v, s_f)


def kernel(**inputs):
    return run(inputs, 8)
```

```python
import numpy as np
from contextlib import ExitStack
import concourse.bass as bass
import concourse.mybir as mybir
from concourse.bass_utils import run_bass_kernel_spmd

F32 = mybir.dt.float32
BF16 = mybir.dt.bfloat16
U32 = mybir.dt.uint32
AF = mybir.ActivationFunctionType
ALU = mybir.AluOpType
AX = mybir.AxisListType

SEM_CH = 30000
D = 1024
DEPTH = 4
ALPHA = float((2.0 * DEPTH) ** 0.25)
LN_EPS = 1e-5
RMS_EPS = 1e-5
ATT_SCALE = 64 ** -0.5
NCON = 11


class Tile:
    def __init__(self, K, handle, name, dma_sem=False, multi=False):
        self.K = K
        self.h = handle
        self.name = name
        self.w = None
        self.wd = {} if multi else None
        self.r = {}
        self.is_psum = False
        self.dsem = {}
        if dma_sem:
            K.dma_tiles.append(self)
            if K.phase_tiles is not None:
                K.phase_tiles.append(self)

    def __getitem__(self, key):
        return self.h[key]


class Eng:
    def __init__(self, K, name, eng):
        self.K = K
        self.name = name
        self.eng = eng
        self.n = 0
        self.sems = []
        self.seen = {}

    def token(self, n):
        b = (n - 1) // SEM_CH
        while len(self.sems) <= b:
            self.sems.append(self.K.new_sem("e_%s_%d" % (self.name, len(self.sems))))
        return (self.sems[b], (n - 1) % SEM_CH + 1, self)


class Kern:
    def __init__(self, nc, stack):
        self.nc = nc
        self.stack = stack
        self.dma_tiles = []
        self.sem_pool = {"hw": [], "sw": []}
        self.sem_latest = {}
        self.phase_tiles = None
        self.pe = Eng(self, "pe", nc.tensor)
        self.act = Eng(self, "act", nc.scalar)
        self.dve = Eng(self, "dve", nc.vector)
        self.pool = Eng(self, "pool", nc.gpsimd)
        self.sp = Eng(self, "sp", nc.sync)
        self.engs = [self.pe, self.act, self.dve, self.pool, self.sp]
        self.nsem = 0
        self.nwait = 0
        self.uid = 0

    def new_sem(self, name):
        self.nsem += 1
        return self.stack.enter_context(self.nc.semaphore(name))

    def sbuf(self, name, shape, dt, dma=False, stack=None):
        self.uid += 1
        nm = "%s_%d" % (name, self.uid)
        h = (stack or self.stack).enter_context(self.nc.sbuf_tensor(nm, list(shape), dt))
        return Tile(self, h, nm, dma_sem=dma)

    def psum(self, name, shape, dt):
        h = self.stack.enter_context(self.nc.psum_tensor(name, list(shape), dt))
        t = Tile(self, h, name)
        t.is_psum = True
        return t

    def dram(self, name, shape, dt, kind="Internal", multi=False):
        h = self.nc.dram_tensor(name, list(shape), dt, kind=kind).ap()
        return Tile(self, h, name, multi=multi)

    def _wait(self, E, tok):
        sem, val, src = tok
        if src is E and E.name == "pe":
            return
        key = id(sem)
        if src is None:
            val = max(val, self.sem_latest.get(key, 0))
        if E.seen.get(key, 0) >= val:
            return
        E.eng.wait_ge(sem, val)
        self.nwait += 1
        E.seen[key] = val

    def _sync(self, E, reads, writes):
        for t in reads:
            if t.w is not None:
                self._wait(E, t.w)
            if t.wd:
                for tok in t.wd.values():
                    self._wait(E, tok)
            if t.is_psum:
                for k_, tok in t.r.items():
                    if k_ != E.name:
                        self._wait(E, tok)
        for t in writes:
            if t.w is not None:
                self._wait(E, t.w)
            if t.wd:
                for tok in t.wd.values():
                    self._wait(E, tok)
            for tok in t.r.values():
                self._wait(E, tok)

    def op(self, E, fn, reads=(), writes=()):
        self._sync(E, reads, writes)
        inst = fn()
        E.n += 1
        tok = E.token(E.n)
        inst.then_inc(tok[0], 1)
        for t in reads:
            t.r[E.name] = tok
        for t in writes:
            t.w = tok
            t.r = {}
            if t.wd is not None:
                t.wd = {}
        return inst

    def phase(self):
        return _Phase(self)

    def dma(self, Q, out_ap, in_ap, reads, writes, semtile, tag=None):
        import os
        if tag is not None and tag in os.environ.get("KDIS", "").split(","):
            return None
        self._sync(Q, reads, writes)
        inst = Q.eng.dma_start(out=out_ap, in_=in_ap)
        kind = "sw" if Q is self.pool else "hw"
        ent = semtile.dsem.get(kind)
        if ent is None or ent[1] + 16 > SEM_CH:
            if ent is None and self.sem_pool[kind]:
                ent = list(self.sem_pool[kind].pop())
            else:
                ent = [self.new_sem("d%s_%s" % (kind, semtile.name)), 0]
            semtile.dsem[kind] = ent
        ent[1] += 16
        inst.then_inc(ent[0], 16)
        self.sem_latest[id(ent[0])] = ent[1]
        tok = (ent[0], ent[1], None)
        for t in reads:
            t.r["dma%d" % id(ent[0])] = tok
        for t in writes:
            if t.wd is not None:
                t.wd[id(ent[0])] = tok
            else:
                t.w = tok
            t.r = {}
        return inst

    def barrier(self):
        toks = []
        for E in self.engs:
            if E.n > 0:
                toks.append(E.token(E.n))
        for t in self.dma_tiles:
            for ent in t.dsem.values():
                toks.append((ent[0], ent[1], None))
        for E in self.engs:
            for tok in toks:
                if tok[2] is E:
                    continue
                self._wait(E, tok)

    def finish(self):
        for t in self.dma_tiles:
            for ent in t.dsem.values():
                self._wait(self.sp, (ent[0], ent[1], None))


class _Phase:
    def __init__(self, K):
        self.K = K

    def __enter__(self):
        self.stack = ExitStack()
        self.stack.__enter__()
        self.K.phase_tiles = []
        return self.stack

    def __exit__(self, *a):
        K = self.K
        K.barrier()
        for t in K.phase_tiles:
            K.dma_tiles.remove(t)
            for kind, ent in t.dsem.items():
                if ent[1] < 20000:
                    K.sem_pool[kind].append((ent[0], ent[1]))
        K.phase_tiles = None
        return self.stack.__exit__(*a)


def build(cfg):
    NP, SEQ, PAST = cfg
    NT_P = SEQ // 256
    KMAX = max(SEQ, PAST + 64)
    NKT_PAST = PAST // 128
    nc = bass.Bass("TRN2", target_bir_lowering=False)
    st = ExitStack()
    with st:
        K = Kern(nc, st)
        PE, ACT, DVE, POOL, SP = K.pe, K.act, K.dve, K.pool, K.sp

        def din(name, shape):
            return K.dram(name, shape, F32, "ExternalInput")

        def dout(name, shape):
            return K.dram(name, shape, F32, "ExternalOutput")

        x_p = din("x_p", [NP, SEQ, D]); x_s = din("x_s", [4, 64, D])
        st_ssm = din("st_ssm", [2, 4, 2048, 128]); st_conv = din("st_conv", [2, 4, 3, 3072])
        c_k = din("c_k", [4, PAST, D]); c_v = din("c_v", [4, PAST, D]); c_f = din("c_f", [4, PAST, 16])
        w_in = din("w_in", [2, D, 5152]); convp = din("convp", [2, 128, 120])
        dtb = din("dtb", [2, 32]); alog = din("alog", [2, 32]); dsk = din("dsk", [2, 32])
        normw = din("normw", [2, 128, 16]); w_out = din("w_out", [2, 2048, D])
        kv_wkv = din("kv_wkv", [D, 2048]); kv_wf = din("kv_wf", [D, 16]); kvb = din("kvb", [16])
        w_qg = din("w_qg", [2, D, 2048]); w_o = din("w_o", [2, D, D])
        pwq = din("pwq", [4, D, 2048]); skT = din("skT", [4, 2, 128, 128])
        import os
        NEXP = int(os.environ.get("KEXP", "16384"))
        uT = din("uT", [4, D, NEXP]); vt = din("vt", [4, NEXP, D])
        lng = din("lng", [4, 2, D]); lnb = din("lnb", [4, 2, D])
        cons_d = din("consts", [128, NCON * 128])

        y_p = dout("y_p", [NP, SEQ, D]); y_s = dout("y_s", [4, 64, D])
        o_pssm = dout("o_pssm", [2, NP, 2048, 128]); o_pconv = dout("o_pconv", [2, NP, 3, 3072])
        o_pk = dout("o_pk", [NP, SEQ, D]); o_pv = dout("o_pv", [NP, SEQ, D]); o_pf = dout("o_pf", [NP, SEQ, 16])
        o_sssm = dout("o_sssm", [2, 4, 2048, 128]); o_sconv = dout("o_sconv", [2, 4, 3, 3072])
        o_sk = dout("o_sk", [4, 64, D]); o_sv = dout("o_sv", [4, 64, D]); o_sf = dout("o_sf", [4, 64, 16])

        NSEQ = NP + 4
        NCGT = NEXP // 512
        uTb = [K.dram("uTb%d" % l, [NCGT, 128, 4096], BF16, multi=True) for l in range(4)]
        vtb = [K.dram("vtb%d" % l, [NCGT, 128, 4096], BF16, multi=True) for l in range(4)]
        KTscr = [K.dram("ktscr%d" % i, [8, 128, KMAX], BF16, multi=True) for i in range(NSEQ)]
        Vscr = [K.dram("vscr%d" % i, [KMAX + 128, 1056], BF16, multi=True) for i in range(NSEQ)]

        ps = [K.psum("ps%d" % i, [128, 512], F32) for i in range(8)]
        con = K.sbuf("con", [128, NCON, 128], F32, dma=True)
        conb = K.sbuf("conb", [128, NCON, 128], BF16)
        C_ID, C_TRI, C_TRI2, C_ONE, C_ONE2, C_IOTA, C_TSU, C_MS0, C_MS1, C_CM0, C_CM1 = range(NCON)
        x = K.sbuf("x", [128, 2, D], F32, dma=True)
        xb = K.sbuf("xb", [128, 2, D], BF16)
        xT = K.sbuf("xT", [128, 8, 256], BF16)
        junk = K.sbuf("junk", [128, D], BF16)
        gb = K.sbuf("gb", [128, 2, D], F32, dma=True)
        st4 = K.sbuf("st4", [128, 16], F32)
        NSLOT = 4
        wsl = [K.sbuf("wsl%d" % i, [128, 4096], BF16, dma=True) for i in range(NSLOT)]
        wsl_i = [0]
        cvp = K.sbuf("cvp", [128, 2, 120], F32, dma=True)
        nwp = K.sbuf("nwp", [128, 2, 16], F32, dma=True)
        dtb_s = K.sbuf("dtb_s", [128, 2, 32], F32, dma=True)
        aneg = K.sbuf("aneg", [128, 2, 32], F32, dma=True)
        dsk_s = K.sbuf("dsk_s", [128, 2, 32], F32, dma=True)
        kvb_s = K.sbuf("kvb_s", [128, 16], F32, dma=True)
        hTp = [K.sbuf("hTp%d" % l, [128, 2048], F32) for l in range(2)]
        cst = [K.sbuf("cst%d" % l, [128, 24, 4, 3], F32) for l in range(2)]
        SUF = K.sbuf("SUF", [128, 4, 16, 16], F32)
        AT0 = K.sbuf("AT0", [128, 1, 16, 16], F32)
        AT1 = K.sbuf("AT1", [128, 1, 16, 16], F32)
        NCS = K.sbuf("NCS", [128, 4, 16], F32)

        def wload(src_ap, view):
            s = wsl[wsl_i[0] % NSLOT]
            wsl_i[0] += 1
            v = view(s)
            K.dma(POOL, v, src_ap, [], [s], s)
            return s, v

        def v3(n1, n2):
            return lambda s: s[:, 0:n1 * n2].rearrange("p (a b) -> p a b", b=n2)

        pre = [K.sbuf("pre%d" % l, [128, 2], F32, dma=True) for l in range(4)]
        for l in range(4):
            for cg in range(NCGT):
                K.dma(POOL, uTb[l][cg, :, :].rearrange("p (a e) -> p a e", e=512),
                      uT[l, :, cg * 512:(cg + 1) * 512].rearrange("(a p) e -> p a e", p=128), [], [uTb[l]], pre[l])
                K.dma(POOL, vtb[l][cg, :, :].rearrange("p (a d) -> p a d", d=1024),
                      vt[l, cg * 512:(cg + 1) * 512, :].rearrange("(a p) d -> p a d", p=128), [], [vtb[l]], pre[l])
        K.dma(SP, con[:], cons_d[:, :].rearrange("p (a b) -> p a b", b=128), [], [con], con)
        K.op(DVE, lambda: nc.vector.tensor_copy(out=conb[:], in_=con[:]), [con], [conb])
        K.dma(SP, cvp[:], convp[:, :, :].rearrange("l p f -> p l f"), [], [cvp], cvp)
        K.dma(SP, nwp[:], normw[:, :, :].rearrange("l p f -> p l f"), [], [nwp], nwp)
        for l in range(2):
            K.dma(SP, dtb_s[:, l, :], dtb[l, :].partition_broadcast(128), [], [dtb_s], dtb_s)
            K.dma(SP, aneg[:, l, :], alog[l, :].partition_broadcast(128), [], [aneg], aneg)
            K.dma(SP, dsk_s[:, l, :], dsk[l, :].partition_broadcast(128), [], [dsk_s], dsk_s)
        K.dma(SP, kvb_s[:], kvb[:].partition_broadcast(128), [], [kvb_s], kvb_s)
        K.op(ACT, lambda: nc.scalar.activation(out=aneg[:], in_=aneg[:], func=AF.Exp), [aneg], [aneg])
        K.op(DVE, lambda: nc.vector.tensor_scalar(out=aneg[:], in0=aneg[:], scalar1=-1.0, scalar2=None, op0=ALU.mult), [aneg], [aneg])

        def cF(i):
            return con[:, i, :]

        def cB(i):
            return conb[:, i, :]

        def mm(out, lhsT, rhs, start, stop, reads, writes):
            K.op(PE, lambda: nc.tensor.matmul(out, lhsT, rhs, start=start, stop=stop), reads, writes)

        def tr(out, in_, ident, reads, writes):
            K.op(PE, lambda: nc.tensor.transpose(out, in_, ident), reads, writes)

        def act(out, in_, func, reads, writes, **kw):
            K.op(ACT, lambda: nc.scalar.activation(out=out, in_=in_, func=func, **kw), reads, writes)

        def tt(out, in0, in1, op, reads, writes, E=None):
            E = E or DVE
            K.op(E, lambda: E.eng.tensor_tensor(out=out, in0=in0, in1=in1, op=op), reads, writes)

        def ts(out, in0, s1, s2, op0, op1, reads, writes):
            if s2 is None:
                K.op(DVE, lambda: nc.vector.tensor_scalar(out=out, in0=in0, scalar1=s1, scalar2=None, op0=op0), reads, writes)
            else:
                K.op(DVE, lambda: nc.vector.tensor_scalar(out=out, in0=in0, scalar1=s1, scalar2=s2, op0=op0, op1=op1), reads, writes)

        def stt(out, in0, scalar, in1, op0, op1, reads, writes):
            K.op(DVE, lambda: nc.vector.scalar_tensor_tensor(out=out, in0=in0, scalar=scalar, in1=in1, op0=op0, op1=op1), reads, writes)

        def cpy(E, out, in_, reads, writes):
            if E is ACT:
                K.op(ACT, lambda: nc.scalar.copy(out=out, in_=in_), reads, writes)
            else:
                K.op(E, lambda: E.eng.tensor_copy(out=out, in_=in_), reads, writes)

        def make_xT():
            for s in range(2):
                cpy(ACT, xb[:, s, :], x[:, s, :], [x], [xb])
                pb = ps[7 - s][:].bitcast(BF16)
                for dk in range(8):
                    tr(pb[:, dk * 128:(dk + 1) * 128], xb[:, s, dk * 128:(dk + 1) * 128], cB(C_ID), [xb, conb], [ps[7 - s]])
                cpy(DVE, xT[:, :, s * 128:(s + 1) * 128], pb[:, 0:1024].rearrange("p (a b) -> p a b", b=128), [ps[7 - s]], [xT])

        def deepnorm(s, banks, li, which):
            for hh in range(2):
                stt(x[:, s, hh * 512:(hh + 1) * 512], x[:, s, hh * 512:(hh + 1) * 512], ALPHA, banks[hh][:, :],
                    ALU.mult, ALU.add, [x, banks[hh]], [x])
            act(junk[:], x[:, s, :], AF.Identity, [x], [junk, st4], accum_out=st4[:, 0:1])
            act(junk[:], x[:, s, :], AF.Square, [x], [junk, st4], accum_out=st4[:, 1:2])
            ts(st4[:, 2:3], st4[:, 0:1], 1.0 / D, None, ALU.mult, None, [st4], [st4])
            tt(st4[:, 3:4], st4[:, 2:3], st4[:, 2:3], ALU.mult, [st4], [st4])
            stt(st4[:, 4:5], st4[:, 1:2], 1.0 / D, st4[:, 3:4], ALU.mult, ALU.subtract, [st4], [st4])
            act(st4[:, 7:8], st4[:, 4:5], AF.Sqrt, [st4], [st4], bias=LN_EPS)
            K.op(DVE, lambda: nc.vector.reciprocal(out=st4[:, 5:6], in_=st4[:, 7:8]), [st4], [st4])
            stt(st4[:, 6:7], st4[:, 2:3], -1.0, st4[:, 5:6], ALU.mult, ALU.mult, [st4], [st4])
            act(x[:, s, :], x[:, s, :], AF.Identity, [x, st4], [x], scale=st4[:, 5:6], bias=st4[:, 6:7])
            tt(x[:, s, :], x[:, s, :], gb[:, 0, :], ALU.mult, [x, gb], [x])
            tt(x[:, s, :], x[:, s, :], gb[:, 1, :], ALU.add, [x, gb], [x])

        def load_gb(li, which):
            K.dma(SP, gb[:, 0, :], lng[li, which, :].partition_broadcast(128), [], [gb], gb)
            K.dma(SP, gb[:, 1, :], lnb[li, which, :].partition_broadcast(128), [], [gb], gb)

        def mamba_phase(li, tl):
            segs = tl["segs"]
            nseg = len(segs)
            L = 256 // nseg
            nsub = 2 if nseg == 4 else 1
            TM = C_TRI if nsub == 1 else C_TRI2
            with K.phase() as ph:
                zs = K.sbuf("zs", [128, 2, 2048], BF16, stack=ph)
                xsT = K.sbuf("xsT", [128, 16, 256], BF16, stack=ph)
                BT = K.sbuf("BT", [128, 4, 256], BF16, stack=ph)
                CTt = K.sbuf("CT", [128, 4, 256], BF16, stack=ph)
                CTm = K.sbuf("CTm", [128, 2, 4, 128], BF16, stack=ph)
                ext = [K.sbuf("ext%d" % i, [128, 4, 67], F32, stack=ph) for i in range(2)]
                cvt = [K.sbuf("cvt%d" % i, [128, 4, 64], F32, stack=ph) for i in range(2)]
                lastT = K.sbuf("lastT", [128, 8, 12], BF16, stack=ph)
                cvout = [K.sbuf("cvout%d" % i, [12, 512], F32, dma=True, stack=ph) for i in range(2)]
                hb16 = [K.sbuf("hb16_%d" % i, [128, 512], BF16, stack=ph) for i in range(2)]
                dt = K.sbuf("dt", [128, 2, 32], F32, stack=ph)
                aa = K.sbuf("aa", [128, 2, 32], F32, stack=ph)
                acs = K.sbuf("acs", [128, 32], F32, stack=ph)
                eacs = K.sbuf("eacs", [128, 32], F32, stack=ph)
                xs_tm = K.sbuf("xs_tm", [128, 2048], BF16, stack=ph)
                xdt = K.sbuf("xdt", [128, 2048], BF16, stack=ph)
                xdtw = K.sbuf("xdtw", [128, 512], BF16, stack=ph)
                Btm = K.sbuf("Btm", [128, 4, 128], BF16, stack=ph)
                abc = K.sbuf("abc", [128, 8, 128], F32, stack=ph)
                t1 = K.sbuf("t1", [128, 8, 128], F32, stack=ph)
                Dm = K.sbuf("Dm", [128, 8, 128], BF16, stack=ph)
                MT = K.sbuf("MT", [128, 8, 128], BF16, stack=ph)
                CBm = K.sbuf("CBm", [128, 128], BF16, stack=ph)
                yt = K.sbuf("yt", [128, 2048], F32, stack=ph)
                ytmp = K.sbuf("ytmp", [128, 512], F32, stack=ph)
                sm = K.sbuf("sm", [128, 64], F32, stack=ph)
                yn = K.sbuf("yn", [128, 2048], BF16, stack=ph)
                ynT = K.sbuf("ynT", [128, 2, 16, 128], BF16, stack=ph)
                stg = K.sbuf("stg", [128, 2048], F32, dma=True, stack=ph)
                if nseg == 1:
                    hT = [hTp[li]]
                else:
                    hT = [K.sbuf("hTs%d" % i, [128, 2048], F32, stack=ph) for i in range(2)]
                    cl = K.sbuf("cl", [12, 1536], F32, dma=True, stack=ph)
                    for hf in range(2):
                        K.dma(SP, cl[:], st_conv[li, :, :, hf * 1536:(hf + 1) * 1536].rearrange("b r c -> (b r) c"), [], [cl], cl)
                        for c in range(12):
                            tr(ps[hf][:, c * 12:c * 12 + 12], cl[:, c * 128:(c + 1) * 128], con[0:12, C_ID, 0:12], [cl, con], [ps[hf]])
                        cpy(DVE, cst[li][:, hf * 12:(hf + 1) * 12, :, :], ps[hf][:, 0:144].rearrange("p (c s r) -> p c s r", s=4, r=3), [ps[hf]], [cst[li]])

                def load_state(h_, src):
                    K.dma(SP, stg[:].rearrange("p (c n) -> p c n", n=128), src.rearrange("(c p) n -> p c n", p=128), [], [stg], stg)
                    for q4 in range(4):
                        for q in range(4):
                            cq = q4 * 4 + q
                            tr(ps[4 + q4][:, q * 128:(q + 1) * 128], stg[:, cq * 128:(cq + 1) * 128], cF(C_ID), [stg, con], [ps[4 + q4]])
                        cpy(DVE if q4 % 2 else ACT, h_[:, q4 * 512:(q4 + 1) * 512], ps[4 + q4][:, :], [ps[4 + q4]], [h_])

                def store_state(h_, dst):
                    for q4 in range(4):
                        for q in range(4):
                            cq = q4 * 4 + q
                            tr(ps[q4][:, q * 128:(q + 1) * 128], h_[:, cq * 128:(cq + 1) * 128], cF(C_ID), [h_, con], [ps[q4]])
                        cpy(ACT if q4 % 2 else DVE, stg[:, q4 * 512:(q4 + 1) * 512], ps[q4][:, :], [ps[q4]], [stg])
                    K.dma(SP, dst.rearrange("(c p) n -> p c n", p=128), stg[:].rearrange("p (c n) -> p c n", n=128), [stg], [], stg)

                is_last = tl["last"]
                if is_last:
                    cpy(DVE, lastT[:, :, 0:3 * nseg].rearrange("p a (s r) -> p a s r", r=3),
                        xT[:, :, :].rearrange("p a (s l) -> p a s l", l=L)[:, :, :, L - 3:L], [xT], [lastT])
                for cg in range(4):
                    wt, wv = wload(w_in[li, :, cg * 512:(cg + 1) * 512].rearrange("(a p) c -> p a c", p=128), v3(8, 512))
                    for s in range(2):
                        bank = ps[s]
                        for dk in range(8):
                            mm(bank[:, :], xT[:, dk, s * 128:(s + 1) * 128], wv[:, dk, :], dk == 0, dk == 7, [xT, wt], [bank])
                        act(zs[:, s, cg * 512:(cg + 1) * 512], bank[:, :], AF.Silu, [bank], [zs])
                for g in range(6):
                    wt, wv = wload(w_in[li, :, 2048 + g * 512:2048 + (g + 1) * 512].rearrange("(a p) c -> p a c", p=128), v3(8, 512))
                    for cc in range(4):
                        c = g * 4 + cc
                        bank = ps[2 + (c % 2)]
                        for dk in range(8):
                            mm(bank[:, 0:256], wv[:, dk, cc * 128:(cc + 1) * 128], xT[:, dk, :], dk == 0, dk == 7, [xT, wt], [bank])
                        e = ext[c % 2]
                        cv = cvt[c % 2]
                        ev = e[:, 0:nseg, 0:3 + L] if nseg == 4 else e[:, :, :].rearrange("p a b -> p (a b)")[:, 0:259].unsqueeze(1)
                        cvv = cv[:, 0:nseg, 0:L] if nseg == 4 else cv[:, :, :].rearrange("p a b -> p (a b)").unsqueeze(1)
                        cpy(ACT, ev[:, :, 3:3 + L], bank[:, 0:256].rearrange("p (s l) -> p s l", l=L), [bank], [e])
                        cpy(DVE, ev[:, :, 0:3], cst[li][:, c, 0:nseg, :], [cst[li]], [e])
                        wcol = lambda k: cvp[:, li, c * 5 + k:c * 5 + k + 1]
                        ts(cvv, ev[:, :, 0:L], wcol(0), wcol(4), ALU.mult, ALU.add, [e, cvp], [cv])
                        for k in range(1, 4):
                            stt(cvv, ev[:, :, k:k + L], wcol(k), cvv, ALU.mult, ALU.add, [e, cvp, cv], [cv])
                        cpy(DVE, cst[li][:, c, 0:nseg, :], ev[:, :, L:L + 3], [e], [cst[li]])
                        if c < 16:
                            dst, dt_ = xsT[:, c, :], xsT
                        elif c < 20:
                            dst, dt_ = BT[:, c - 16, :], BT
                        else:
                            dst, dt_ = CTt[:, c - 20, :], CTt
                        act(dst.rearrange("p (s l) -> p s l", l=L), cvv, AF.Silu, [cv], [dt_])
                    if is_last:
                        bank = ps[4]
                        for dk in range(8):
                            mm(bank[0:3 * nseg, :], lastT[:, dk, 0:3 * nseg], wv[:, dk, :], dk == 0, dk == 7, [lastT, wt], [bank])
                        co = cvout[g % 2]
                        cpy(ACT, co[0:3 * nseg, :], bank[0:3 * nseg, :], [bank], [co])
                        if tl["kind"] == "p":
                            K.dma(SP, o_pconv[li, tl["seq"], :, g * 512:(g + 1) * 512], co[0:3, :], [co], [], co)
                        else:
                            K.dma(SP, o_sconv[li, :, :, g * 512:(g + 1) * 512].rearrange("b r c -> (b r) c"), co[0:12, :], [co], [], co)
                wt, wv = wload(w_in[li, :, 5120:5152].rearrange("(a p) c -> p a c", p=128), v3(8, 32))
                for s in range(2):
                    bank = ps[4 + s]
                    for dk in range(8):
                        mm(bank[:, 0:32], xT[:, dk, s * 128:(s + 1) * 128], wv[:, dk, :], dk == 0, dk == 7, [xT, wt], [bank])
                    tt(dt[:, s, :], bank[:, 0:32], dtb_s[:, li, :], ALU.add, [bank, dtb_s], [dt])
                act(dt[:], dt[:], AF.Exp, [dt], [dt])
                act(dt[:], dt[:], AF.Ln, [dt], [dt], bias=1.0)
                tt(aa[:], dt[:], aneg[:, li, :].unsqueeze(1).to_broadcast([128, 2, 32]), ALU.mult, [dt, aneg], [aa])
                if nsub == 2:
                    cpy(DVE, CTm[:].rearrange("p a b c -> p (a b c)"), conb[:, C_CM0, 64:65].to_broadcast([128, 1024]), [conb], [CTm])

                for s in range(2):
                    cols = slice(s * 128, (s + 1) * 128)
                    sub_segs = [0] if nsub == 1 else [0, 1]
                    if nsub == 2:
                        for si in range(2):
                            load_state(hT[si], st_ssm[li, 2 * s + si, :, :])
                    prs = [slice(0, 128)] if nsub == 1 else [slice(0, 64), slice(64, 128)]
                    lastc = [127] if nsub == 1 else [63, 127]
                    for half in range(2):
                        pb = ps[half][:].bitcast(BF16)
                        for q in range(8):
                            tr(pb[:, q * 128:(q + 1) * 128], xsT[:, half * 8 + q, cols], cB(C_ID), [xsT, conb], [ps[half]])
                        cpy(ACT, xs_tm[:, half * 1024:(half + 1) * 1024], pb[:, 0:1024], [ps[half]], [xs_tm])
                    pb = ps[2][:].bitcast(BF16)
                    for g in range(4):
                        tr(pb[:, g * 128:(g + 1) * 128], BT[:, g, cols], cB(C_ID), [BT, conb], [ps[2]])
                    cpy(ACT, Btm[:].rearrange("p a b -> p (a b)"), pb[:, 0:512], [ps[2]], [Btm])
                    tt(xdt[:].rearrange("p (r q) -> p r q", q=64), xs_tm[:].rearrange("p (r q) -> p r q", q=64),
                       dt[:, s, :].unsqueeze(2).to_broadcast([128, 32, 64]), ALU.mult, [xs_tm, dt], [xdt])
                    mm(ps[3][:, 0:32], cF(TM), aa[:, s, :], True, True, [con, aa], [ps[3]])
                    cpy(DVE, acs[:], ps[3][:, 0:32], [ps[3]], [acs])
                    act(eacs[:], ps[3][:, 0:32], AF.Exp, [ps[3]], [eacs])
                    if nsub == 2:
                        for sg in range(2):
                            cpy(DVE, CTm[:, sg, :, sg * 64:(sg + 1) * 64], CTt[:, :, s * 128 + sg * 64:s * 128 + (sg + 1) * 64], [CTt], [CTm])
                    for g in range(4):
                        hs = slice(g * 8, (g + 1) * 8)
                        cpy(DVE, abc[:], aa[:, s, hs].unsqueeze(2).to_broadcast([128, 8, 128]), [aa], [abc])
                        for rr in range(8):
                            bank = ps[4 + rr // 4]
                            mm(bank[:, (rr % 4) * 128:(rr % 4 + 1) * 128], abc[:, rr, :], cF(TM), True, True, [abc, con], [bank])
                        for hb in range(2):
                            tt(t1[:, hb * 4:(hb + 1) * 4, :], ps[4 + hb][:, :].rearrange("p (r l) -> p r l", l=128),
                               acs[:, g * 8 + hb * 4:g * 8 + hb * 4 + 4].unsqueeze(2).to_broadcast([128, 4, 128]), ALU.subtract,
                               [ps[4 + hb], acs], [t1])
                        ts(t1[:], t1[:], 0.0, None, ALU.min, None, [t1], [t1])
                        act(Dm[:], t1[:], AF.Exp, [t1], [Dm])
                        mm(ps[6][:, 0:128], BT[:, g, cols], CTt[:, g, cols], True, True, [BT, CTt], [ps[6]])
                        tt(CBm[:], ps[6][:, 0:128], cF(TM), ALU.mult, [ps[6], con], [CBm])
                        tt(MT[:], Dm[:], CBm[:].unsqueeze(1).to_broadcast([128, 8, 128]), ALU.mult, [Dm, CBm], [MT])
                        for si, sgslot in enumerate(sub_segs):
                            pr = prs[si]
                            lc = lastc[si]
                            for hb in range(2):
                                rowv = ps[4 + hb][:, :].rearrange("p (r l) -> p r l", l=128)
                                tt(sm[pr, hb * 4:(hb + 1) * 4], rowv[pr, :, lc], acs[pr, g * 8 + hb * 4:g * 8 + hb * 4 + 4], ALU.subtract,
                                   [ps[4 + hb], acs], [sm])
                                act(sm[:, 16 + si * 8 + hb * 4:16 + si * 8 + (hb + 1) * 4], rowv[:, :, lc], AF.Exp, [ps[4 + hb]], [sm])
                        act(sm[:, 8:16], sm[:, 0:8], AF.Exp, [sm], [sm])
                        for rr in range(8):
                            r = g * 8 + rr
                            mm(ps[7][:, rr * 64:(rr + 1) * 64], MT[:, rr, :], xdt[:, r * 64:(r + 1) * 64], True, True, [MT, xdt], [ps[7]])
                        for si, sgslot in enumerate(sub_segs):
                            lhs = CTt[:, g, cols] if nsub == 1 else CTm[:, si, g, :]
                            hb_ = hb16[si]
                            cpy(ACT, hb_[:], hT[sgslot][:, g * 512:(g + 1) * 512], [hT[sgslot]], [hb_])
                            mm(ps[6][:, :], lhs, hb_[:], si == 0, si == len(sub_segs) - 1,
                               [CTt, CTm, hb_], [ps[6]])
                        tt(ytmp[:].rearrange("p (r q) -> p r q", q=64), ps[6][:, :].rearrange("p (r q) -> p r q", q=64),
                           eacs[:, hs].unsqueeze(2).to_broadcast([128, 8, 64]), ALU.mult, [ps[6], eacs], [ytmp])
                        tt(yt[:, g * 512:(g + 1) * 512], ytmp[:], ps[7][:, :], ALU.add, [ytmp, ps[7]], [yt])
                        tt(ytmp[:].rearrange("p (r q) -> p r q", q=64), xs_tm[:, g * 512:(g + 1) * 512].rearrange("p (r q) -> p r q", q=64),
                           dsk_s[:, li, hs].unsqueeze(2).to_broadcast([128, 8, 64]), ALU.mult, [xs_tm, dsk_s], [ytmp])
                        tt(yt[:, g * 512:(g + 1) * 512], yt[:, g * 512:(g + 1) * 512], ytmp[:], ALU.add, [yt, ytmp], [yt])
                        tt(xdtw[:].rearrange("p (r q) -> p r q", q=64), xdt[:, g * 512:(g + 1) * 512].rearrange("p (r q) -> p r q", q=64),
                           sm[:, 8:16].unsqueeze(2).to_broadcast([128, 8, 64]), ALU.mult, [xdt, sm], [xdtw])
                        for si, sgslot in enumerate(sub_segs):
                            pr = prs[si]
                            h_ = hT[sgslot]
                            mm(ps[6][:, :], Btm[pr, g, :], xdtw[pr, :], True, True, [Btm, xdtw], [ps[6]])
                            tt(ytmp[:].rearrange("p (r q) -> p r q", q=64), h_[:, g * 512:(g + 1) * 512].rearrange("p (r q) -> p r q", q=64),
                               sm[:, 16 + si * 8:24 + si * 8].unsqueeze(2).to_broadcast([128, 8, 64]), ALU.mult, [h_, sm], [ytmp])
                            tt(h_[:, g * 512:(g + 1) * 512], ytmp[:], ps[6][:, :], ALU.add, [ytmp, ps[6]], [h_])
                    tt(yt[:], yt[:], zs[:, s, :], ALU.mult, [yt, zs], [yt])
                    for g in range(4):
                        act(junk[:, 0:512], yt[:, g * 512:(g + 1) * 512], AF.Square, [yt], [junk, sm], accum_out=sm[:, 32 + g:33 + g])
                    act(sm[:, 40:44], sm[:, 32:36], AF.Sqrt, [sm], [sm], scale=1.0 / 512, bias=RMS_EPS)
                    K.op(DVE, lambda: nc.vector.reciprocal(out=sm[:, 36:40], in_=sm[:, 40:44]), [sm], [sm])
                    for g in range(4):
                        act(yn[:, g * 512:(g + 1) * 512], yt[:, g * 512:(g + 1) * 512], AF.Copy, [yt, sm], [yn], scale=sm[:, 36 + g:37 + g])
                    for half in range(2):
                        pb = ps[half][:].bitcast(BF16)
                        for q in range(8):
                            tr(pb[:, q * 128:(q + 1) * 128], yn[:, (half * 8 + q) * 128:(half * 8 + q + 1) * 128], cB(C_ID), [yn, conb], [ps[half]])
                        tt(ynT[:, s, half * 8:(half + 1) * 8, :], pb[:, 0:1024].rearrange("p (a b) -> p a b", b=128),
                           nwp[:, li, half * 8:(half + 1) * 8].unsqueeze(2).to_broadcast([128, 8, 128]), ALU.mult, [ps[half], nwp], [ynT])
                    if nsub == 2:
                        for si in range(2):
                            store_state(hT[si], o_sssm[li, 2 * s + si, :, :])
                if is_last and nsub == 1:
                    store_state(hT[0], o_pssm[li, tl["seq"], :, :])
                load_gb(li, 0)
                banks = [ps[0], ps[1], ps[2], ps[3]]
                for j in range(4):
                    wt, wv = wload(w_out[li, j * 512:(j + 1) * 512, :].rearrange("(a p) c -> p a c", p=128), v3(4, 1024))
                    for cc in range(4):
                        cq = j * 4 + cc
                        for s in range(2):
                            for hh in range(2):
                                mm(banks[s * 2 + hh][:, :], ynT[:, s, cq, :], wv[:, cc, hh * 512:(hh + 1) * 512], cq == 0, cq == 15,
                                   [ynT, wt], [banks[s * 2 + hh]])
                for s in range(2):
                    deepnorm(s, banks[s * 2:s * 2 + 2], li, 0)
            make_xT()

        import os
        pstop = int(os.environ.get("PSTOP", "99"))

        def peer_phase(li):
            with K.phase() as ph:
              for _once in (0,):
                    GT = K.sbuf("GT", [128, 128, 256], BF16, stack=ph)
                    qT = K.sbuf("qT", [128, 16, 256], BF16, stack=ph)
                    skb = K.sbuf("skb", [128, 2, 128], BF16, dma=True, stack=ph)
                    ssb = K.sbuf("ssb", [128, 16, 128], F32, stack=ph)
                    tmp = K.sbuf("tmp", [128, 256], F32, stack=ph)
                    sv = K.sbuf("sv", [128, 16, 16], F32, stack=ph)
                    siu = K.sbuf("siu", [128, 8, 16], U32, stack=ph)
                    sif = K.sbuf("sif", [128, 8, 16], F32, stack=ph)
                    cand = K.sbuf("cand", [128, 8, 256], F32, stack=ph)
                    best = K.sbuf("best", [128, 8, 16], F32, stack=ph)
                    eb = K.sbuf("eb", [128, 8, 16], F32, stack=ph)
                    zz = K.sbuf("zz", [128, 32], F32, stack=ph)
                    bia = K.sbuf("bia", [128, 8, 16], F32, stack=ph)
                    thr = K.sbuf("thr", [128, 8, 16], F32, stack=ph)
                    biaT = K.sbuf("biaT", [128, 256], F32, stack=ph)
                    thrT = K.sbuf("thrT", [128, 256], F32, stack=ph)
                    idxT = K.sbuf("idxT", [128, 256], F32, stack=ph)
                    Qrep = [K.sbuf("Qrep%d" % i, [128, 8, 128], BF16, stack=ph) for i in range(2)]
                    OH = [K.sbuf("OH%d" % i, [128, 8, 128], BF16, stack=ph) for i in range(2)]
                    et = [K.sbuf("et%d" % i, [128, 128], F32, stack=ph) for i in range(4)]
                    Rt = [K.sbuf("Rt%d" % i, [128, 128], BF16, stack=ph) for i in range(4)]
                    gh = [K.sbuf("gh%d" % i, [128, 256], BF16, stack=ph) for i in range(3)]
                    ATt = [K.sbuf("ATt%d" % i, [128, 256], BF16, stack=ph) for i in range(3)]

                    K.dma(POOL, skb[:], skT[li, :, :, :].rearrange("k d n -> d k n"), [], [skb], skb)
                    for cg in range(4):
                        wt, wv = wload(pwq[li, :, cg * 512:(cg + 1) * 512].rearrange("(a p) c -> p a c", p=128), v3(8, 512))
                        for cc in range(4):
                            oc = cg * 4 + cc
                            bank = ps[oc % 2]
                            for dk in range(8):
                                mm(bank[:, 0:256], wv[:, dk, cc * 128:(cc + 1) * 128], xT[:, dk, :], dk == 0, dk == 7, [xT, wt], [bank])
                            cpy(ACT if oc % 2 else DVE, qT[:, oc, :], bank[:, 0:256], [bank], [qT])
                    if pstop == 1:
                        break
                    for s in range(2):
                        cols = slice(s * 128, (s + 1) * 128)
                        for oc in range(16):
                            bank = ps[2 + oc // 4]
                            mm(bank[:, (oc % 4) * 128:(oc % 4 + 1) * 128], qT[:, oc, cols], skb[:, oc % 2, :], True, True, [qT, skb], [bank])
                        for q4 in range(4):
                            cpy(ACT, ssb[:, q4 * 4:(q4 + 1) * 4, :], ps[2 + q4][:, :].rearrange("p (a b) -> p a b", b=128), [ps[2 + q4]], [ssb])
                        for oc in range(16):
                            K.op(DVE, lambda: nc.vector.max(out=sv[:, oc, 0:8], in_=ssb[:, oc, :]), [ssb], [sv])
                            if oc % 2 == 0:
                                K.op(DVE, lambda: nc.vector.max_index(out=siu[:, oc // 2, 0:8], in_max=sv[:, oc, 0:8], in_values=ssb[:, oc, :]), [ssb, sv], [siu])
                            K.op(DVE, lambda: nc.vector.match_replace(out=tmp[:, 0:128], in_to_replace=sv[:, oc, 0:8], in_values=ssb[:, oc, :], imm_value=-1e30), [ssb, sv], [tmp])
                            K.op(DVE, lambda: nc.vector.max(out=sv[:, oc, 8:16], in_=tmp[:, 0:128]), [tmp], [sv])
                            if oc % 2 == 0:
                                K.op(DVE, lambda: nc.vector.max_index(out=siu[:, oc // 2, 8:16], in_max=sv[:, oc, 8:16], in_values=tmp[:, 0:128]), [tmp, sv], [siu])
                        cpy(DVE, sif[:], siu[:], [siu], [sif])
                        sv4 = sv[:].rearrange("p (h k) a -> p h k a", k=2)
                        tt(cand[:].rearrange("p h (a b) -> p h a b", b=16), sv4[:, :, 0, :].unsqueeze(3).to_broadcast([128, 8, 16, 16]),
                           sv4[:, :, 1, :].unsqueeze(2).to_broadcast([128, 8, 16, 16]), ALU.add, [sv], [cand])
                        for h in range(8):
                            K.op(DVE, lambda: nc.vector.max(out=best[:, h, 0:8], in_=cand[:, h, :]), [cand], [best])
                            K.op(DVE, lambda: nc.vector.match_replace(out=tmp[:], in_to_replace=best[:, h, 0:8], in_values=cand[:, h, :], imm_value=-1e30), [cand, best], [tmp])
                            K.op(DVE, lambda: nc.vector.max(out=best[:, h, 8:16], in_=tmp[:]), [tmp], [best])
                        tt(eb[:], best[:], best[:, :, 0:1].to_broadcast([128, 8, 16]), ALU.subtract, [best], [eb])
                        act(eb[:], eb[:], AF.Exp, [eb], [eb])
                        K.op(DVE, lambda: nc.vector.tensor_reduce(out=zz[:, 0:8], in_=eb[:], axis=AX.X, op=ALU.add), [eb], [zz])
                        act(zz[:, 8:16], zz[:, 0:8], AF.Ln, [zz], [zz])
                        tt(zz[:, 16:24], best[:, :, 0], zz[:, 8:16], ALU.add, [best, zz], [zz])
                        tt(bia[:], sv4[:, :, 0, :], zz[:, 16:24].unsqueeze(2).to_broadcast([128, 8, 16]), ALU.subtract, [sv, zz], [bia])
                        stt(thr[:], sv4[:, :, 0, :], -1.0, best[:, :, 15:16].to_broadcast([128, 8, 16]), ALU.mult, ALU.add, [sv, best], [thr])
                        ts(thr[:], thr[:], -2e-5, None, ALU.add, None, [thr], [thr])
                        for k_, (src, dstT) in enumerate(((bia, biaT), (thr, thrT), (sif, idxT))):
                            bank = ps[2 + k_]
                            tr(bank[:, 0:128], src[:].rearrange("p h a -> p (h a)"), cF(C_ID), [src, con], [bank])
                            cpy(ACT, dstT[:, cols], bank[:, 0:128], [bank], [dstT])
                    if pstop == 2:
                        break
                    LA = 3

                    def fill_batch(b):
                        t0 = b * 8
                        q_, o_ = Qrep[b % 2], OH[b % 2]
                        for h in range(8):
                            cpy(DVE, q_[:, :, h * 16:(h + 1) * 16], qT[:, 2 * h + 1, t0:t0 + 8].unsqueeze(2).to_broadcast([128, 8, 16]), [qT], [q_])
                        tt(o_[:], con[:, C_IOTA, :].unsqueeze(1).to_broadcast([128, 8, 128]),
                           idxT[:, t0:t0 + 8].unsqueeze(2).to_broadcast([128, 8, 128]), ALU.is_equal, [con, idxT], [o_])

                    def stage_a(t):
                        if t % 8 == 0:
                            fill_batch(t // 8)
                        q_ = Qrep[(t // 8) % 2]
                        abank = ps[t % 4]
                        mm(abank[:, 0:128], q_[:, t % 8, :], skb[:, 1, :], True, True, [q_, skb], [abank])

                    def stage_b(t):
                        abank = ps[t % 4]
                        o_ = OH[(t // 8) % 2]
                        gbank = ps[4 + (t // 4) % 2]
                        e_ = et[t % 4]
                        r_ = Rt[t % 4]
                        act(e_[:], abank[:, 0:128], AF.Exp, [abank, biaT], [e_], bias=biaT[:, t:t + 1])
                        stt(r_[:], abank[:, 0:128], thrT[:, t:t + 1], e_[:], ALU.is_ge, ALU.mult, [abank, thrT, e_], [r_])
                        mm(gbank[:, (t % 4) * 128:(t % 4 + 1) * 128], r_[:], o_[:, t % 8, :], True, True, [r_, o_], [gbank])
                        if t % 4 == 3:
                            cpy(ACT, GT[:, :, t - 3:t + 1], gbank[:, :].rearrange("p (t i) -> p i t", i=128), [gbank], [GT])

                    for t in range(LA):
                        stage_a(t)
                    for t in range(256):
                        if t + LA < 256:
                            stage_a(t + LA)
                        stage_b(t)
                    if pstop == 3:
                        break
                    load_gb(li, 1)
                    accb = [ps[0], ps[1], ps[2], ps[3]]
                    NCG = int(os.environ.get('NCG', '32'))
                    wts = {}

                    def stage_1(c):
                        cg, cc = c // 4, c % 4
                        if cc == 0:
                            ut = wsl[wsl_i[0] % NSLOT]
                            vt_ = wsl[(wsl_i[0] + 1) % NSLOT]
                            wsl_i[0] += 2
                            K.dma(SP, ut[:, :], uTb[li][cg, :, :], [uTb[li]], [ut], ut)
                            K.dma(SP, vt_[:, :], vtb[li][cg, :, :], [vtb[li]], [vt_], vt_)
                            uv = v3(8, 512)(ut)
                            vv = v3(4, 1024)(vt_)
                            wts[cg] = (ut, uv, vt_, vv)
                        ut, uv, vt_, vv = wts[cg]
                        hb = ps[4 + c % 3]
                        for dk in range(8):
                            mm(hb[:, 0:256], uv[:, dk, cc * 128:(cc + 1) * 128], xT[:, dk, :], dk == 0, dk == 7, [xT, ut], [hb])
                        g_ = gh[c % 3]
                        a_ = ATt[c % 3]
                        act(g_[:], hb[:, 0:256], AF.Gelu, [hb], [g_])
                        tt(a_[:], g_[:], GT[:, c, :], ALU.mult, [g_, GT], [a_])

                    def stage_2(c):
                        cg, cc = c // 4, c % 4
                        ut, uv, vt_, vv = wts[cg]
                        a_ = ATt[c % 3]
                        for s in range(2):
                            for hh in range(2):
                                mm(accb[s * 2 + hh][:, :], a_[:, s * 128:(s + 1) * 128], vv[:, cc, hh * 512:(hh + 1) * 512], c == 0, c == NCG * 4 - 1,
                                   [a_, vt_], [accb[s * 2 + hh]])

                    NCH = NCG * 4
                    stage_1(0)
                    stage_1(1)
                    for c in range(NCH):
                        if c + 2 < NCH:
                            stage_1(c + 2)
                        stage_2(c)
                    for s in range(2):
                        deepnorm(s, accb[s * 2:s * 2 + 2], li, 1)
            if pstop < 99:
                raise _Stop()

        kvstop = int(os.environ.get("KVSTOP", "99"))

        def kv_phase(tl):
            segs = tl["segs"]
            nseg = len(segs)
            kind = tl["kind"]
            with K.phase() as ph:
              for _once in (0,):
                    kst = [K.sbuf("kst%d" % i, [128, D], F32, dma=True, stack=ph) for i in range(2)]
                    vaug = [K.sbuf("vaug%d" % i, [128, 16, 66], BF16, dma=True, stack=ph) for i in range(2)]
                    ktile = [K.sbuf("ktile%d" % i, [128, 256], BF16, dma=True, stack=ph) for i in range(2)]
                    lf = K.sbuf("lf", [128, 2, 16], F32, dma=True, stack=ph)
                    tb = K.sbuf("tb", [128, 16], F32, stack=ph)
                    pk = K.sbuf("pk", [128, D], BF16, dma=True, stack=ph)
                    pf = K.sbuf("pf", [128, 16], F32, dma=True, stack=ph)
                    rsum = K.sbuf("rsum", [128, 16], F32, stack=ph)
                    KVX = os.environ.get("KVX", "")
                    for i in range(2):
                      if "a" not in KVX:
                        cpy(DVE, vaug[i][:, :, 64:66], conb[:, C_ONE, 0:32].rearrange("p (h q) -> p h q", q=2), [conb], [vaug[i]])

                    def tok_rows(dst, s):
                        if kind == "p":
                            b, T = tl["seq"], tl["T"]
                            return dst[b, T * 256 + s * 128:T * 256 + (s + 1) * 128, :]
                        return dst[2 * s:2 * s + 2, :, :].rearrange("b l f -> (b l) f")

                    ok, ov, of = (o_pk, o_pv, o_pf) if kind == "p" else (o_sk, o_sv, o_sf)
                    cnt = 0
                    for part in range(2):
                        for cgp in range(2):
                            cg = part * 2 + cgp
                            wt, wv = wload(kv_wkv[:, cg * 512:(cg + 1) * 512].rearrange("(a p) c -> p a c", p=128), v3(8, 512))
                            for s in range(2):
                                bank = ps[s]
                                for dk in range(8):
                                    mm(bank[:, :], xT[:, dk, s * 128:(s + 1) * 128], wv[:, dk, :], dk == 0, dk == 7, [xT, wt], [bank])
                                kk = kst[s]
                                if "b" not in KVX:
                                    cpy(ACT, kk[:, cgp * 512:(cgp + 1) * 512], bank[:, :], [bank], [kk])
                                if part == 1 and "c" not in KVX:
                                    cpy(DVE, vaug[s][:, cgp * 8:(cgp + 1) * 8, 0:64], bank[:, :].rearrange("p (h q) -> p h q", q=64), [bank], [vaug[s]])
                                if cgp == 1:
                                    K.dma(SP, tok_rows(ok if part == 0 else ov, s), kk[:], [kk], [], kk, tag="ok")
                            if part == 0 and "d" not in KVX:
                                for cc in range(4):
                                    c = cgp * 4 + cc
                                    bank = ps[2 + c % 2]
                                    for dk in range(8):
                                        mm(bank[:, 0:256], wv[:, dk, cc * 128:(cc + 1) * 128], xT[:, dk, :], dk == 0, dk == 7, [xT, wt], [bank])
                                    kt_ = ktile[c % 2]
                                    cpy(ACT if c % 2 else DVE, kt_[:], bank[:, 0:256], [bank], [kt_])
                                    if kind == "p":
                                        sl = tl["slot"]
                                        K.dma(SP, KTscr[sl][c, :, tl["T"] * 256:(tl["T"] + 1) * 256], kt_[:], [kt_], [KTscr[sl]], kt_, tag="kt")
                                    else:
                                        for i in range(4):
                                            K.dma(SP, KTscr[NP + i][c, :, PAST:PAST + 64], kt_[:, i * 64:(i + 1) * 64], [kt_], [KTscr[NP + i]], kt_)
                    if kvstop == 1:
                        break
                    for s in range(2):
                        if kind == "p":
                            sl = tl["slot"]
                            r0 = tl["T"] * 256 + s * 128
                            K.dma(SP, Vscr[sl][r0:r0 + 128, :], vaug[s][:].rearrange("p h q -> p (h q)"), [vaug[s]], [Vscr[sl]], vaug[s], tag="vs")
                        else:
                            for sg in range(2):
                                K.dma(SP, Vscr[NP + 2 * s + sg][PAST:PAST + 64, :], vaug[s][sg * 64:(sg + 1) * 64, :, :].rearrange("p h q -> p (h q)"),
                                      [vaug[s]], [Vscr[NP + 2 * s + sg]], vaug[s])
                    if kvstop == 2:
                        break
                    wf32 = K.sbuf("wf32", [128, 8, 16], F32, dma=True, stack=ph)
                    wt = K.sbuf("wfb", [128, 8, 16], BF16, stack=ph)
                    K.dma(SP, wf32[:], kv_wf[:, :].rearrange("(a p) c -> p a c", p=128), [], [wf32], wf32)
                    cpy(DVE, wt[:], wf32[:], [wf32], [wt])
                    wv = wt[:]
                    for s in range(2):
                        bank = ps[4 + s]
                        for dk in range(8):
                            mm(bank[:, 0:16], xT[:, dk, s * 128:(s + 1) * 128], wv[:, dk, :], dk == 0, dk == 7, [xT, wt], [bank])
                        tt(lf[:, s, :], bank[:, 0:16], kvb_s[:], ALU.add, [bank, kvb_s], [lf])
                    act(lf[:], lf[:], AF.Exp, [lf], [lf], scale=-1.0)
                    act(lf[:], lf[:], AF.Ln, [lf], [lf], bias=1.0)
                    ts(lf[:], lf[:], -1.0, None, ALU.mult, None, [lf], [lf])
                    for s in range(2):
                        K.dma(SP, tok_rows(of, s), lf[:, s, :], [lf], [], lf, tag="lf")
                    if kvstop == 3:
                        break
                    if kind == "p":
                        sl = 0
                        T = tl["T"]
                        if T == 0:
                            K.op(DVE, lambda: nc.vector.memset(SUF[:, 0, :, :], 0.0), [], [SUF])
                        for s in range(2):
                            j = T * 2 + s
                            mm(ps[6][:, 0:16], cF(C_TSU), lf[:, s, :], True, True, [con, lf], [ps[6]])
                            mm(ps[6][:, 16:32], cF(C_ONE), lf[:, s, :], True, True, [con, lf], [ps[6]])
                            mm(ps[6][:, 32:48], cF(C_TRI), lf[:, s, :], True, True, [con, lf], [ps[6]])
                            AT = AT0 if s == 0 else AT1
                            if j > 0:
                                cpy(DVE, AT[:, 0, 0:j, :], SUF[:, 0, 0:j, :], [SUF], [AT])
                            ts(AT[:, 0, j, :], ps[6][:, 32:48], -1.0, None, ALU.mult, None, [ps[6]], [AT])
                            if j > 0:
                                tt(SUF[:, 0, 0:j, :], SUF[:, 0, 0:j, :], ps[6][:, 16:32].unsqueeze(1).to_broadcast([128, j, 16]), ALU.add, [SUF, ps[6]], [SUF])
                            cpy(DVE, SUF[:, 0, j, :], ps[6][:, 0:16], [ps[6]], [SUF])
                    else:
                        for i in range(4):
                            s, sg = i // 2, i % 2
                            pr = slice(sg * 64, (sg + 1) * 64)
                            mm(ps[6][0:64, i * 16:(i + 1) * 16], con[pr, C_TRI, sg * 64:(sg + 1) * 64], lf[pr, s, :], True, True, [con, lf], [ps[6]])
                        ts(NCS[0:64, :, :].rearrange("p a b -> p (a b)"), ps[6][0:64, 0:64], -1.0, None, ALU.mult, None, [ps[6]], [NCS])
                        for i in range(4):
                            K.op(DVE, lambda: nc.vector.memset(rsum[:], 0.0), [], [rsum])
                            for j in range(NKT_PAST - 1, -1, -1):
                                K.dma(SP, pf[:], c_f[i, j * 128:(j + 1) * 128, :], [], [pf], pf)
                                mm(ps[6][:, 0:16], cF(C_TSU), pf[:], True, False, [con, pf], [ps[6]])
                                mm(ps[6][:, 0:16], cF(C_ONE), rsum[:], False, True, [con, rsum], [ps[6]])
                                cpy(DVE, SUF[:, i, j, :], ps[6][:, 0:16], [ps[6]], [SUF])
                                tt(rsum[:], rsum[:], pf[:], ALU.add, [rsum, pf], [rsum])
                            for j in range(NKT_PAST):
                                K.dma(POOL, pk[:], c_k[i, j * 128:(j + 1) * 128, :], [], [pk], pk)
                                for half in range(1):
                                    pb = ps[j % 2][:].bitcast(BF16)
                                    for q in range(8):
                                        tr(pb[:, q * 128:(q + 1) * 128], pk[:, q * 128:(q + 1) * 128], cB(C_ID), [pk, conb], [ps[j % 2]])
                                    kt_ = kst[j % 2]
                                    ktb = kt_[:].bitcast(BF16)
                                    cpy(ACT if j % 2 else DVE, ktb[:, 0:1024], pb[:, 0:1024], [ps[j % 2]], [kt_])
                                    K.dma(SP, KTscr[NP + i][:, :, j * 128:(j + 1) * 128].rearrange("c p k -> p c k"),
                                          ktb[:, 0:1024].rearrange("p (c k) -> p c k", k=128), [kt_], [KTscr[NP + i]], kt_)
                                va = vaug[j % 2]
                                K.dma(POOL, va[:, :, 0:64], c_v[i, j * 128:(j + 1) * 128, :].rearrange("p (h q) -> p h q", q=64), [], [va], va)
                                K.dma(SP, Vscr[NP + i][j * 128:(j + 1) * 128, :], va[:].rearrange("p h q -> p (h q)"), [va], [Vscr[NP + i]], va)

        def attn_phase(lj, li, tl):
            segs = tl["segs"]
            kind = tl["kind"]
            with K.phase() as ph:
                KT = K.sbuf("KT", [128, 8, KMAX], BF16, dma=True, stack=ph)
                VA = K.sbuf("VA", [128, KMAX // 128 + 1, 1056], BF16, dma=True, stack=ph)
                qTt = K.sbuf("qTt", [128, 8, 256], BF16, stack=ph)
                gs = K.sbuf("gs", [128, 2, D], BF16, stack=ph)
                PT = [K.sbuf("PT%d" % i, [128, 128], BF16, stack=ph) for i in range(4)]
                rc = K.sbuf("rc", [128, 4], F32, stack=ph)
                og = K.sbuf("og", [128, D], F32, stack=ph)
                ogb = K.sbuf("ogb", [128, D], BF16, stack=ph)
                ogT = K.sbuf("ogT", [128, 2, 8, 128], BF16, stack=ph)
                for cg in range(2):
                    wt, wv = wload(w_qg[lj, :, cg * 512:(cg + 1) * 512].rearrange("(a p) c -> p a c", p=128), v3(8, 512))
                    for cc in range(4):
                        c = cg * 4 + cc
                        bank = ps[c % 2]
                        for dk in range(8):
                            mm(bank[:, 0:256], wv[:, dk, cc * 128:(cc + 1) * 128], xT[:, dk, :], dk == 0, dk == 7, [xT, wt], [bank])
                        cpy(ACT if c % 2 else DVE, qTt[:, c, :], bank[:, 0:256], [bank], [qTt])
                for cg in range(2):
                    wt, wv = wload(w_qg[lj, :, 1024 + cg * 512:1024 + (cg + 1) * 512].rearrange("(a p) c -> p a c", p=128), v3(8, 512))
                    for s in range(2):
                        bank = ps[2 + s]
                        for dk in range(8):
                            mm(bank[:, :], xT[:, dk, s * 128:(s + 1) * 128], wv[:, dk, :], dk == 0, dk == 7, [xT, wt], [bank])
                        act(gs[:, s, cg * 512:(cg + 1) * 512], bank[:, :], AF.Sigmoid, [bank], [gs])
                cnt = [0]

                def attend(s, keysets):
                    nks = len(keysets)
                    for ki, ks in enumerate(keysets):
                        ks["load"]()
                        nk = ks["nfull"] + (1 if ks["part"] else 0)
                        for h in range(16):
                            accb = ps[4 + h // 4]
                            hh = h % 4
                            c, pbs = h // 2, (h % 2) * 64
                            for j in range(nk):
                                part = (j == ks["nfull"])
                                kp = 64 if part else 128
                                sb = ps[cnt[0] % 4]
                                p_ = PT[cnt[0] % 4]
                                cnt[0] += 1
                                mm(sb[0:kp, 0:128], KT[pbs:pbs + 64, c, j * 128:j * 128 + kp], qTt[pbs:pbs + 64, c, s * 128:(s + 1) * 128], True, True,
                                   [KT, qTt], [sb])
                                bias_ap, btile = ks["bias"](j, h)
                                act(p_[0:kp, :], sb[0:kp, 0:128], AF.Exp, [sb, btile], [p_], scale=ATT_SCALE, bias=bias_ap)
                                msk = ks["mask"](j)
                                if msk is not None:
                                    tt(p_[0:kp, :], p_[0:kp, :], msk, ALU.mult, [p_, conb], [p_])
                                mm(accb[:, hh * 65:(hh + 1) * 65], p_[0:kp, :], VA[0:kp, j, h * 66:h * 66 + 65],
                                   j == 0, j == nk - 1, [p_, VA], [accb])
                        pr = ks["rows"]
                        npr = pr.stop - pr.start
                        for hg in range(4):
                            accb = ps[4 + hg]
                            av = accb[pr, 0:260].rearrange("p (h q) -> p h q", q=65)
                            K.op(DVE, lambda: nc.vector.reciprocal(out=rc[pr, :], in_=av[:, :, 64]), [accb], [rc])
                            tt(og[pr, hg * 256:(hg + 1) * 256].rearrange("p (h q) -> p h q", q=64), av[:, :, 0:64],
                               rc[pr, :].unsqueeze(2).to_broadcast([npr, 4, 64]), ALU.mult, [accb, rc], [og])

                for s in range(2):
                    if kind == "p":
                        sl = tl["slot"]
                        T = tl["T"]
                        nkeys = (T + 1) * 256
                        nfull = T * 2 + s + 1

                        def load_p(sl=sl, nkeys=nkeys, s=s):
                            if s == 0:
                                K.dma(SP, KT[:, :, 0:nkeys], KTscr[sl][:, :, 0:nkeys].rearrange("c p k -> p c k"), [KTscr[sl]], [KT], KT)
                                K.dma(SP, VA[:, 0:nkeys // 128, :], Vscr[sl][0:nkeys, :].rearrange("(j p) f -> p j f", p=128), [Vscr[sl]], [VA], VA)
                        AT = AT0 if s == 0 else AT1
                        attend(s, [dict(load=load_p, nfull=nfull, part=False, rows=slice(0, 128),
                                        bias=lambda j, h, AT=AT: (AT[:, 0, j, h:h + 1], AT),
                                        mask=lambda j, nfull=nfull: (cB(C_TRI) if j == nfull - 1 else None))])
                    else:
                        kss = []
                        for sg in range(2):
                            i = 2 * s + sg

                            def load_s(i=i):
                                K.dma(SP, KT[:, :, 0:PAST + 64], KTscr[NP + i][:, :, 0:PAST + 64].rearrange("c p k -> p c k"), [KTscr[NP + i]], [KT], KT)
                                K.dma(SP, VA[:, 0:NKT_PAST, :], Vscr[NP + i][0:PAST, :].rearrange("(j p) f -> p j f", p=128), [Vscr[NP + i]], [VA], VA)
                                K.dma(SP, VA[0:64, NKT_PAST, :], Vscr[NP + i][PAST:PAST + 64, :], [Vscr[NP + i]], [VA], VA)

                            def bias_s(j, h, i=i):
                                if j == NKT_PAST:
                                    return NCS[0:64, i, h:h + 1], NCS
                                return SUF[:, i, j, h:h + 1], SUF

                            def mask_s(j, sg=sg):
                                if j == NKT_PAST:
                                    return conb[0:64, C_MS0 + sg, :]
                                return cB(C_CM0 + sg)
                            kss.append(dict(load=load_s, nfull=NKT_PAST, part=True, bias=bias_s, mask=mask_s, rows=slice(sg * 64, (sg + 1) * 64)))
                        attend(s, kss)
                    tt(ogb[:], og[:], gs[:, s, :], ALU.mult, [og, gs], [ogb])
                    pb = ps[s][:].bitcast(BF16)
                    for q in range(8):
                        tr(pb[:, q * 128:(q + 1) * 128], ogb[:, q * 128:(q + 1) * 128], cB(C_ID), [ogb, conb], [ps[s]])
                    cpy(ACT, ogT[:, s, :, :], pb[:, 0:1024].rearrange("p (a b) -> p a b", b=128), [ps[s]], [ogT])
                load_gb(li, 0)
                banks = [ps[0], ps[1], ps[2], ps[3]]
                for j in range(2):
                    wt, wv = wload(w_o[lj, j * 512:(j + 1) * 512, :].rearrange("(a p) c -> p a c", p=128), v3(4, 1024))
                    for cc in range(4):
                        cq = j * 4 + cc
                        for s in range(2):
                            for hh in range(2):
                                mm(banks[s * 2 + hh][:, :], ogT[:, s, cq, :], wv[:, cc, hh * 512:(hh + 1) * 512], cq == 0, cq == 7,
                                   [ogT, wt], [banks[s * 2 + hh]])
                for s in range(2):
                    deepnorm(s, banks[s * 2:s * 2 + 2], li, 0)
            make_xT()

        tiles = []
        for b in range(NP):
            for T in range(NT_P):
                tiles.append(dict(kind="p", seq=b, T=T, slot=b, segs=[0], last=(T == NT_P - 1), outseq=[("p", b)]))
        tiles.append(dict(kind="s", segs=[0, 1, 2, 3], last=True, outseq=[("s", i) for i in range(4)], T=0))

        stop_at = int(os.environ.get("KSTOP", "100000"))
        stage = [0]

        class _Stop(Exception):
            pass

        def chk():
            stage[0] += 1
            if stage[0] >= stop_at:
                raise _Stop()

        try:
          for tl in tiles:
              kind = tl["kind"]
              if kind == "p":
                  K.dma(SP, x[:], x_p[tl["seq"], tl["T"] * 256:(tl["T"] + 1) * 256, :].rearrange("(s p) d -> p s d", p=128), [], [x], x)
                  if tl["T"] == 0:
                      for l in range(2):
                          K.op(DVE, lambda: nc.vector.memset(hTp[l][:], 0.0), [], [hTp[l]])
                          K.op(DVE, lambda: nc.vector.memset(cst[l][:], 0.0), [], [cst[l]])
              else:
                  K.dma(SP, x[:], x_s[:, :, :].rearrange("b l d -> (b l) d").rearrange("(s p) d -> p s d", p=128), [], [x], x)
              make_xT()
              chk()
              konly = os.environ.get("KONLY", "")
              if konly == "kv":
                  kv_phase(tl)
                  raise _Stop()
              for li in range(DEPTH):
                  if li < 2:
                      mamba_phase(li, tl)
                  else:
                      attn_phase(li - 2, li, tl)
                  chk()
                  peer_phase(li)
                  chk()
                  if li < DEPTH - 1:
                      make_xT()
                  if li == 1:
                      kv_phase(tl)
                      chk()
              if kind == "p":
                  K.dma(SP, y_p[tl["seq"], tl["T"] * 256:(tl["T"] + 1) * 256, :].rearrange("(s p) d -> p s d", p=128), x[:], [x], [], x)
              else:
                  K.dma(SP, y_s[:, :, :].rearrange("b l d -> (b l) d").rearrange("(s p) d -> p s d", p=128), x[:], [x], [], x)
        except _Stop:
            K.dma(SP, y_p[0, 0:256, :].rearrange("(s p) d -> p s d", p=128), x[:], [x], [], x)
        K.finish()
        build.stats = dict(nsem=K.nsem, nwait=K.nwait, n={E.name: E.n for E in K.engs})
    return nc


def _consts():
    t = np.zeros((NCON, 128, 128), np.float32)
    idx = np.arange(128)
    tri = (idx[:, None] <= idx[None, :]).astype(np.float32)
    blk = ((idx[:, None] // 64) == (idx[None, :] // 64)).astype(np.float32)
    t[0] = np.eye(128, dtype=np.float32)
    t[1] = tri
    t[2] = tri * blk
    t[3] = 1.0
    t[4] = blk
    t[5] = np.broadcast_to(idx[None, :].astype(np.float32), (128, 128))
    t[6] = (idx[:, None] > idx[None, :]).astype(np.float32)
    t[7][0:64, 0:64] = tri[0:64, 0:64]
    t[8][0:64, 64:128] = tri[0:64, 0:64]
    t[9][:, 0:64] = 1.0
    t[10][:, 64:128] = 1.0
    return np.ascontiguousarray(t.transpose(1, 0, 2).reshape(128, NCON * 128))


_NC_CACHE = {}


def _NEXP():
    import os
    return int(os.environ.get("KEXP", "16384"))


def run(inputs, n_cores):
    f = lambda a: np.ascontiguousarray(np.asarray(a, dtype=np.float32))
    xp = f(inputs["x_prompt"]); xs = f(inputs["x_sample"])
    B, SEQ, _ = xp.shape
    DB = xs.shape[0]
    assert DB == 4 * n_cores and B % n_cores == 0 and xs.shape[1] == 64
    NP = B // n_cores
    PAST = inputs["cache_k"].shape[1]
    cfg = (NP, SEQ, PAST)
    if cfg not in _NC_CACHE:
        _NC_CACHE[cfg] = build(cfg)
    nc = _NC_CACHE[cfg]
    ssm = f(inputs["state_ssm"]).reshape(2, DB, 2048, 128)
    cvs = f(inputs["state_conv"])
    ck = f(inputs["cache_k"]).reshape(DB, PAST, D); cv = f(inputs["cache_v"]).reshape(DB, PAST, D); cf = f(inputs["cache_logf"])
    cw = f(inputs["a_conv_w"]); cb = f(inputs["a_conv_b"])
    convp = np.zeros((2, 128, 24, 5), np.float32)
    convp[:, :, :, 0:4] = cw.reshape(2, 4, 24, 128).transpose(0, 3, 2, 1)
    convp[:, :, :, 4] = cb.reshape(2, 24, 128).transpose(0, 2, 1)
    shared = {
        "w_in": f(inputs["a_w_in"]), "convp": np.ascontiguousarray(convp.reshape(2, 128, 120)),
        "dtb": f(inputs["a_dt_bias"]), "alog": f(inputs["a_A_log"]), "dsk": f(inputs["a_D"]),
        "normw": np.ascontiguousarray(f(inputs["a_norm_w"]).reshape(2, 16, 128).transpose(0, 2, 1)),
        "w_out": f(inputs["a_w_out"]), "kv_wkv": np.ascontiguousarray(f(inputs["kv_w"])[:, 0:2048]),
        "kv_wf": np.ascontiguousarray(f(inputs["kv_w"])[:, 2048:2064]), "kvb": f(inputs["kv_b_f"]),
        "w_qg": f(inputs["b_w_qg"]), "w_o": f(inputs["b_w_o"]), "pwq": f(inputs["peer_w_q"]),
        "skT": np.ascontiguousarray(f(inputs["peer_subkeys"]).transpose(0, 1, 3, 2)),
        "uT": np.ascontiguousarray(f(inputs["peer_u"])[:, 0:_NEXP()].transpose(0, 2, 1)), "vt": np.ascontiguousarray(f(inputs["peer_v"])[:, 0:_NEXP()]),
        "lng": f(inputs["ln_g"]), "lnb": f(inputs["ln_b"]), "consts": _consts(),
    }
    in_maps = []
    for i in range(n_cores):
        m = dict(shared)
        m["x_p"] = np.ascontiguousarray(xp[i * NP:(i + 1) * NP]); m["x_s"] = np.ascontiguousarray(xs[4 * i:4 * i + 4])
        m["st_ssm"] = np.ascontiguousarray(ssm[:, 4 * i:4 * i + 4]); m["st_conv"] = np.ascontiguousarray(cvs[:, 4 * i:4 * i + 4])
        m["c_k"] = np.ascontiguousarray(ck[4 * i:4 * i + 4]); m["c_v"] = np.ascontiguousarray(cv[4 * i:4 * i + 4])
        m["c_f"] = np.ascontiguousarray(cf[4 * i:4 * i + 4])
        in_maps.append(m)
    res = run_bass_kernel_spmd(nc, in_maps, core_ids=list(range(n_cores)))
    R = res.results
    cat = lambda k, ax: np.concatenate([np.asarray(r[k]) for r in R], axis=ax)
    y_p = cat("y_p", 0); y_s = cat("y_s", 0)
    p_ssm = cat("o_pssm", 1).reshape(2, B, 32, 64, 128); p_conv = cat("o_pconv", 1)
    p_k = cat("o_pk", 0).reshape(B, SEQ, 16, 64); p_v = cat("o_pv", 0).reshape(B, SEQ, 16, 64); p_f = cat("o_pf", 0)
    s_ssm = cat("o_sssm", 1).reshape(2, DB, 32, 64, 128); s_conv = cat("o_sconv", 1)
    s_k = cat("o_sk", 0).reshape(DB, 64, 16, 64); s_v = cat("o_sv", 0).reshape(DB, 64, 16, 64); s_f = cat("o_sf", 0)
    return (y_p, y_s, p_ssm, p_conv, p_k, p_v, p_f, s_ssm, s_conv, s_k, s_v, s_f)


def kernel(**inputs):
    return run(inputs, 8)
```

```python
import numpy as np
from contextlib import ExitStack
import concourse.bass as bass
import concourse.mybir as mybir
from concourse.bass_utils import run_bass_kernel_spmd

F32 = mybir.dt.float32
BF16 = mybir.dt.bfloat16
U32 = mybir.dt.uint32
AF = mybir.ActivationFunctionType
ALU = mybir.AluOpType
AX = mybir.AxisListType

SEM_CH = 30000
D = 1024
DEPTH = 4
ALPHA = float((2.0 * DEPTH) ** 0.25)
LN_EPS = 1e-5
RMS_EPS = 1e-5
ATT_SCALE = 64 ** -0.5
NCON = 11


class Tile:
    def __init__(self, K, handle, name, dma_sem=False, multi=False):
        self.K = K
        self.h = handle
        self.name = name
        self.w = None
        self.wd = {} if multi else None
        self.r = {}
        self.is_psum = False
        self.dsem = {}
        if dma_sem:
            K.dma_tiles.append(self)
            if K.phase_tiles is not None:
                K.phase_tiles.append(self)

    def __getitem__(self, key):
        return self.h[key]


class Eng:
    def __init__(self, K, name, eng):
        self.K = K
        self.name = name
        self.eng = eng
        self.n = 0
        self.sems = []
        self.seen = {}

    def token(self, n):
        b = (n - 1) // SEM_CH
        while len(self.sems) <= b:
            self.sems.append(self.K.new_sem("e_%s_%d" % (self.name, len(self.sems))))
        return (self.sems[b], (n - 1) % SEM_CH + 1, self)


class Kern:
    def __init__(self, nc, stack):
        self.nc = nc
        self.stack = stack
        self.dma_tiles = []
        self.sem_pool = {"hw": [], "sw": []}
        self.sem_latest = {}
        self.phase_tiles = None
        self.pe = Eng(self, "pe", nc.tensor)
        self.act = Eng(self, "act", nc.scalar)
        self.dve = Eng(self, "dve", nc.vector)
        self.pool = Eng(self, "pool", nc.gpsimd)
        self.sp = Eng(self, "sp", nc.sync)
        self.engs = [self.pe, self.act, self.dve, self.pool, self.sp]
        self.nsem = 0
        self.nwait = 0
        self.uid = 0

    def new_sem(self, name):
        self.nsem += 1
        return self.stack.enter_context(self.nc.semaphore(name))

    def sbuf(self, name, shape, dt, dma=False, stack=None):
        self.uid += 1
        nm = "%s_%d" % (name, self.uid)
        h = (stack or self.stack).enter_context(self.nc.sbuf_tensor(nm, list(shape), dt))
        return Tile(self, h, nm, dma_sem=dma)

    def psum(self, name, shape, dt):
        h = self.stack.enter_context(self.nc.psum_tensor(name, list(shape), dt))
        t = Tile(self, h, name)
        t.is_psum = True
        return t

    def dram(self, name, shape, dt, kind="Internal", multi=False):
        h = self.nc.dram_tensor(name, list(shape), dt, kind=kind).ap()
        return Tile(self, h, name, multi=multi)

    def _wait(self, E, tok):
        sem, val, src = tok
        if src is E and E.name == "pe":
            return
        key = id(sem)
        if src is None:
            val = max(val, self.sem_latest.get(key, 0))
        if E.seen.get(key, 0) >= val:
            return
        E.eng.wait_ge(sem, val)
        self.nwait += 1
        E.seen[key] = val

    def _sync(self, E, reads, writes):
        for t in reads:
            if t.w is not None:
                self._wait(E, t.w)
            if t.wd:
                for tok in t.wd.values():
                    self._wait(E, tok)
            if t.is_psum:
                for k_, tok in t.r.items():
                    if k_ != E.name:
                        self._wait(E, tok)
        for t in writes:
            if t.w is not None:
                self._wait(E, t.w)
            if t.wd:
                for tok in t.wd.values():
                    self._wait(E, tok)
            for tok in t.r.values():
                self._wait(E, tok)

    def op(self, E, fn, reads=(), writes=()):
        self._sync(E, reads, writes)
        inst = fn()
        E.n += 1
        tok = E.token(E.n)
        inst.then_inc(tok[0], 1)
        for t in reads:
            t.r[E.name] = tok
        for t in writes:
            t.w = tok
            t.r = {}
            if t.wd is not None:
                t.wd = {}
        return inst

    def phase(self):
        return _Phase(self)

    def dma(self, Q, out_ap, in_ap, reads, writes, semtile, tag=None):
        import os
        if tag is not None and tag in os.environ.get("KDIS", "").split(","):
            return None
        self._sync(Q, reads, writes)
        inst = Q.eng.dma_start(out=out_ap, in_=in_ap)
        kind = "sw" if Q is self.pool else "hw"
        ent = semtile.dsem.get(kind)
        if ent is None or ent[1] + 16 > SEM_CH:
            if ent is None and self.sem_pool[kind]:
                ent = list(self.sem_pool[kind].pop())
            else:
                ent = [self.new_sem("d%s_%s" % (kind, semtile.name)), 0]
            semtile.dsem[kind] = ent
        ent[1] += 16
        inst.then_inc(ent[0], 16)
        self.sem_latest[id(ent[0])] = ent[1]
        tok = (ent[0], ent[1], None)
        for t in reads:
            t.r["dma%d" % id(ent[0])] = tok
        for t in writes:
            if t.wd is not None:
                t.wd[id(ent[0])] = tok
            else:
                t.w = tok
            t.r = {}
        return inst

    def barrier(self):
        toks = []
        for E in self.engs:
            if E.n > 0:
                toks.append(E.token(E.n))
        for t in self.dma_tiles:
            for ent in t.dsem.values():
                toks.append((ent[0], ent[1], None))
        for E in self.engs:
            for tok in toks:
                if tok[2] is E:
                    continue
                self._wait(E, tok)

    def finish(self):
        for t in self.dma_tiles:
            for ent in t.dsem.values():
                self._wait(self.sp, (ent[0], ent[1], None))


class _Phase:
    def __init__(self, K):
        self.K = K

    def __enter__(self):
        self.stack = ExitStack()
        self.stack.__enter__()
        self.K.phase_tiles = []
        return self.stack

    def __exit__(self, *a):
        K = self.K
        K.barrier()
        for t in K.phase_tiles:
            K.dma_tiles.remove(t)
            for kind, ent in t.dsem.items():
                if ent[1] < 20000:
                    K.sem_pool[kind].append((ent[0], ent[1]))
        K.phase_tiles = None
        return self.stack.__exit__(*a)


def build(cfg):
    NP, SEQ, PAST = cfg
    NT_P = SEQ // 256
    KMAX = max(SEQ, PAST + 64)
    NKT_PAST = PAST // 128
    nc = bass.Bass("TRN2", target_bir_lowering=False)
    st = ExitStack()
    with st:
        K = Kern(nc, st)
        PE, ACT, DVE, POOL, SP = K.pe, K.act, K.dve, K.pool, K.sp

        def din(name, shape):
            return K.dram(name, shape, F32, "ExternalInput")

        def dout(name, shape):
            return K.dram(name, shape, F32, "ExternalOutput")

        x_p = din("x_p", [NP, SEQ, D]); x_s = din("x_s", [4, 64, D])
        st_ssm = din("st_ssm", [2, 4, 2048, 128]); st_conv = din("st_conv", [2, 4, 3, 3072])
        c_k = din("c_k", [4, PAST, D]); c_v = din("c_v", [4, PAST, D]); c_f = din("c_f", [4, PAST, 16])
        w_in = din("w_in", [2, D, 5152]); convp = din("convp", [2, 128, 120])
        dtb = din("dtb", [2, 32]); alog = din("alog", [2, 32]); dsk = din("dsk", [2, 32])
        normw = din("normw", [2, 128, 16]); w_out = din("w_out", [2, 2048, D])
        kv_wkv = din("kv_wkv", [D, 2048]); kv_wf = din("kv_wf", [D, 16]); kvb = din("kvb", [16])
        w_qg = din("w_qg", [2, D, 2048]); w_o = din("w_o", [2, D, D])
        pwq = din("pwq", [4, D, 2048]); skT = din("skT", [4, 2, 128, 128])
        import os
        NEXP = int(os.environ.get("KEXP", "16384"))
        uT = din("uT", [4, D, NEXP]); vt = din("vt", [4, NEXP, D])
        lng = din("lng", [4, 2, D]); lnb = din("lnb", [4, 2, D])
        cons_d = din("consts", [128, NCON * 128])

        y_p = dout("y_p", [NP, SEQ, D]); y_s = dout("y_s", [4, 64, D])
        o_pssm = dout("o_pssm", [2, NP, 2048, 128]); o_pconv = dout("o_pconv", [2, NP, 3, 3072])
        o_pk = dout("o_pk", [NP, SEQ, D]); o_pv = dout("o_pv", [NP, SEQ, D]); o_pf = dout("o_pf", [NP, SEQ, 16])
        o_sssm = dout("o_sssm", [2, 4, 2048, 128]); o_sconv = dout("o_sconv", [2, 4, 3, 3072])
        o_sk = dout("o_sk", [4, 64, D]); o_sv = dout("o_sv", [4, 64, D]); o_sf = dout("o_sf", [4, 64, 16])

        NSEQ = NP + 4
        NCGT = NEXP // 512
        uTb = [K.dram("uTb%d" % l, [NCGT, 128, 4096], BF16, multi=True) for l in range(4)]
        vtb = [K.dram("vtb%d" % l, [NCGT, 128, 4096], BF16, multi=True) for l in range(4)]
        KTscr = [K.dram("ktscr%d" % i, [8, 128, KMAX], BF16, multi=True) for i in range(NSEQ)]
        Vscr = [K.dram("vscr%d" % i, [KMAX + 128, 1056], BF16, multi=True) for i in range(NSEQ)]

        ps = [K.psum("ps%d" % i, [128, 512], F32) for i in range(8)]
        con = K.sbuf("con", [128, NCON, 128], F32, dma=True)
        conb = K.sbuf("conb", [128, NCON, 128], BF16)
        C_ID, C_TRI, C_TRI2, C_ONE, C_ONE2, C_IOTA, C_TSU, C_MS0, C_MS1, C_CM0, C_CM1 = range(NCON)
        x = K.sbuf("x", [128, 2, D], F32, dma=True)
        xb = K.sbuf("xb", [128, 2, D], BF16)
        xT = K.sbuf("xT", [128, 8, 256], BF16)
        junk = K.sbuf("junk", [128, D], BF16)
        gb = K.sbuf("gb", [128, 2, D], F32, dma=True)
        st4 = K.sbuf("st4", [128, 16], F32)
        NSLOT = 4
        wsl = [K.sbuf("wsl%d" % i, [128, 4096], BF16, dma=True) for i in range(NSLOT)]
        wsl_i = [0]
        cvp = K.sbuf("cvp", [128, 2, 120], F32, dma=True)
        nwp = K.sbuf("nwp", [128, 2, 16], F32, dma=True)
        dtb_s = K.sbuf("dtb_s", [128, 2, 32], F32, dma=True)
        aneg = K.sbuf("aneg", [128, 2, 32], F32, dma=True)
        dsk_s = K.sbuf("dsk_s", [128, 2, 32], F32, dma=True)
        kvb_s = K.sbuf("kvb_s", [128, 16], F32, dma=True)
        hTp = [K.sbuf("hTp%d" % l, [128, 2048], F32) for l in range(2)]
        cst = [K.sbuf("cst%d" % l, [128, 24, 4, 3], F32) for l in range(2)]
        SUF = K.sbuf("SUF", [128, 4, 16, 16], F32)
        AT0 = K.sbuf("AT0", [128, 1, 16, 16], F32)
        AT1 = K.sbuf("AT1", [128, 1, 16, 16], F32)
        NCS = K.sbuf("NCS", [128, 4, 16], F32)

        wcache = {}

        def wload(src_ap, view):
            s = wsl[wsl_i[0] % NSLOT]
            wsl_i[0] += 1
            v = view(s)
            n = 1
            for d_ in v.shape[1:]:
                n *= d_
            key = (src_ap.name, str(src_ap.offset), tuple(src_ap.shape))
            if key in wcache:
                scr = wcache[key]
                K.dma(SP, s[:, 0:n], scr[:, 0:n], [scr], [s], s)
                return s, v
            K.dma(POOL, v, src_ap, [], [s], s)
            if n >= 1024:
                scr = K.dram("wscr%d" % len(wcache), [128, n], BF16, multi=True)
                K.dma(SP, scr[:, :], s[:, 0:n], [s], [scr], s)
                wcache[key] = scr
            return s, v

        def v3(n1, n2):
            return lambda s: s[:, 0:n1 * n2].rearrange("p (a b) -> p a b", b=n2)

        pre = [K.sbuf("pre%d" % l, [128, 2], F32, dma=True) for l in range(4)]
        for l in range(4):
            for cg in range(NCGT):
                K.dma(POOL, uTb[l][cg, :, :].rearrange("p (a e) -> p a e", e=512),
                      uT[l, :, cg * 512:(cg + 1) * 512].rearrange("(a p) e -> p a e", p=128), [], [uTb[l]], pre[l])
                K.dma(POOL, vtb[l][cg, :, :].rearrange("p (a d) -> p a d", d=1024),
                      vt[l, cg * 512:(cg + 1) * 512, :].rearrange("(a p) d -> p a d", p=128), [], [vtb[l]], pre[l])
        K.dma(SP, con[:], cons_d[:, :].rearrange("p (a b) -> p a b", b=128), [], [con], con)
        K.op(DVE, lambda: nc.vector.tensor_copy(out=conb[:], in_=con[:]), [con], [conb])
        K.dma(SP, cvp[:], convp[:, :, :].rearrange("l p f -> p l f"), [], [cvp], cvp)
        K.dma(SP, nwp[:], normw[:, :, :].rearrange("l p f -> p l f"), [], [nwp], nwp)
        for l in range(2):
            K.dma(SP, dtb_s[:, l, :], dtb[l, :].partition_broadcast(128), [], [dtb_s], dtb_s)
            K.dma(SP, aneg[:, l, :], alog[l, :].partition_broadcast(128), [], [aneg], aneg)
            K.dma(SP, dsk_s[:, l, :], dsk[l, :].partition_broadcast(128), [], [dsk_s], dsk_s)
        K.dma(SP, kvb_s[:], kvb[:].partition_broadcast(128), [], [kvb_s], kvb_s)
        K.op(ACT, lambda: nc.scalar.activation(out=aneg[:], in_=aneg[:], func=AF.Exp), [aneg], [aneg])
        K.op(DVE, lambda: nc.vector.tensor_scalar(out=aneg[:], in0=aneg[:], scalar1=-1.0, scalar2=None, op0=ALU.mult), [aneg], [aneg])

        def cF(i):
            return con[:, i, :]

        def cB(i):
            return conb[:, i, :]

        def mm(out, lhsT, rhs, start, stop, reads, writes):
            K.op(PE, lambda: nc.tensor.matmul(out, lhsT, rhs, start=start, stop=stop), reads, writes)

        def tr(out, in_, ident, reads, writes):
            K.op(PE, lambda: nc.tensor.transpose(out, in_, ident), reads, writes)

        def act(out, in_, func, reads, writes, **kw):
            K.op(ACT, lambda: nc.scalar.activation(out=out, in_=in_, func=func, **kw), reads, writes)

        def tt(out, in0, in1, op, reads, writes, E=None):
            E = E or DVE
            K.op(E, lambda: E.eng.tensor_tensor(out=out, in0=in0, in1=in1, op=op), reads, writes)

        def ts(out, in0, s1, s2, op0, op1, reads, writes):
            if s2 is None:
                K.op(DVE, lambda: nc.vector.tensor_scalar(out=out, in0=in0, scalar1=s1, scalar2=None, op0=op0), reads, writes)
            else:
                K.op(DVE, lambda: nc.vector.tensor_scalar(out=out, in0=in0, scalar1=s1, scalar2=s2, op0=op0, op1=op1), reads, writes)

        def stt(out, in0, scalar, in1, op0, op1, reads, writes):
            K.op(DVE, lambda: nc.vector.scalar_tensor_tensor(out=out, in0=in0, scalar=scalar, in1=in1, op0=op0, op1=op1), reads, writes)

        def cpy(E, out, in_, reads, writes):
            if E is ACT:
                K.op(ACT, lambda: nc.scalar.copy(out=out, in_=in_), reads, writes)
            else:
                K.op(E, lambda: E.eng.tensor_copy(out=out, in_=in_), reads, writes)

        def make_xT():
            for s in range(2):
                cpy(ACT, xb[:, s, :], x[:, s, :], [x], [xb])
                pb = ps[7 - s][:].bitcast(BF16)
                for dk in range(8):
                    tr(pb[:, dk * 128:(dk + 1) * 128], xb[:, s, dk * 128:(dk + 1) * 128], cB(C_ID), [xb, conb], [ps[7 - s]])
                cpy(DVE, xT[:, :, s * 128:(s + 1) * 128], pb[:, 0:1024].rearrange("p (a b) -> p a b", b=128), [ps[7 - s]], [xT])

        def deepnorm(s, banks, li, which):
            for hh in range(2):
                stt(x[:, s, hh * 512:(hh + 1) * 512], x[:, s, hh * 512:(hh + 1) * 512], ALPHA, banks[hh][:, :],
                    ALU.mult, ALU.add, [x, banks[hh]], [x])
            act(junk[:], x[:, s, :], AF.Identity, [x], [junk, st4], accum_out=st4[:, 0:1])
            act(junk[:], x[:, s, :], AF.Square, [x], [junk, st4], accum_out=st4[:, 1:2])
            ts(st4[:, 2:3], st4[:, 0:1], 1.0 / D, None, ALU.mult, None, [st4], [st4])
            tt(st4[:, 3:4], st4[:, 2:3], st4[:, 2:3], ALU.mult, [st4], [st4])
            stt(st4[:, 4:5], st4[:, 1:2], 1.0 / D, st4[:, 3:4], ALU.mult, ALU.subtract, [st4], [st4])
            act(st4[:, 7:8], st4[:, 4:5], AF.Sqrt, [st4], [st4], bias=LN_EPS)
            K.op(DVE, lambda: nc.vector.reciprocal(out=st4[:, 5:6], in_=st4[:, 7:8]), [st4], [st4])
            stt(st4[:, 6:7], st4[:, 2:3], -1.0, st4[:, 5:6], ALU.mult, ALU.mult, [st4], [st4])
            act(x[:, s, :], x[:, s, :], AF.Identity, [x, st4], [x], scale=st4[:, 5:6], bias=st4[:, 6:7])
            tt(x[:, s, :], x[:, s, :], gb[:, 0, :], ALU.mult, [x, gb], [x])
            tt(x[:, s, :], x[:, s, :], gb[:, 1, :], ALU.add, [x, gb], [x])

        def load_gb(li, which):
            K.dma(SP, gb[:, 0, :], lng[li, which, :].partition_broadcast(128), [], [gb], gb)
            K.dma(SP, gb[:, 1, :], lnb[li, which, :].partition_broadcast(128), [], [gb], gb)

        def mamba_phase(li, tl):
            segs = tl["segs"]
            nseg = len(segs)
            L = 256 // nseg
            nsub = 2 if nseg == 4 else 1
            TM = C_TRI if nsub == 1 else C_TRI2
            with K.phase() as ph:
                zs = K.sbuf("zs", [128, 2, 2048], BF16, stack=ph)
                xsT = K.sbuf("xsT", [128, 16, 256], BF16, stack=ph)
                BT = K.sbuf("BT", [128, 4, 256], BF16, stack=ph)
                CTt = K.sbuf("CT", [128, 4, 256], BF16, stack=ph)
                CTm = K.sbuf("CTm", [128, 2, 4, 128], BF16, stack=ph)
                ext = [K.sbuf("ext%d" % i, [128, 4, 67], F32, stack=ph) for i in range(2)]
                cvt = [K.sbuf("cvt%d" % i, [128, 4, 64], F32, stack=ph) for i in range(2)]
                lastT = K.sbuf("lastT", [128, 8, 12], BF16, stack=ph)
                cvout = [K.sbuf("cvout%d" % i, [12, 512], F32, dma=True, stack=ph) for i in range(2)]
                hb16 = [K.sbuf("hb16_%d" % i, [128, 512], BF16, stack=ph) for i in range(2)]
                dt = K.sbuf("dt", [128, 2, 32], F32, stack=ph)
                aa = K.sbuf("aa", [128, 2, 32], F32, stack=ph)
                acs = K.sbuf("acs", [128, 32], F32, stack=ph)
                eacs = K.sbuf("eacs", [128, 32], F32, stack=ph)
                xs_tm = K.sbuf("xs_tm", [128, 2048], BF16, stack=ph)
                xdt = K.sbuf("xdt", [128, 2048], BF16, stack=ph)
                xdtw = K.sbuf("xdtw", [128, 512], BF16, stack=ph)
                Btm = K.sbuf("Btm", [128, 4, 128], BF16, stack=ph)
                abc = K.sbuf("abc", [128, 8, 128], F32, stack=ph)
                t1 = K.sbuf("t1", [128, 8, 128], F32, stack=ph)
                Dm = K.sbuf("Dm", [128, 8, 128], BF16, stack=ph)
                MT = K.sbuf("MT", [128, 8, 128], BF16, stack=ph)
                CBm = K.sbuf("CBm", [128, 128], BF16, stack=ph)
                yt = K.sbuf("yt", [128, 2048], F32, stack=ph)
                ytmp = K.sbuf("ytmp", [128, 512], F32, stack=ph)
                sm = K.sbuf("sm", [128, 64], F32, stack=ph)
                yn = K.sbuf("yn", [128, 2048], BF16, stack=ph)
                ynT = K.sbuf("ynT", [128, 2, 16, 128], BF16, stack=ph)
                stg = K.sbuf("stg", [128, 2048], F32, dma=True, stack=ph)
                if nseg == 1:
                    hT = [hTp[li]]
                else:
                    hT = [K.sbuf("hTs%d" % i, [128, 2048], F32, stack=ph) for i in range(2)]
                    cl = K.sbuf("cl", [12, 1536], F32, dma=True, stack=ph)
                    for hf in range(2):
                        K.dma(SP, cl[:], st_conv[li, :, :, hf * 1536:(hf + 1) * 1536].rearrange("b r c -> (b r) c"), [], [cl], cl)
                        for c in range(12):
                            tr(ps[hf][:, c * 12:c * 12 + 12], cl[:, c * 128:(c + 1) * 128], con[0:12, C_ID, 0:12], [cl, con], [ps[hf]])
                        cpy(DVE, cst[li][:, hf * 12:(hf + 1) * 12, :, :], ps[hf][:, 0:144].rearrange("p (c s r) -> p c s r", s=4, r=3), [ps[hf]], [cst[li]])

                def load_state(h_, src):
                    K.dma(SP, stg[:].rearrange("p (c n) -> p c n", n=128), src.rearrange("(c p) n -> p c n", p=128), [], [stg], stg)
                    for q4 in range(4):
                        for q in range(4):
                            cq = q4 * 4 + q
                            tr(ps[4 + q4][:, q * 128:(q + 1) * 128], stg[:, cq * 128:(cq + 1) * 128], cF(C_ID), [stg, con], [ps[4 + q4]])
                        cpy(DVE if q4 % 2 else ACT, h_[:, q4 * 512:(q4 + 1) * 512], ps[4 + q4][:, :], [ps[4 + q4]], [h_])

                def store_state(h_, dst):
                    for q4 in range(4):
                        for q in range(4):
                            cq = q4 * 4 + q
                            tr(ps[q4][:, q * 128:(q + 1) * 128], h_[:, cq * 128:(cq + 1) * 128], cF(C_ID), [h_, con], [ps[q4]])
                        cpy(ACT if q4 % 2 else DVE, stg[:, q4 * 512:(q4 + 1) * 512], ps[q4][:, :], [ps[q4]], [stg])
                    K.dma(SP, dst.rearrange("(c p) n -> p c n", p=128), stg[:].rearrange("p (c n) -> p c n", n=128), [stg], [], stg)

                is_last = tl["last"]
                if is_last:
                    cpy(DVE, lastT[:, :, 0:3 * nseg].rearrange("p a (s r) -> p a s r", r=3),
                        xT[:, :, :].rearrange("p a (s l) -> p a s l", l=L)[:, :, :, L - 3:L], [xT], [lastT])
                for cg in range(4):
                    wt, wv = wload(w_in[li, :, cg * 512:(cg + 1) * 512].rearrange("(a p) c -> p a c", p=128), v3(8, 512))
                    for s in range(2):
                        bank = ps[s]
                        for dk in range(8):
                            mm(bank[:, :], xT[:, dk, s * 128:(s + 1) * 128], wv[:, dk, :], dk == 0, dk == 7, [xT, wt], [bank])
                        act(zs[:, s, cg * 512:(cg + 1) * 512], bank[:, :], AF.Silu, [bank], [zs])
                for g in range(6):
                    wt, wv = wload(w_in[li, :, 2048 + g * 512:2048 + (g + 1) * 512].rearrange("(a p) c -> p a c", p=128), v3(8, 512))
                    for cc in range(4):
                        c = g * 4 + cc
                        bank = ps[2 + (c % 2)]
                        for dk in range(8):
                            mm(bank[:, 0:256], wv[:, dk, cc * 128:(cc + 1) * 128], xT[:, dk, :], dk == 0, dk == 7, [xT, wt], [bank])
                        e = ext[c % 2]
                        cv = cvt[c % 2]
                        ev = e[:, 0:nseg, 0:3 + L] if nseg == 4 else e[:, :, :].rearrange("p a b -> p (a b)")[:, 0:259].unsqueeze(1)
                        cvv = cv[:, 0:nseg, 0:L] if nseg == 4 else cv[:, :, :].rearrange("p a b -> p (a b)").unsqueeze(1)
                        cpy(ACT, ev[:, :, 3:3 + L], bank[:, 0:256].rearrange("p (s l) -> p s l", l=L), [bank], [e])
                        cpy(DVE, ev[:, :, 0:3], cst[li][:, c, 0:nseg, :], [cst[li]], [e])
                        wcol = lambda k: cvp[:, li, c * 5 + k:c * 5 + k + 1]
                        ts(cvv, ev[:, :, 0:L], wcol(0), wcol(4), ALU.mult, ALU.add, [e, cvp], [cv])
                        for k in range(1, 4):
                            stt(cvv, ev[:, :, k:k + L], wcol(k), cvv, ALU.mult, ALU.add, [e, cvp, cv], [cv])
                        cpy(DVE, cst[li][:, c, 0:nseg, :], ev[:, :, L:L + 3], [e], [cst[li]])
                        if c < 16:
                            dst, dt_ = xsT[:, c, :], xsT
                        elif c < 20:
                            dst, dt_ = BT[:, c - 16, :], BT
                        else:
                            dst, dt_ = CTt[:, c - 20, :], CTt
                        act(dst.rearrange("p (s l) -> p s l", l=L), cvv, AF.Silu, [cv], [dt_])
                    if is_last:
                        bank = ps[4]
                        for dk in range(8):
                            mm(bank[0:3 * nseg, :], lastT[:, dk, 0:3 * nseg], wv[:, dk, :], dk == 0, dk == 7, [lastT, wt], [bank])
                        co = cvout[g % 2]
                        cpy(ACT, co[0:3 * nseg, :], bank[0:3 * nseg, :], [bank], [co])
                        if tl["kind"] == "p":
                            K.dma(SP, o_pconv[li, tl["seq"], :, g * 512:(g + 1) * 512], co[0:3, :], [co], [], co)
                        else:
                            K.dma(SP, o_sconv[li, :, :, g * 512:(g + 1) * 512].rearrange("b r c -> (b r) c"), co[0:12, :], [co], [], co)
                wt, wv = wload(w_in[li, :, 5120:5152].rearrange("(a p) c -> p a c", p=128), v3(8, 32))
                for s in range(2):
                    bank = ps[4 + s]
                    for dk in range(8):
                        mm(bank[:, 0:32], xT[:, dk, s * 128:(s + 1) * 128], wv[:, dk, :], dk == 0, dk == 7, [xT, wt], [bank])
                    tt(dt[:, s, :], bank[:, 0:32], dtb_s[:, li, :], ALU.add, [bank, dtb_s], [dt])
                act(dt[:], dt[:], AF.Exp, [dt], [dt])
                act(dt[:], dt[:], AF.Ln, [dt], [dt], bias=1.0)
                tt(aa[:], dt[:], aneg[:, li, :].unsqueeze(1).to_broadcast([128, 2, 32]), ALU.mult, [dt, aneg], [aa])
                if nsub == 2:
                    cpy(DVE, CTm[:].rearrange("p a b c -> p (a b c)"), conb[:, C_CM0, 64:65].to_broadcast([128, 1024]), [conb], [CTm])

                for s in range(2):
                    cols = slice(s * 128, (s + 1) * 128)
                    sub_segs = [0] if nsub == 1 else [0, 1]
                    if nsub == 2:
                        for si in range(2):
                            load_state(hT[si], st_ssm[li, 2 * s + si, :, :])
                    prs = [slice(0, 128)] if nsub == 1 else [slice(0, 64), slice(64, 128)]
                    lastc = [127] if nsub == 1 else [63, 127]
                    for half in range(2):
                        pb = ps[half][:].bitcast(BF16)
                        for q in range(8):
                            tr(pb[:, q * 128:(q + 1) * 128], xsT[:, half * 8 + q, cols], cB(C_ID), [xsT, conb], [ps[half]])
                        cpy(ACT, xs_tm[:, half * 1024:(half + 1) * 1024], pb[:, 0:1024], [ps[half]], [xs_tm])
                    pb = ps[2][:].bitcast(BF16)
                    for g in range(4):
                        tr(pb[:, g * 128:(g + 1) * 128], BT[:, g, cols], cB(C_ID), [BT, conb], [ps[2]])
                    cpy(ACT, Btm[:].rearrange("p a b -> p (a b)"), pb[:, 0:512], [ps[2]], [Btm])
                    tt(xdt[:].rearrange("p (r q) -> p r q", q=64), xs_tm[:].rearrange("p (r q) -> p r q", q=64),
                       dt[:, s, :].unsqueeze(2).to_broadcast([128, 32, 64]), ALU.mult, [xs_tm, dt], [xdt])
                    mm(ps[3][:, 0:32], cF(TM), aa[:, s, :], True, True, [con, aa], [ps[3]])
                    cpy(DVE, acs[:], ps[3][:, 0:32], [ps[3]], [acs])
                    act(eacs[:], ps[3][:, 0:32], AF.Exp, [ps[3]], [eacs])
                    if nsub == 2:
                        for sg in range(2):
                            cpy(DVE, CTm[:, sg, :, sg * 64:(sg + 1) * 64], CTt[:, :, s * 128 + sg * 64:s * 128 + (sg + 1) * 64], [CTt], [CTm])
                    for g in range(4):
                        hs = slice(g * 8, (g + 1) * 8)
                        cpy(DVE, abc[:], aa[:, s, hs].unsqueeze(2).to_broadcast([128, 8, 128]), [aa], [abc])
                        for rr in range(8):
                            bank = ps[4 + rr // 4]
                            mm(bank[:, (rr % 4) * 128:(rr % 4 + 1) * 128], abc[:, rr, :], cF(TM), True, True, [abc, con], [bank])
                        for hb in range(2):
                            tt(t1[:, hb * 4:(hb + 1) * 4, :], ps[4 + hb][:, :].rearrange("p (r l) -> p r l", l=128),
                               acs[:, g * 8 + hb * 4:g * 8 + hb * 4 + 4].unsqueeze(2).to_broadcast([128, 4, 128]), ALU.subtract,
                               [ps[4 + hb], acs], [t1])
                        ts(t1[:], t1[:], 0.0, None, ALU.min, None, [t1], [t1])
                        act(Dm[:], t1[:], AF.Exp, [t1], [Dm])
                        mm(ps[6][:, 0:128], BT[:, g, cols], CTt[:, g, cols], True, True, [BT, CTt], [ps[6]])
                        tt(CBm[:], ps[6][:, 0:128], cF(TM), ALU.mult, [ps[6], con], [CBm])
                        tt(MT[:], Dm[:], CBm[:].unsqueeze(1).to_broadcast([128, 8, 128]), ALU.mult, [Dm, CBm], [MT])
                        for si, sgslot in enumerate(sub_segs):
                            pr = prs[si]
                            lc = lastc[si]
                            for hb in range(2):
                                rowv = ps[4 + hb][:, :].rearrange("p (r l) -> p r l", l=128)
                                tt(sm[pr, hb * 4:(hb + 1) * 4], rowv[pr, :, lc], acs[pr, g * 8 + hb * 4:g * 8 + hb * 4 + 4], ALU.subtract,
                                   [ps[4 + hb], acs], [sm])
                                act(sm[:, 16 + si * 8 + hb * 4:16 + si * 8 + (hb + 1) * 4], rowv[:, :, lc], AF.Exp, [ps[4 + hb]], [sm])
                        act(sm[:, 8:16], sm[:, 0:8], AF.Exp, [sm], [sm])
                        for rr in range(8):
                            r = g * 8 + rr
                            mm(ps[7][:, rr * 64:(rr + 1) * 64], MT[:, rr, :], xdt[:, r * 64:(r + 1) * 64], True, True, [MT, xdt], [ps[7]])
                        for si, sgslot in enumerate(sub_segs):
                            lhs = CTt[:, g, cols] if nsub == 1 else CTm[:, si, g, :]
                            hb_ = hb16[si]
                            cpy(ACT, hb_[:], hT[sgslot][:, g * 512:(g + 1) * 512], [hT[sgslot]], [hb_])
                            mm(ps[6][:, :], lhs, hb_[:], si == 0, si == len(sub_segs) - 1,
                               [CTt, CTm, hb_], [ps[6]])
                        tt(ytmp[:].rearrange("p (r q) -> p r q", q=64), ps[6][:, :].rearrange("p (r q) -> p r q", q=64),
                           eacs[:, hs].unsqueeze(2).to_broadcast([128, 8, 64]), ALU.mult, [ps[6], eacs], [ytmp])
                        tt(yt[:, g * 512:(g + 1) * 512], ytmp[:], ps[7][:, :], ALU.add, [ytmp, ps[7]], [yt])
                        tt(ytmp[:].rearrange("p (r q) -> p r q", q=64), xs_tm[:, g * 512:(g + 1) * 512].rearrange("p (r q) -> p r q", q=64),
                           dsk_s[:, li, hs].unsqueeze(2).to_broadcast([128, 8, 64]), ALU.mult, [xs_tm, dsk_s], [ytmp])
                        tt(yt[:, g * 512:(g + 1) * 512], yt[:, g * 512:(g + 1) * 512], ytmp[:], ALU.add, [yt, ytmp], [yt])
                        tt(xdtw[:].rearrange("p (r q) -> p r q", q=64), xdt[:, g * 512:(g + 1) * 512].rearrange("p (r q) -> p r q", q=64),
                           sm[:, 8:16].unsqueeze(2).to_broadcast([128, 8, 64]), ALU.mult, [xdt, sm], [xdtw])
                        for si, sgslot in enumerate(sub_segs):
                            pr = prs[si]
                            h_ = hT[sgslot]
                            mm(ps[6][:, :], Btm[pr, g, :], xdtw[pr, :], True, True, [Btm, xdtw], [ps[6]])
                            tt(ytmp[:].rearrange("p (r q) -> p r q", q=64), h_[:, g * 512:(g + 1) * 512].rearrange("p (r q) -> p r q", q=64),
                               sm[:, 16 + si * 8:24 + si * 8].unsqueeze(2).to_broadcast([128, 8, 64]), ALU.mult, [h_, sm], [ytmp])
                            tt(h_[:, g * 512:(g + 1) * 512], ytmp[:], ps[6][:, :], ALU.add, [ytmp, ps[6]], [h_])
                    tt(yt[:], yt[:], zs[:, s, :], ALU.mult, [yt, zs], [yt])
                    for g in range(4):
                        act(junk[:, 0:512], yt[:, g * 512:(g + 1) * 512], AF.Square, [yt], [junk, sm], accum_out=sm[:, 32 + g:33 + g])
                    act(sm[:, 40:44], sm[:, 32:36], AF.Sqrt, [sm], [sm], scale=1.0 / 512, bias=RMS_EPS)
                    K.op(DVE, lambda: nc.vector.reciprocal(out=sm[:, 36:40], in_=sm[:, 40:44]), [sm], [sm])
                    for g in range(4):
                        act(yn[:, g * 512:(g + 1) * 512], yt[:, g * 512:(g + 1) * 512], AF.Copy, [yt, sm], [yn], scale=sm[:, 36 + g:37 + g])
                    for half in range(2):
                        pb = ps[half][:].bitcast(BF16)
                        for q in range(8):
                            tr(pb[:, q * 128:(q + 1) * 128], yn[:, (half * 8 + q) * 128:(half * 8 + q + 1) * 128], cB(C_ID), [yn, conb], [ps[half]])
                        tt(ynT[:, s, half * 8:(half + 1) * 8, :], pb[:, 0:1024].rearrange("p (a b) -> p a b", b=128),
                           nwp[:, li, half * 8:(half + 1) * 8].unsqueeze(2).to_broadcast([128, 8, 128]), ALU.mult, [ps[half], nwp], [ynT])
                    if nsub == 2:
                        for si in range(2):
                            store_state(hT[si], o_sssm[li, 2 * s + si, :, :])
                if is_last and nsub == 1:
                    store_state(hT[0], o_pssm[li, tl["seq"], :, :])
                load_gb(li, 0)
                banks = [ps[0], ps[1], ps[2], ps[3]]
                for j in range(4):
                    wt, wv = wload(w_out[li, j * 512:(j + 1) * 512, :].rearrange("(a p) c -> p a c", p=128), v3(4, 1024))
                    for cc in range(4):
                        cq = j * 4 + cc
                        for s in range(2):
                            for hh in range(2):
                                mm(banks[s * 2 + hh][:, :], ynT[:, s, cq, :], wv[:, cc, hh * 512:(hh + 1) * 512], cq == 0, cq == 15,
                                   [ynT, wt], [banks[s * 2 + hh]])
                for s in range(2):
                    deepnorm(s, banks[s * 2:s * 2 + 2], li, 0)
            make_xT()

        import os
        pstop = int(os.environ.get("PSTOP", "99"))

        def peer_phase(li):
            with K.phase() as ph:
              for _once in (0,):
                    GT = K.sbuf("GT", [128, 128, 256], BF16, stack=ph)
                    qT = K.sbuf("qT", [128, 16, 256], BF16, stack=ph)
                    skb = K.sbuf("skb", [128, 2, 128], BF16, dma=True, stack=ph)
                    ssb = K.sbuf("ssb", [128, 16, 128], F32, stack=ph)
                    tmp = K.sbuf("tmp", [128, 256], F32, stack=ph)
                    sv = K.sbuf("sv", [128, 16, 16], F32, stack=ph)
                    siu = K.sbuf("siu", [128, 8, 16], U32, stack=ph)
                    sif = K.sbuf("sif", [128, 8, 16], F32, stack=ph)
                    cand = K.sbuf("cand", [128, 8, 256], F32, stack=ph)
                    best = K.sbuf("best", [128, 8, 16], F32, stack=ph)
                    eb = K.sbuf("eb", [128, 8, 16], F32, stack=ph)
                    zz = K.sbuf("zz", [128, 32], F32, stack=ph)
                    bia = K.sbuf("bia", [128, 8, 16], F32, stack=ph)
                    thr = K.sbuf("thr", [128, 8, 16], F32, stack=ph)
                    biaT = K.sbuf("biaT", [128, 256], F32, stack=ph)
                    thrT = K.sbuf("thrT", [128, 256], F32, stack=ph)
                    idxT = K.sbuf("idxT", [128, 256], F32, stack=ph)
                    Qrep = [K.sbuf("Qrep%d" % i, [128, 8, 128], BF16, stack=ph) for i in range(2)]
                    OH = [K.sbuf("OH%d" % i, [128, 8, 128], BF16, stack=ph) for i in range(2)]
                    et = [K.sbuf("et%d" % i, [128, 128], F32, stack=ph) for i in range(4)]
                    Rt = [K.sbuf("Rt%d" % i, [128, 128], BF16, stack=ph) for i in range(4)]
                    gh = [K.sbuf("gh%d" % i, [128, 256], BF16, stack=ph) for i in range(3)]
                    ATt = [K.sbuf("ATt%d" % i, [128, 256], BF16, stack=ph) for i in range(3)]

                    K.dma(POOL, skb[:], skT[li, :, :, :].rearrange("k d n -> d k n"), [], [skb], skb)
                    for cg in range(4):
                        wt, wv = wload(pwq[li, :, cg * 512:(cg + 1) * 512].rearrange("(a p) c -> p a c", p=128), v3(8, 512))
                        for cc in range(4):
                            oc = cg * 4 + cc
                            bank = ps[oc % 2]
                            for dk in range(8):
                                mm(bank[:, 0:256], wv[:, dk, cc * 128:(cc + 1) * 128], xT[:, dk, :], dk == 0, dk == 7, [xT, wt], [bank])
                            cpy(ACT if oc % 2 else DVE, qT[:, oc, :], bank[:, 0:256], [bank], [qT])
                    if pstop == 1:
                        break
                    for s in range(2):
                        cols = slice(s * 128, (s + 1) * 128)
                        for oc in range(16):
                            bank = ps[2 + oc // 4]
                            mm(bank[:, (oc % 4) * 128:(oc % 4 + 1) * 128], qT[:, oc, cols], skb[:, oc % 2, :], True, True, [qT, skb], [bank])
                        for q4 in range(4):
                            cpy(ACT, ssb[:, q4 * 4:(q4 + 1) * 4, :], ps[2 + q4][:, :].rearrange("p (a b) -> p a b", b=128), [ps[2 + q4]], [ssb])
                        for oc in range(16):
                            K.op(DVE, lambda: nc.vector.max(out=sv[:, oc, 0:8], in_=ssb[:, oc, :]), [ssb], [sv])
                            if oc % 2 == 0:
                                K.op(DVE, lambda: nc.vector.max_index(out=siu[:, oc // 2, 0:8], in_max=sv[:, oc, 0:8], in_values=ssb[:, oc, :]), [ssb, sv], [siu])
                            K.op(DVE, lambda: nc.vector.match_replace(out=tmp[:, 0:128], in_to_replace=sv[:, oc, 0:8], in_values=ssb[:, oc, :], imm_value=-1e30), [ssb, sv], [tmp])
                            K.op(DVE, lambda: nc.vector.max(out=sv[:, oc, 8:16], in_=tmp[:, 0:128]), [tmp], [sv])
                            if oc % 2 == 0:
                                K.op(DVE, lambda: nc.vector.max_index(out=siu[:, oc // 2, 8:16], in_max=sv[:, oc, 8:16], in_values=tmp[:, 0:128]), [tmp, sv], [siu])
                        cpy(DVE, sif[:], siu[:], [siu], [sif])
                        sv4 = sv[:].rearrange("p (h k) a -> p h k a", k=2)
                        tt(cand[:].rearrange("p h (a b) -> p h a b", b=16), sv4[:, :, 0, :].unsqueeze(3).to_broadcast([128, 8, 16, 16]),
                           sv4[:, :, 1, :].unsqueeze(2).to_broadcast([128, 8, 16, 16]), ALU.add, [sv], [cand])
                        for h in range(8):
                            K.op(DVE, lambda: nc.vector.max(out=best[:, h, 0:8], in_=cand[:, h, :]), [cand], [best])
                            K.op(DVE, lambda: nc.vector.match_replace(out=tmp[:], in_to_replace=best[:, h, 0:8], in_values=cand[:, h, :], imm_value=-1e30), [cand, best], [tmp])
                            K.op(DVE, lambda: nc.vector.max(out=best[:, h, 8:16], in_=tmp[:]), [tmp], [best])
                        tt(eb[:], best[:], best[:, :, 0:1].to_broadcast([128, 8, 16]), ALU.subtract, [best], [eb])
                        act(eb[:], eb[:], AF.Exp, [eb], [eb])
                        K.op(DVE, lambda: nc.vector.tensor_reduce(out=zz[:, 0:8], in_=eb[:], axis=AX.X, op=ALU.add), [eb], [zz])
                        act(zz[:, 8:16], zz[:, 0:8], AF.Ln, [zz], [zz])
                        tt(zz[:, 16:24], best[:, :, 0], zz[:, 8:16], ALU.add, [best, zz], [zz])
                        tt(bia[:], sv4[:, :, 0, :], zz[:, 16:24].unsqueeze(2).to_broadcast([128, 8, 16]), ALU.subtract, [sv, zz], [bia])
                        stt(thr[:], sv4[:, :, 0, :], -1.0, best[:, :, 15:16].to_broadcast([128, 8, 16]), ALU.mult, ALU.add, [sv, best], [thr])
                        ts(thr[:], thr[:], -2e-5, None, ALU.add, None, [thr], [thr])
                        for k_, (src, dstT) in enumerate(((bia, biaT), (thr, thrT), (sif, idxT))):
                            bank = ps[2 + k_]
                            tr(bank[:, 0:128], src[:].rearrange("p h a -> p (h a)"), cF(C_ID), [src, con], [bank])
                            cpy(ACT, dstT[:, cols], bank[:, 0:128], [bank], [dstT])
                    if pstop == 2:
                        break
                    LA = 3

                    def fill_batch(b):
                        t0 = b * 8
                        q_, o_ = Qrep[b % 2], OH[b % 2]
                        for h in range(8):
                            cpy(DVE, q_[:, :, h * 16:(h + 1) * 16], qT[:, 2 * h + 1, t0:t0 + 8].unsqueeze(2).to_broadcast([128, 8, 16]), [qT], [q_])
                        tt(o_[:], con[:, C_IOTA, :].unsqueeze(1).to_broadcast([128, 8, 128]),
                           idxT[:, t0:t0 + 8].unsqueeze(2).to_broadcast([128, 8, 128]), ALU.is_equal, [con, idxT], [o_])

                    def stage_a(t):
                        if t % 8 == 0:
                            fill_batch(t // 8)
                        q_ = Qrep[(t // 8) % 2]
                        abank = ps[t % 4]
                        mm(abank[:, 0:128], q_[:, t % 8, :], skb[:, 1, :], True, True, [q_, skb], [abank])

                    def stage_b(t):
                        abank = ps[t % 4]
                        o_ = OH[(t // 8) % 2]
                        gbank = ps[4 + (t // 4) % 2]
                        e_ = et[t % 4]
                        r_ = Rt[t % 4]
                        act(e_[:], abank[:, 0:128], AF.Exp, [abank, biaT], [e_], bias=biaT[:, t:t + 1])
                        stt(r_[:], abank[:, 0:128], thrT[:, t:t + 1], e_[:], ALU.is_ge, ALU.mult, [abank, thrT, e_], [r_])
                        mm(gbank[:, (t % 4) * 128:(t % 4 + 1) * 128], r_[:], o_[:, t % 8, :], True, True, [r_, o_], [gbank])
                        if t % 4 == 3:
                            cpy(ACT, GT[:, :, t - 3:t + 1], gbank[:, :].rearrange("p (t i) -> p i t", i=128), [gbank], [GT])

                    for t in range(LA):
                        stage_a(t)
                    for t in range(256):
                        if t + LA < 256:
                            stage_a(t + LA)
                        stage_b(t)
                    if pstop == 3:
                        break
                    load_gb(li, 1)
                    accb = [ps[0], ps[1], ps[2], ps[3]]
                    NCG = int(os.environ.get('NCG', '32'))
                    wts = {}

                    def stage_1(c):
                        cg, cc = c // 4, c % 4
                        if cc == 0:
                            ut = wsl[wsl_i[0] % NSLOT]
                            vt_ = wsl[(wsl_i[0] + 1) % NSLOT]
                            wsl_i[0] += 2
                            K.dma(SP, ut[:, :], uTb[li][cg, :, :], [uTb[li]], [ut], ut)
                            K.dma(SP, vt_[:, :], vtb[li][cg, :, :], [vtb[li]], [vt_], vt_)
                            uv = v3(8, 512)(ut)
                            vv = v3(4, 1024)(vt_)
                            wts[cg] = (ut, uv, vt_, vv)
                        ut, uv, vt_, vv = wts[cg]
                        hb = ps[4 + c % 3]
                        for dk in range(8):
                            mm(hb[:, 0:256], uv[:, dk, cc * 128:(cc + 1) * 128], xT[:, dk, :], dk == 0, dk == 7, [xT, ut], [hb])
                        g_ = gh[c % 3]
                        a_ = ATt[c % 3]
                        act(g_[:], hb[:, 0:256], AF.Gelu, [hb], [g_])
                        tt(a_[:], g_[:], GT[:, c, :], ALU.mult, [g_, GT], [a_])

                    def stage_2(c):
                        cg, cc = c // 4, c % 4
                        ut, uv, vt_, vv = wts[cg]
                        a_ = ATt[c % 3]
                        for s in range(2):
                            for hh in range(2):
                                mm(accb[s * 2 + hh][:, :], a_[:, s * 128:(s + 1) * 128], vv[:, cc, hh * 512:(hh + 1) * 512], c == 0, c == NCG * 4 - 1,
                                   [a_, vt_], [accb[s * 2 + hh]])

                    NCH = NCG * 4
                    stage_1(0)
                    stage_1(1)
                    for c in range(NCH):
                        if c + 2 < NCH:
                            stage_1(c + 2)
                        stage_2(c)
                    for s in range(2):
                        deepnorm(s, accb[s * 2:s * 2 + 2], li, 1)
            if pstop < 99:
                raise _Stop()

        kvstop = int(os.environ.get("KVSTOP", "99"))

        def kv_phase(tl):
            segs = tl["segs"]
            nseg = len(segs)
            kind = tl["kind"]
            with K.phase() as ph:
              for _once in (0,):
                    kst = [K.sbuf("kst%d" % i, [128, D], F32, dma=True, stack=ph) for i in range(2)]
                    vaug = [K.sbuf("vaug%d" % i, [128, 16, 66], BF16, dma=True, stack=ph) for i in range(2)]
                    ktile = [K.sbuf("ktile%d" % i, [128, 256], BF16, dma=True, stack=ph) for i in range(2)]
                    lf = K.sbuf("lf", [128, 2, 16], F32, dma=True, stack=ph)
                    tb = K.sbuf("tb", [128, 16], F32, stack=ph)
                    pk = K.sbuf("pk", [128, D], BF16, dma=True, stack=ph)
                    pf = K.sbuf("pf", [128, 16], F32, dma=True, stack=ph)
                    rsum = K.sbuf("rsum", [128, 16], F32, stack=ph)
                    KVX = os.environ.get("KVX", "")
                    for i in range(2):
                      if "a" not in KVX:
                        cpy(DVE, vaug[i][:, :, 64:66], conb[:, C_ONE, 0:32].rearrange("p (h q) -> p h q", q=2), [conb], [vaug[i]])

                    def tok_rows(dst, s):
                        if kind == "p":
                            b, T = tl["seq"], tl["T"]
                            return dst[b, T * 256 + s * 128:T * 256 + (s + 1) * 128, :]
                        return dst[2 * s:2 * s + 2, :, :].rearrange("b l f -> (b l) f")

                    ok, ov, of = (o_pk, o_pv, o_pf) if kind == "p" else (o_sk, o_sv, o_sf)
                    cnt = 0
                    for part in range(2):
                        for cgp in range(2):
                            cg = part * 2 + cgp
                            wt, wv = wload(kv_wkv[:, cg * 512:(cg + 1) * 512].rearrange("(a p) c -> p a c", p=128), v3(8, 512))
                            for s in range(2):
                                bank = ps[s]
                                for dk in range(8):
                                    mm(bank[:, :], xT[:, dk, s * 128:(s + 1) * 128], wv[:, dk, :], dk == 0, dk == 7, [xT, wt], [bank])
                                kk = kst[s]
                                if "b" not in KVX:
                                    cpy(ACT, kk[:, cgp * 512:(cgp + 1) * 512], bank[:, :], [bank], [kk])
                                if part == 1 and "c" not in KVX:
                                    cpy(DVE, vaug[s][:, cgp * 8:(cgp + 1) * 8, 0:64], bank[:, :].rearrange("p (h q) -> p h q", q=64), [bank], [vaug[s]])
                                if cgp == 1:
                                    K.dma(SP, tok_rows(ok if part == 0 else ov, s), kk[:], [kk], [], kk, tag="ok")
                            if part == 0 and "d" not in KVX:
                                for cc in range(4):
                                    c = cgp * 4 + cc
                                    bank = ps[2 + c % 2]
                                    for dk in range(8):
                                        mm(bank[:, 0:256], wv[:, dk, cc * 128:(cc + 1) * 128], xT[:, dk, :], dk == 0, dk == 7, [xT, wt], [bank])
                                    kt_ = ktile[c % 2]
                                    cpy(ACT if c % 2 else DVE, kt_[:], bank[:, 0:256], [bank], [kt_])
                                    if kind == "p":
                                        sl = tl["slot"]
                                        K.dma(SP, KTscr[sl][c, :, tl["T"] * 256:(tl["T"] + 1) * 256], kt_[:], [kt_], [KTscr[sl]], kt_, tag="kt")
                                    else:
                                        for i in range(4):
                                            K.dma(SP, KTscr[NP + i][c, :, PAST:PAST + 64], kt_[:, i * 64:(i + 1) * 64], [kt_], [KTscr[NP + i]], kt_)
                    if kvstop == 1:
                        break
                    for s in range(2):
                        if kind == "p":
                            sl = tl["slot"]
                            r0 = tl["T"] * 256 + s * 128
                            K.dma(SP, Vscr[sl][r0:r0 + 128, :], vaug[s][:].rearrange("p h q -> p (h q)"), [vaug[s]], [Vscr[sl]], vaug[s], tag="vs")
                        else:
                            for sg in range(2):
                                K.dma(SP, Vscr[NP + 2 * s + sg][PAST:PAST + 64, :], vaug[s][sg * 64:(sg + 1) * 64, :, :].rearrange("p h q -> p (h q)"),
                                      [vaug[s]], [Vscr[NP + 2 * s + sg]], vaug[s])
                    if kvstop == 2:
                        break
                    wf32 = K.sbuf("wf32", [128, 8, 16], F32, dma=True, stack=ph)
                    wt = K.sbuf("wfb", [128, 8, 16], BF16, stack=ph)
                    K.dma(SP, wf32[:], kv_wf[:, :].rearrange("(a p) c -> p a c", p=128), [], [wf32], wf32)
                    cpy(DVE, wt[:], wf32[:], [wf32], [wt])
                    wv = wt[:]
                    for s in range(2):
                        bank = ps[4 + s]
                        for dk in range(8):
                            mm(bank[:, 0:16], xT[:, dk, s * 128:(s + 1) * 128], wv[:, dk, :], dk == 0, dk == 7, [xT, wt], [bank])
                        tt(lf[:, s, :], bank[:, 0:16], kvb_s[:], ALU.add, [bank, kvb_s], [lf])
                    act(lf[:], lf[:], AF.Exp, [lf], [lf], scale=-1.0)
                    act(lf[:], lf[:], AF.Ln, [lf], [lf], bias=1.0)
                    ts(lf[:], lf[:], -1.0, None, ALU.mult, None, [lf], [lf])
                    for s in range(2):
                        K.dma(SP, tok_rows(of, s), lf[:, s, :], [lf], [], lf, tag="lf")
                    if kvstop == 3:
                        break
                    if kind == "p":
                        sl = 0
                        T = tl["T"]
                        if T == 0:
                            K.op(DVE, lambda: nc.vector.memset(SUF[:, 0, :, :], 0.0), [], [SUF])
                        for s in range(2):
                            j = T * 2 + s
                            mm(ps[6][:, 0:16], cF(C_TSU), lf[:, s, :], True, True, [con, lf], [ps[6]])
                            mm(ps[6][:, 16:32], cF(C_ONE), lf[:, s, :], True, True, [con, lf], [ps[6]])
                            mm(ps[6][:, 32:48], cF(C_TRI), lf[:, s, :], True, True, [con, lf], [ps[6]])
                            AT = AT0 if s == 0 else AT1
                            if j > 0:
                                cpy(DVE, AT[:, 0, 0:j, :], SUF[:, 0, 0:j, :], [SUF], [AT])
                            ts(AT[:, 0, j, :], ps[6][:, 32:48], -1.0, None, ALU.mult, None, [ps[6]], [AT])
                            if j > 0:
                                tt(SUF[:, 0, 0:j, :], SUF[:, 0, 0:j, :], ps[6][:, 16:32].unsqueeze(1).to_broadcast([128, j, 16]), ALU.add, [SUF, ps[6]], [SUF])
                            cpy(DVE, SUF[:, 0, j, :], ps[6][:, 0:16], [ps[6]], [SUF])
                    else:
                        for i in range(4):
                            s, sg = i // 2, i % 2
                            pr = slice(sg * 64, (sg + 1) * 64)
                            mm(ps[6][0:64, i * 16:(i + 1) * 16], con[pr, C_TRI, sg * 64:(sg + 1) * 64], lf[pr, s, :], True, True, [con, lf], [ps[6]])
                        ts(NCS[0:64, :, :].rearrange("p a b -> p (a b)"), ps[6][0:64, 0:64], -1.0, None, ALU.mult, None, [ps[6]], [NCS])
                        for i in range(4):
                            K.op(DVE, lambda: nc.vector.memset(rsum[:], 0.0), [], [rsum])
                            for j in range(NKT_PAST - 1, -1, -1):
                                K.dma(SP, pf[:], c_f[i, j * 128:(j + 1) * 128, :], [], [pf], pf)
                                mm(ps[6][:, 0:16], cF(C_TSU), pf[:], True, False, [con, pf], [ps[6]])
                                mm(ps[6][:, 0:16], cF(C_ONE), rsum[:], False, True, [con, rsum], [ps[6]])
                                cpy(DVE, SUF[:, i, j, :], ps[6][:, 0:16], [ps[6]], [SUF])
                                tt(rsum[:], rsum[:], pf[:], ALU.add, [rsum, pf], [rsum])
                            for j in range(NKT_PAST):
                                K.dma(POOL, pk[:], c_k[i, j * 128:(j + 1) * 128, :], [], [pk], pk)
                                for half in range(1):
                                    pb = ps[j % 2][:].bitcast(BF16)
                                    for q in range(8):
                                        tr(pb[:, q * 128:(q + 1) * 128], pk[:, q * 128:(q + 1) * 128], cB(C_ID), [pk, conb], [ps[j % 2]])
                                    kt_ = kst[j % 2]
                                    ktb = kt_[:].bitcast(BF16)
                                    cpy(ACT if j % 2 else DVE, ktb[:, 0:1024], pb[:, 0:1024], [ps[j % 2]], [kt_])
                                    K.dma(SP, KTscr[NP + i][:, :, j * 128:(j + 1) * 128].rearrange("c p k -> p c k"),
                                          ktb[:, 0:1024].rearrange("p (c k) -> p c k", k=128), [kt_], [KTscr[NP + i]], kt_)
                                va = vaug[j % 2]
                                K.dma(POOL, va[:, :, 0:64], c_v[i, j * 128:(j + 1) * 128, :].rearrange("p (h q) -> p h q", q=64), [], [va], va)
                                K.dma(SP, Vscr[NP + i][j * 128:(j + 1) * 128, :], va[:].rearrange("p h q -> p (h q)"), [va], [Vscr[NP + i]], va)

        def attn_phase(lj, li, tl):
            segs = tl["segs"]
            kind = tl["kind"]
            with K.phase() as ph:
                KT = K.sbuf("KT", [128, 8, KMAX], BF16, dma=True, stack=ph)
                VA = K.sbuf("VA", [128, KMAX // 128 + 1, 1056], BF16, dma=True, stack=ph)
                qTt = K.sbuf("qTt", [128, 8, 256], BF16, stack=ph)
                gs = K.sbuf("gs", [128, 2, D], BF16, stack=ph)
                PT = [K.sbuf("PT%d" % i, [128, 128], BF16, stack=ph) for i in range(4)]
                rc = K.sbuf("rc", [128, 4], F32, stack=ph)
                og = K.sbuf("og", [128, D], F32, stack=ph)
                ogb = K.sbuf("ogb", [128, D], BF16, stack=ph)
                ogT = K.sbuf("ogT", [128, 2, 8, 128], BF16, stack=ph)
                for cg in range(2):
                    wt, wv = wload(w_qg[lj, :, cg * 512:(cg + 1) * 512].rearrange("(a p) c -> p a c", p=128), v3(8, 512))
                    for cc in range(4):
                        c = cg * 4 + cc
                        bank = ps[c % 2]
                        for dk in range(8):
                            mm(bank[:, 0:256], wv[:, dk, cc * 128:(cc + 1) * 128], xT[:, dk, :], dk == 0, dk == 7, [xT, wt], [bank])
                        cpy(ACT if c % 2 else DVE, qTt[:, c, :], bank[:, 0:256], [bank], [qTt])
                for cg in range(2):
                    wt, wv = wload(w_qg[lj, :, 1024 + cg * 512:1024 + (cg + 1) * 512].rearrange("(a p) c -> p a c", p=128), v3(8, 512))
                    for s in range(2):
                        bank = ps[2 + s]
                        for dk in range(8):
                            mm(bank[:, :], xT[:, dk, s * 128:(s + 1) * 128], wv[:, dk, :], dk == 0, dk == 7, [xT, wt], [bank])
                        act(gs[:, s, cg * 512:(cg + 1) * 512], bank[:, :], AF.Sigmoid, [bank], [gs])
                cnt = [0]

                def attend(s, keysets):
                    nks = len(keysets)
                    for ki, ks in enumerate(keysets):
                        ks["load"]()
                        nk = ks["nfull"] + (1 if ks["part"] else 0)
                        for h in range(16):
                            accb = ps[4 + h // 4]
                            hh = h % 4
                            c, pbs = h // 2, (h % 2) * 64
                            for j in range(nk):
                                part = (j == ks["nfull"])
                                kp = 64 if part else 128
                                sb = ps[cnt[0] % 4]
                                p_ = PT[cnt[0] % 4]
                                cnt[0] += 1
                                mm(sb[0:kp, 0:128], KT[pbs:pbs + 64, c, j * 128:j * 128 + kp], qTt[pbs:pbs + 64, c, s * 128:(s + 1) * 128], True, True,
                                   [KT, qTt], [sb])
                                bias_ap, btile = ks["bias"](j, h)
                                act(p_[0:kp, :], sb[0:kp, 0:128], AF.Exp, [sb, btile], [p_], scale=ATT_SCALE, bias=bias_ap)
                                msk = ks["mask"](j)
                                if msk is not None:
                                    tt(p_[0:kp, :], p_[0:kp, :], msk, ALU.mult, [p_, conb], [p_])
                                mm(accb[:, hh * 65:(hh + 1) * 65], p_[0:kp, :], VA[0:kp, j, h * 66:h * 66 + 65],
                                   j == 0, j == nk - 1, [p_, VA], [accb])
                        pr = ks["rows"]
                        npr = pr.stop - pr.start
                        for hg in range(4):
                            accb = ps[4 + hg]
                            av = accb[pr, 0:260].rearrange("p (h q) -> p h q", q=65)
                            K.op(DVE, lambda: nc.vector.reciprocal(out=rc[pr, :], in_=av[:, :, 64]), [accb], [rc])
                            tt(og[pr, hg * 256:(hg + 1) * 256].rearrange("p (h q) -> p h q", q=64), av[:, :, 0:64],
                               rc[pr, :].unsqueeze(2).to_broadcast([npr, 4, 64]), ALU.mult, [accb, rc], [og])

                for s in range(2):
                    if kind == "p":
                        sl = tl["slot"]
                        T = tl["T"]
                        nkeys = (T + 1) * 256
                        nfull = T * 2 + s + 1

                        def load_p(sl=sl, nkeys=nkeys, s=s):
                            if s == 0:
                                K.dma(SP, KT[:, :, 0:nkeys], KTscr[sl][:, :, 0:nkeys].rearrange("c p k -> p c k"), [KTscr[sl]], [KT], KT)
                                K.dma(SP, VA[:, 0:nkeys // 128, :], Vscr[sl][0:nkeys, :].rearrange("(j p) f -> p j f", p=128), [Vscr[sl]], [VA], VA)
                        AT = AT0 if s == 0 else AT1
                        attend(s, [dict(load=load_p, nfull=nfull, part=False, rows=slice(0, 128),
                                        bias=lambda j, h, AT=AT: (AT[:, 0, j, h:h + 1], AT),
                                        mask=lambda j, nfull=nfull: (cB(C_TRI) if j == nfull - 1 else None))])
                    else:
                        kss = []
                        for sg in range(2):
                            i = 2 * s + sg

                            def load_s(i=i):
                                K.dma(SP, KT[:, :, 0:PAST + 64], KTscr[NP + i][:, :, 0:PAST + 64].rearrange("c p k -> p c k"), [KTscr[NP + i]], [KT], KT)
                                K.dma(SP, VA[:, 0:NKT_PAST, :], Vscr[NP + i][0:PAST, :].rearrange("(j p) f -> p j f", p=128), [Vscr[NP + i]], [VA], VA)
                                K.dma(SP, VA[0:64, NKT_PAST, :], Vscr[NP + i][PAST:PAST + 64, :], [Vscr[NP + i]], [VA], VA)

                            def bias_s(j, h, i=i):
                                if j == NKT_PAST:
                                    return NCS[0:64, i, h:h + 1], NCS
                                return SUF[:, i, j, h:h + 1], SUF

                            def mask_s(j, sg=sg):
                                if j == NKT_PAST:
                                    return conb[0:64, C_MS0 + sg, :]
                                return cB(C_CM0 + sg)
                            kss.append(dict(load=load_s, nfull=NKT_PAST, part=True, bias=bias_s, mask=mask_s, rows=slice(sg * 64, (sg + 1) * 64)))
                        attend(s, kss)
                    tt(ogb[:], og[:], gs[:, s, :], ALU.mult, [og, gs], [ogb])
                    pb = ps[s][:].bitcast(BF16)
                    for q in range(8):
                        tr(pb[:, q * 128:(q + 1) * 128], ogb[:, q * 128:(q + 1) * 128], cB(C_ID), [ogb, conb], [ps[s]])
                    cpy(ACT, ogT[:, s, :, :], pb[:, 0:1024].rearrange("p (a b) -> p a b", b=128), [ps[s]], [ogT])
                load_gb(li, 0)
                banks = [ps[0], ps[1], ps[2], ps[3]]
                for j in range(2):
                    wt, wv = wload(w_o[lj, j * 512:(j + 1) * 512, :].rearrange("(a p) c -> p a c", p=128), v3(4, 1024))
                    for cc in range(4):
                        cq = j * 4 + cc
                        for s in range(2):
                            for hh in range(2):
                                mm(banks[s * 2 + hh][:, :], ogT[:, s, cq, :], wv[:, cc, hh * 512:(hh + 1) * 512], cq == 0, cq == 7,
                                   [ogT, wt], [banks[s * 2 + hh]])
                for s in range(2):
                    deepnorm(s, banks[s * 2:s * 2 + 2], li, 0)
            make_xT()

        tiles = []
        for b in range(NP):
            for T in range(NT_P):
                tiles.append(dict(kind="p", seq=b, T=T, slot=b, segs=[0], last=(T == NT_P - 1), outseq=[("p", b)]))
        tiles.append(dict(kind="s", segs=[0, 1, 2, 3], last=True, outseq=[("s", i) for i in range(4)], T=0))

        stop_at = int(os.environ.get("KSTOP", "100000"))
        stage = [0]

        class _Stop(Exception):
            pass

        def chk():
            stage[0] += 1
            if stage[0] >= stop_at:
                raise _Stop()

        try:
          for tl in tiles:
              kind = tl["kind"]
              if kind == "p":
                  K.dma(SP, x[:], x_p[tl["seq"], tl["T"] * 256:(tl["T"] + 1) * 256, :].rearrange("(s p) d -> p s d", p=128), [], [x], x)
                  if tl["T"] == 0:
                      for l in range(2):
                          K.op(DVE, lambda: nc.vector.memset(hTp[l][:], 0.0), [], [hTp[l]])
                          K.op(DVE, lambda: nc.vector.memset(cst[l][:], 0.0), [], [cst[l]])
              else:
                  K.dma(SP, x[:], x_s[:, :, :].rearrange("b l d -> (b l) d").rearrange("(s p) d -> p s d", p=128), [], [x], x)
              make_xT()
              chk()
              konly = os.environ.get("KONLY", "")
              if konly == "kv":
                  kv_phase(tl)
                  raise _Stop()
              for li in range(DEPTH):
                  if li < 2:
                      mamba_phase(li, tl)
                  else:
                      attn_phase(li - 2, li, tl)
                  chk()
                  peer_phase(li)
                  chk()
                  if li < DEPTH - 1:
                      make_xT()
                  if li == 1:
                      kv_phase(tl)
                      chk()
              if kind == "p":
                  K.dma(SP, y_p[tl["seq"], tl["T"] * 256:(tl["T"] + 1) * 256, :].rearrange("(s p) d -> p s d", p=128), x[:], [x], [], x)
              else:
                  K.dma(SP, y_s[:, :, :].rearrange("b l d -> (b l) d").rearrange("(s p) d -> p s d", p=128), x[:], [x], [], x)
        except _Stop:
            K.dma(SP, y_p[0, 0:256, :].rearrange("(s p) d -> p s d", p=128), x[:], [x], [], x)
        K.finish()
        build.stats = dict(nsem=K.nsem, nwait=K.nwait, n={E.name: E.n for E in K.engs})
    return nc


def _consts():
    t = np.zeros((NCON, 128, 128), np.float32)
    idx = np.arange(128)
    tri = (idx[:, None] <= idx[None, :]).astype(np.float32)
    blk = ((idx[:, None] // 64) == (idx[None, :] // 64)).astype(np.float32)
    t[0] = np.eye(128, dtype=np.float32)
    t[1] = tri
    t[2] = tri * blk
    t[3] = 1.0
    t[4] = blk
    t[5] = np.broadcast_to(idx[None, :].astype(np.float32), (128, 128))
    t[6] = (idx[:, None] > idx[None, :]).astype(np.float32)
    t[7][0:64, 0:64] = tri[0:64, 0:64]
    t[8][0:64, 64:128] = tri[0:64, 0:64]
    t[9][:, 0:64] = 1.0
    t[10][:, 64:128] = 1.0
    return np.ascontiguousarray(t.transpose(1, 0, 2).reshape(128, NCON * 128))


_NC_CACHE = {}


def _NEXP():
    import os
    return int(os.environ.get("KEXP", "16384"))


def run(inputs, n_cores):
    f = lambda a: np.ascontiguousarray(np.asarray(a, dtype=np.float32))
    xp = f(inputs["x_prompt"]); xs = f(inputs["x_sample"])
    B, SEQ, _ = xp.shape
    DB = xs.shape[0]
    assert DB == 4 * n_cores and B % n_cores == 0 and xs.shape[1] == 64
    NP = B // n_cores
    PAST = inputs["cache_k"].shape[1]
    cfg = (NP, SEQ, PAST)
    if cfg not in _NC_CACHE:
        _NC_CACHE[cfg] = build(cfg)
    nc = _NC_CACHE[cfg]
    ssm = f(inputs["state_ssm"]).reshape(2, DB, 2048, 128)
    cvs = f(inputs["state_conv"])
    ck = f(inputs["cache_k"]).reshape(DB, PAST, D); cv = f(inputs["cache_v"]).reshape(DB, PAST, D); cf = f(inputs["cache_logf"])
    cw = f(inputs["a_conv_w"]); cb = f(inputs["a_conv_b"])
    convp = np.zeros((2, 128, 24, 5), np.float32)
    convp[:, :, :, 0:4] = cw.reshape(2, 4, 24, 128).transpose(0, 3, 2, 1)
    convp[:, :, :, 4] = cb.reshape(2, 24, 128).transpose(0, 2, 1)
    shared = {
        "w_in": f(inputs["a_w_in"]), "convp": np.ascontiguousarray(convp.reshape(2, 128, 120)),
        "dtb": f(inputs["a_dt_bias"]), "alog": f(inputs["a_A_log"]), "dsk": f(inputs["a_D"]),
        "normw": np.ascontiguousarray(f(inputs["a_norm_w"]).reshape(2, 16, 128).transpose(0, 2, 1)),
        "w_out": f(inputs["a_w_out"]), "kv_wkv": np.ascontiguousarray(f(inputs["kv_w"])[:, 0:2048]),
        "kv_wf": np.ascontiguousarray(f(inputs["kv_w"])[:, 2048:2064]), "kvb": f(inputs["kv_b_f"]),
        "w_qg": f(inputs["b_w_qg"]), "w_o": f(inputs["b_w_o"]), "pwq": f(inputs["peer_w_q"]),
        "skT": np.ascontiguousarray(f(inputs["peer_subkeys"]).transpose(0, 1, 3, 2)),
        "uT": np.ascontiguousarray(f(inputs["peer_u"])[:, 0:_NEXP()].transpose(0, 2, 1)), "vt": np.ascontiguousarray(f(inputs["peer_v"])[:, 0:_NEXP()]),
        "lng": f(inputs["ln_g"]), "lnb": f(inputs["ln_b"]), "consts": _consts(),
    }
    in_maps = []
    for i in range(n_cores):
        m = dict(shared)
        m["x_p"] = np.ascontiguousarray(xp[i * NP:(i + 1) * NP]); m["x_s"] = np.ascontiguousarray(xs[4 * i:4 * i + 4])
        m["st_ssm"] = np.ascontiguousarray(ssm[:, 4 * i:4 * i + 4]); m["st_conv"] = np.ascontiguousarray(cvs[:, 4 * i:4 * i + 4])
        m["c_k"] = np.ascontiguousarray(ck[4 * i:4 * i + 4]); m["c_v"] = np.ascontiguousarray(cv[4 * i:4 * i + 4])
        m["c_f"] = np.ascontiguousarray(cf[4 * i:4 * i + 4])
        in_maps.append(m)
    res = run_bass_kernel_spmd(nc, in_maps, core_ids=list(range(n_cores)))
    R = res.results
    cat = lambda k, ax: np.concatenate([np.asarray(r[k]) for r in R], axis=ax)
    y_p = cat("y_p", 0); y_s = cat("y_s", 0)
    p_ssm = cat("o_pssm", 1).reshape(2, B, 32, 64, 128); p_conv = cat("o_pconv", 1)
    p_k = cat("o_pk", 0).reshape(B, SEQ, 16, 64); p_v = cat("o_pv", 0).reshape(B, SEQ, 16, 64); p_f = cat("o_pf", 0)
    s_ssm = cat("o_sssm", 1).reshape(2, DB, 32, 64, 128); s_conv = cat("o_sconv", 1)
    s_k = cat("o_sk", 0).reshape(DB, 64, 16, 64); s_v = cat("o_sv", 0).reshape(DB, 64, 16, 64); s_f = cat("o_sf", 0)
    return (y_p, y_s, p_ssm, p_conv, p_k, p_v, p_f, s_ssm, s_conv, s_k, s_v, s_f)


def kernel(**inputs):
    return run(inputs, 8)
```
